# Optimizing a Trainium2 kernel written in Bass

```python
import jax, jax.numpy as jnp
from jax import lax
import numpy as np

D_MODEL = 1024
BATCH = 8
SEQ = 2048
DEPTH = 1
DEC_BATCH = 128
DEC_SEQ = 1
PAST_LEN = 16384
PAGE_SIZE = 128

MIX_WIDTH = 2 * D_MODEL
CONV_CH = MIX_WIDTH // 2
CONV_GROUPS = 16
SHORT_CONV_W = 3
SSM_CH = MIX_WIDTH - CONV_CH
SSM_HEAD_DIM = 64
SSM_HEADS = SSM_CH // SSM_HEAD_DIM
SSM_GROUPS = 2
SSM_STATE = 128
SSM_CONV_W = 4
SSM_CHUNK = 128
XBC_CH = SSM_CH + 2 * SSM_GROUPS * SSM_STATE
IN_COLS = 3 * CONV_CH + SSM_CH + XBC_CH + SSM_HEADS
D_FF = 4 * D_MODEL
ALPHA = (2 * DEPTH) ** 0.25
BETA = (8 * DEPTH) ** -0.25
LN_EPS = 1e-5
RMS_EPS = 1e-5

kernel_name = 'hymba_style_shortconv_mamba2_deepnorm_adaln_step'


def layer_norm(x, g, b):
    xf = x.astype(jnp.float32)
    mu = jnp.mean(xf, axis=-1, keepdims=True)
    var = jnp.mean(jnp.square(xf - mu), axis=-1, keepdims=True)
    y = (xf - mu) * lax.rsqrt(var + LN_EPS) * g.astype(jnp.float32) + b.astype(jnp.float32)
    return y.astype(x.dtype)


def group_rms_norm(x, w, n_groups):
    shape = x.shape
    xf = x.astype(jnp.float32).reshape(*shape[:-1], n_groups, shape[-1] // n_groups)
    xf = xf * lax.rsqrt(jnp.mean(jnp.square(xf), axis=-1, keepdims=True) + RMS_EPS)
    return (xf.reshape(shape) * w.astype(jnp.float32)).astype(x.dtype)


def causal_dwconv(inp, buf, w):
    k_w = w.shape[0]
    l_ = inp.shape[1]
    full = jnp.concatenate([buf.astype(inp.dtype), inp], axis=1)
    out = sum(full[:, k:k + l_] * w[k] for k in range(k_w))
    return out, full[:, l_:]


def ssd_chunked(xh, dt, a, bm, cm, s0):
    b_, l_, h_, p_ = xh.shape
    g_, n_ = bm.shape[2], bm.shape[3]
    e_ = h_ // g_
    cl = min(SSM_CHUNK, l_)
    nc = -(-l_ // cl)
    pad = nc * cl - l_
    padf = lambda t: jnp.pad(t, [(0, 0), (0, pad)] + [(0, 0)] * (t.ndim - 2))
    xdt = padf(xh.astype(jnp.float32) * dt[..., None]).reshape(b_, nc, cl, g_, e_, p_)
    dta = padf(dt * a).reshape(b_, nc, cl, g_, e_).transpose(0, 1, 3, 4, 2)
    bm = padf(bm.astype(jnp.float32)).reshape(b_, nc, cl, g_, n_)
    cm = padf(cm.astype(jnp.float32)).reshape(b_, nc, cl, g_, n_)
    acs = jnp.cumsum(dta, axis=-1)
    causal = jnp.tril(jnp.ones((cl, cl), dtype=bool))
    seg = acs[..., :, None] - acs[..., None, :]
    lmat = jnp.exp(jnp.where(causal, seg, -jnp.inf))
    cb = jnp.einsum('bcsgn,bctgn->bcgst', cm, bm)
    y_diag = jnp.einsum('bcgest,bctgep->bcsgep', cb[:, :, :, None] * lmat, xdt)
    decay_out = jnp.exp(acs[..., -1:] - acs)
    chunk_states = jnp.einsum('bcsgn,bcsgep->bcgepn', bm, xdt * decay_out.transpose(0, 1, 4, 2, 3)[..., None])
    chunk_decay = jnp.exp(acs[..., -1])

    def step(carry, inp):
        st, dec = inp
        return carry * dec[..., None, None] + st, carry

    s0g = s0.astype(jnp.float32).reshape(b_, g_, e_, p_, n_)
    s_final, s_prev = lax.scan(step, s0g, (jnp.moveaxis(chunk_states, 1, 0), jnp.moveaxis(chunk_decay, 1, 0)))
    s_prev = jnp.moveaxis(s_prev, 0, 1)
    c_dec = cm[:, :, :, :, None, :] * jnp.exp(acs).transpose(0, 1, 4, 2, 3)[..., None]
    y_off = jnp.einsum('bcsgen,bcgepn->bcsgep', c_dec, s_prev)
    y = (y_diag + y_off).reshape(b_, nc * cl, h_, p_)[:, :l_]
    return y, s_final.reshape(b_, h_, p_, n_)


def mixer(u, conv_buf, ssm_conv_buf, ssm_state, w_in, conv_w, conv_norm_w, ssm_conv_w, ssm_conv_b,
          dt_bias, a_log, d_skip, ssm_norm_w, w_out):
    b_, l_ = u.shape[0], u.shape[1]
    proj = jnp.einsum('bld,dk->blk', u, w_in)
    cuts = [CONV_CH, 2 * CONV_CH, 3 * CONV_CH, 3 * CONV_CH + SSM_CH, 3 * CONV_CH + SSM_CH + XBC_CH]
    gb, gc, hv, z, xbc, dt_raw = jnp.split(proj, cuts, axis=-1)
    cv, new_conv_buf = causal_dwconv(gc * hv, conv_buf, conv_w)
    y_conv = group_rms_norm(gb * cv, conv_norm_w, CONV_GROUPS)
    xbc_c, new_ssm_conv_buf = causal_dwconv(xbc, ssm_conv_buf, ssm_conv_w)
    xbc_c = jax.nn.silu(xbc_c + ssm_conv_b)
    xs, bm, cm = jnp.split(xbc_c, [SSM_CH, SSM_CH + SSM_GROUPS * SSM_STATE], axis=-1)
    xh = xs.reshape(b_, l_, SSM_HEADS, SSM_HEAD_DIM)
    bm = bm.reshape(b_, l_, SSM_GROUPS, SSM_STATE)
    cm = cm.reshape(b_, l_, SSM_GROUPS, SSM_STATE)
    dt = jax.nn.softplus(dt_raw.astype(jnp.float32) + dt_bias.astype(jnp.float32))
    a = -jnp.exp(a_log.astype(jnp.float32))
    y, new_state = ssd_chunked(xh, dt, a, bm, cm, ssm_state)
    y = y + xh.astype(jnp.float32) * d_skip.astype(jnp.float32)[:, None]
    y = y.reshape(b_, l_, SSM_CH) * jax.nn.silu(z.astype(jnp.float32))
    y_ssm = group_rms_norm(y, ssm_norm_w, SSM_GROUPS).astype(u.dtype)
    out = jnp.einsum('blk,kd->bld', jnp.concatenate([y_conv, y_ssm], axis=-1), w_out)
    return out, new_conv_buf, new_ssm_conv_buf, new_state.astype(ssm_state.dtype)


def decoder_layer(x, c, conv_buf, ssm_conv_buf, ssm_state, w_ada, b_ada, w_in, conv_w, conv_norm_w,
                  ssm_conv_w, ssm_conv_b, dt_bias, a_log, d_skip, ssm_norm_w, w_out,
                  ln1_g, ln1_b, w_up, w_down, ln2_g, ln2_b):
    mod = (c @ w_ada + b_ada)[:, None, :]
    sh1, sc1, g1, sh2, sc2, g2 = jnp.split(mod, 6, axis=-1)
    u = x * (1 + sc1) + sh1
    m, new_conv, new_ssm_conv, new_ssm = mixer(u, conv_buf, ssm_conv_buf, ssm_state, w_in, conv_w, conv_norm_w,
                                               ssm_conv_w, ssm_conv_b, dt_bias, a_log, d_skip, ssm_norm_w, w_out)
    x = layer_norm(ALPHA * x + (1 + g1) * m, ln1_g, ln1_b)
    v = x * (1 + sc2) + sh2
    hid = jnp.square(jax.nn.relu(jnp.einsum('bld,df->blf', v, w_up)))
    x = layer_norm(ALPHA * x + (1 + g2) * jnp.einsum('blf,fd->bld', hid, w_down), ln2_g, ln2_b)
    return x, new_conv, new_ssm_conv, new_ssm


def setup_inputs(seed: int = 0) -> dict:
    key = jax.random.key(seed)
    ks = jax.random.split(key, 32)
    f32 = jnp.float32
    nrm = lambda k, shape, s: jax.random.normal(k, shape, f32) * s
    dt0 = jnp.exp(jax.random.uniform(ks[20], (DEPTH, SSM_HEADS), f32, np.log(1e-3), np.log(1e-1)))
    return {
        'x_prompt': nrm(ks[0], (BATCH, SEQ, D_MODEL), 1.0),
        'x_sample': nrm(ks[1], (DEC_BATCH, DEC_SEQ, D_MODEL), 1.0),
        'state_conv': nrm(ks[2], (DEPTH, DEC_BATCH, SHORT_CONV_W - 1, CONV_CH), 1.0),
        'state_ssm_conv': nrm(ks[3], (DEPTH, DEC_BATCH, SSM_CONV_W - 1, XBC_CH), 1.0),
        'state_ssm': nrm(ks[4], (DEPTH, DEC_BATCH, SSM_HEADS, SSM_HEAD_DIM, SSM_STATE), 0.1),
        'c_prompt': nrm(ks[5], (BATCH, D_MODEL), 1.0),
        'c_sample': nrm(ks[6], (DEC_BATCH, D_MODEL), 1.0),
        'w_ada': nrm(ks[7], (DEPTH, D_MODEL, 6 * D_MODEL), 0.1 * D_MODEL ** -0.5),
        'b_ada': nrm(ks[8], (DEPTH, 6 * D_MODEL), 0.01),
        'w_in': nrm(ks[9], (DEPTH, D_MODEL, IN_COLS), D_MODEL ** -0.5),
        'conv_w': nrm(ks[10], (DEPTH, SHORT_CONV_W, CONV_CH), SHORT_CONV_W ** -0.5),
        'conv_norm_w': 1.0 + nrm(ks[11], (DEPTH, CONV_CH), 0.02),
        'ssm_conv_w': nrm(ks[12], (DEPTH, SSM_CONV_W, XBC_CH), SSM_CONV_W ** -0.5),
        'ssm_conv_b': nrm(ks[13], (DEPTH, XBC_CH), 0.01),
        'dt_bias': dt0 + jnp.log(-jnp.expm1(-dt0)),
        'a_log': jnp.log(jax.random.uniform(ks[14], (DEPTH, SSM_HEADS), f32, 1.0, 16.0)),
        'd_skip': 1.0 + nrm(ks[15], (DEPTH, SSM_HEADS), 0.02),
        'ssm_norm_w': 1.0 + nrm(ks[16], (DEPTH, SSM_CH), 0.02),
        'w_out': nrm(ks[17], (DEPTH, MIX_WIDTH, D_MODEL), BETA * MIX_WIDTH ** -0.5),
        'ln1_g': 1.0 + nrm(ks[18], (DEPTH, D_MODEL), 0.02),
        'ln1_b': nrm(ks[19], (DEPTH, D_MODEL), 0.01),
        'w_up': nrm(ks[21], (DEPTH, D_MODEL, D_FF), D_MODEL ** -0.5),
        'w_down': nrm(ks[22], (DEPTH, D_FF, D_MODEL), BETA * D_FF ** -0.5),
        'ln2_g': 1.0 + nrm(ks[23], (DEPTH, D_MODEL), 0.02),
        'ln2_b': nrm(ks[24], (DEPTH, D_MODEL), 0.01),
    }


def reference(x_prompt, x_sample, state_conv, state_ssm_conv, state_ssm, c_prompt, c_sample,
              w_ada, b_ada, w_in, conv_w, conv_norm_w, ssm_conv_w, ssm_conv_b, dt_bias, a_log, d_skip,
              ssm_norm_w, w_out, ln1_g, ln1_b, w_up, w_down, ln2_g, ln2_b):
    bp = x_prompt.shape[0]
    zero_conv = jnp.zeros((bp, SHORT_CONV_W - 1, CONV_CH), x_prompt.dtype)
    zero_ssm_conv = jnp.zeros((bp, SSM_CONV_W - 1, XBC_CH), x_prompt.dtype)
    zero_ssm = jnp.zeros((bp, SSM_HEADS, SSM_HEAD_DIM, SSM_STATE), state_ssm.dtype)
    hp, hs = x_prompt, x_sample
    p_conv, p_sconv, p_ssm, s_conv, s_sconv, s_ssm = [], [], [], [], [], []
    for i in range(DEPTH):
        lw = (w_ada[i], b_ada[i], w_in[i], conv_w[i], conv_norm_w[i], ssm_conv_w[i], ssm_conv_b[i],
              dt_bias[i], a_log[i], d_skip[i], ssm_norm_w[i], w_out[i], ln1_g[i], ln1_b[i],
              w_up[i], w_down[i], ln2_g[i], ln2_b[i])
        hp, pc, psc, pss = decoder_layer(hp, c_prompt, zero_conv, zero_ssm_conv, zero_ssm, *lw)
        hs, sc, ssc, sss = decoder_layer(hs, c_sample, state_conv[i], state_ssm_conv[i], state_ssm[i], *lw)
        p_conv.append(pc); p_sconv.append(psc); p_ssm.append(pss)
        s_conv.append(sc); s_sconv.append(ssc); s_ssm.append(sss)
    return (hp, hs, jnp.stack(p_conv), jnp.stack(p_sconv), jnp.stack(p_ssm),
            jnp.stack(s_conv), jnp.stack(s_sconv), jnp.stack(s_ssm))
```

```python
import os
import numpy as np
from contextlib import ExitStack
import concourse.bass as bass
import concourse.mybir as mybir
from concourse.bass_utils import run_bass_kernel_spmd

F32 = mybir.dt.float32
BF16 = mybir.dt.bfloat16
AF = mybir.ActivationFunctionType
ALU = mybir.AluOpType

ENGS = ("pe", "act", "dve", "pool", "sp")

T = 2048
NS = 16
NTOK = T + NS
ALPHA = 2.0 ** 0.25
LN_EPS = 1e-5 / (ALPHA * ALPHA)
RMS_EPS = 1e-5
NSLOT = 6
TILES = [(0, 512), (512, 512), (1024, 512), (1536, 512), (2048, 16)]

V_BADA, V_CW, V_CNW, V_SCW, V_SCB = 0, 48, 72, 80, 128
V_L1G, V_L1B, V_L2G, V_L2B = 140, 148, 156, 164
V_DCH, V_ALCH, V_DTBCH, V_SNWCH = 172, 180, 188, 196
V_ALB, V_DTB, V_DSB, V_SNWB = 204, 220, 236, 252
NV = 252 + 1024
C_ID, C_MLE, C_MGT, C_BONES, C_ONES = 0, 128, 256, 384, 512
NC_ = 640


KSTOP = int(os.environ.get('KSTOP', '99'))
KSUB = int(os.environ.get('KSUB', '99'))
QA = int(os.environ.get('QA', '8'))
QB = int(os.environ.get('QB', '12'))
QC = int(os.environ.get('QC', '2'))
QORD = os.environ.get('QORD', 'BAC')
KI = int(os.environ.get('KI', '99'))


class _Stop(Exception):
    pass


class Sched:
    def __init__(self, nc, es):
        self.nc = nc
        self.es = es
        self.q = {e: [] for e in ENGS}
        self.sems = {}
        self.cnt = {}
        self.waited = {e: {} for e in ENGS}
        self.lastw = {}
        self.readers = {}
        for e in ENGS:
            self._sem("E_" + e)

    def _sem(self, name):
        if name not in self.sems:
            self.sems[name] = self.es.enter_context(self.nc.semaphore(name))
            self.cnt[name] = 0
        return self.sems[name]

    def op(self, eng, fn, r=(), w=(), dma=None):
        deps = {}

        def add(d):
            if d is None:
                return
            s, v, e2 = d
            if e2 == "pe" and eng == "pe" and dma is None:
                return
            if deps.get(s, 0) < v:
                deps[s] = v

        w = list(w) + [k for k in r if k.startswith("pb") and k not in w]
        for k in r:
            add(self.lastw.get(k))
        for k in w:
            add(self.lastw.get(k))
            for d in self.readers.get(k, ()):
                add(d)
        waits = []
        for s, v in deps.items():
            if self.waited[eng].get(s, 0) < v:
                self.waited[eng][s] = v
                waits.append((s, v))
        if dma is not None:
            sname = "D_" + dma
            self._sem(sname)
            self.cnt[sname] += 16
            me = (sname, self.cnt[sname], "dma")
            inc = 16
        else:
            sname = "E_" + eng
            self.cnt[sname] += 1
            me = (sname, self.cnt[sname], eng)
            inc = 1
        self.q[eng].append((waits, fn, sname, inc))
        for k in w:
            self.lastw[k] = me
            self.readers[k] = []
        for k in r:
            self.readers.setdefault(k, []).append(me)
        return me

    def fence(self):
        snap = dict(self.cnt)
        for e in ENGS:
            waits = []
            for s, v in snap.items():
                if v > 0 and self.waited[e].get(s, 0) < v and s != "E_" + e:
                    self.waited[e][s] = v
                    waits.append((s, v))
            if waits:
                self.q[e].append((waits, None, None, 0))

    def emit(self, block):
        sems = self.sems
        fin = [(s, v) for s, v in self.cnt.items() if v > 0]

        def make(ename):
            def body(eng):
                for waits, fn, sname, inc in self.q[ename]:
                    for s, v in waits:
                        eng.wait_ge(sems[s], v)
                    if fn is None:
                        continue
                    inst = fn(eng)
                    inst.then_inc(sems[sname], inc)
                if ename == "sp":
                    for s, v in fin:
                        eng.wait_ge(sems[s], v)
            return body

        block.tensor(make("pe"))
        block.scalar(make("act"))
        block.vector(make("dve"))
        block.gpsimd(make("pool"))
        block.sync(make("sp"))


def build_program():
    nc = bass.Bass("TRN2", target_bir_lowering=False)
    din = lambda n, sh: nc.dram_tensor(n, sh, F32, kind="ExternalInput").ap()
    dout = lambda n, sh: nc.dram_tensor(n, sh, F32, kind="ExternalOutput").ap()
    xT_d = din("xT", [1024, T])
    xsT_d = din("xsT", [1024, NS])
    cT_d = din("cT", [1024, 17])
    stc_d = din("stc", [128, 8 * NS * 2])
    stsc_d = din("stsc", [128, 12 * NS * 3])
    sts_d = din("sts", [NS, 16, 64, 128])
    w_ada_d = din("w_ada", [1024, 6144])
    w_in_d = din("w_in", [1024, 5648])
    wdtx_d = din("wdtx", [1024, 1024])
    w_out_d = din("w_out", [2048, 1024])
    w_up_d = din("w_up", [1024, 4096])
    w_down_d = din("w_down", [4096, 1024])
    vecs_d = din("vecs", [128, NV])
    cst_d = din("consts", [128, NC_])
    yT_d = dout("yT", [1024, T])
    ysT_d = dout("ysT", [1024, NS])
    ncp_d = dout("ncp", [128, 16])
    nscp_d = dout("nscp", [128, 36])
    nsp_d = dout("nsp", [128, 1024])
    ncs_d = dout("ncs", [128, 8 * NS * 2])
    nscs_d = dout("nscs", [128, 12 * NS * 3])
    nss_d = dout("nss", [NS, 16, 64, 128])

    with ExitStack() as es:
        S = Sched(nc, es)
        sb = lambda n, sh, dt=F32: es.enter_context(nc.sbuf_tensor("s_" + n, sh, dt))
        R1 = sb("R1", [128, 16 * NTOK], BF16)
        R2 = sb("R2", [128, 8 * NTOK], F32)
        U = sb("U", [128, 8 * NTOK], BF16)
        ring = [sb(f"wr{i}", [128, 8, 256], BF16) for i in range(NSLOT)]
        vecs = sb("vecs", [128, NV])
        cst = sb("cst", [128, NC_])
        mod = sb("mod", [128, 48, 17])
        cTf = sb("cTf", [128, 8, 17])
        cTb = sb("cTb", [128, 8, 17], BF16)
        xs = sb("xs", [128, 8, NS])
        identb = sb("identb", [128, 128], BF16)
        bonesb = sb("bonesb", [128, 128], BF16)
        onesb = sb("onesb", [128, 128], BF16)
        aneg = sb("aneg", [128, 16])
        anegch = sb("anegch", [128, 8])
        wdt = sb("wdt", [128, 8, 16], BF16)
        A2 = sb("A2", [128, 8])
        B2 = sb("B2", [128, 8])
        ncp_sb = sb("ncp_sb", [128, 8, 2])
        nscp_sb = sb("nscp_sb", [128, 12, 3])
        stc = sb("stc", [128, 8, NS, 2])
        stsc = sb("stsc", [128, 12, NS, 3])
        ncs_sb = sb("ncs_sb", [128, 8, NS, 2])
        nscs_sb = sb("nscs_sb", [128, 12, NS, 3])
        xcs = sb("xcs", [128, 12, NS])
        small = sb("small", [128, 64])
        pb = [es.enter_context(nc.psum_tensor(f"pb{i}", [128, 512], F32)) for i in range(8)]
        pbb = [p.bitcast(BF16) for p in pb]

        R1b = R1
        ymix = R1[:, :].rearrange("p (k t) -> p k t", t=NTOK)
        Uv = U[:, :].rearrange("p (k t) -> p k t", t=NTOK)
        R2b = R2.bitcast(BF16)
        Ub32 = U.bitcast(F32)

        def r2f(off, n):
            return R2[:, off:off + n]

        def r2b(off_f32, n):
            return R2b[:, 2 * off_f32:2 * off_f32 + n]

        def act(out, in_, func=AF.Copy, r=(), w=(), **kw):
            S.op("act", lambda e: e.activation(out=out, in_=in_, func=func, **kw), r=r, w=w)

        def tt(out, in0, in1, op, r=(), w=(), eng="dve"):
            S.op(eng, lambda e: e.tensor_tensor(out=out, in0=in0, in1=in1, op=op), r=r, w=w)

        def ts(out, in0, s1, s2=None, op0=ALU.mult, op1=None, r=(), w=(), eng="dve"):
            if op1 is None:
                S.op(eng, lambda e: e.tensor_scalar(out=out, in0=in0, scalar1=s1, scalar2=None, op0=op0), r=r, w=w)
            else:
                S.op(eng, lambda e: e.tensor_scalar(out=out, in0=in0, scalar1=s1, scalar2=s2, op0=op0, op1=op1), r=r, w=w)

        def stt(out, in0, scalar, in1, op0, op1, r=(), w=(), accum=None):
            if accum is None:
                S.op("dve", lambda e: e.scalar_tensor_tensor(out=out, in0=in0, scalar=scalar, in1=in1, op0=op0, op1=op1), r=r, w=w)
            else:
                S.op("dve", lambda e: e.scalar_tensor_tensor(out=out, in0=in0, scalar=scalar, in1=in1, op0=op0, op1=op1, accum_out=accum), r=r, w=w)

        def mmg(out, pairs, r=(), w=()):
            def f(e):
                n = len(pairs)
                last = None
                for i, (l, rr) in enumerate(pairs):
                    last = e.matmul(out, lhsT=l, rhs=rr, start=(i == 0), stop=(i == n - 1))
                return last
            S.op("pe", f, r=r, w=w)

        def trg(items, r=(), w=()):
            def f(e):
                last = None
                for (o, i_, idn) in items:
                    last = e.transpose(out=o, in_=i_, identity=idn)
                return last
            S.op("pe", f, r=r, w=w)

        def dma(eng, out, in_, key, r=(), w=()):
            S.op(eng, lambda e: e.dma_start(out=out, in_=in_), r=r, w=w, dma=key)

        def bc(ap, shape):
            return ap.broadcast_to(shape)

        blocks = []

        def addblk(wd, r0, c0, ncols=256):
            blocks.append((wd[r0:r0 + 1024, c0:c0 + ncols].rearrange("(k p) c -> p k c", p=128), ncols))
            return len(blocks) - 1

        wstate = {"next": 0}

        def wneed(i, base=None):
            lim = min(len(blocks), (i if base is None else base) + NSLOT)
            while wstate["next"] < lim:
                j = wstate["next"]
                src, ncols = blocks[j]
                sl = j % NSLOT
                dma("pool", ring[sl][:, :, 0:ncols], src, f"wr{sl}", w=[f"wr{sl}"])
                wstate["next"] += 1
            assert wstate["next"] > i, ("weight block not loaded", i, base)
            return ring[i % NSLOT], f"wr{i % NSLOT}"

        B_XBC = [addblk(w_in_d, 0, 4096 + c * 256) for c in range(6)]
        B_DTX = [addblk(wdtx_d, 0, c * 256) for c in range(4)]
        B_ADA = [None] * 8 + [addblk(w_ada_d, 0, c * 256) for c in range(8, 24)]
        B_CONV = []
        for kb in range(4):
            B_CONV.append([addblk(w_in_d, 0, g * 1024 + kb * 256) for g in (1, 2, 0)])
        B_OUT = []
        for cb in range(4):
            B_OUT.append([addblk(w_out_d, h * 1024, cb * 256) for h in range(2)])
        B_UP, B_DN = [], []
        for q in range(4):
            B_UP.append([addblk(w_up_d, 0, q * 1024 + i * 256) for i in range(4)])
            B_DN.append([addblk(w_down_d, q * 1024, i * 256) for i in range(4)])

        bank_rr = {"i": 0}

        def nb():
            i = bank_rr["i"] % 8
            bank_rr["i"] += 1
            return i

        def phases():
            nonlocal o
            if KSTOP < 0:
                return
            dma("sp", vecs[:], vecs_d, "vecs", w=["vecs"])
            dma("sp", cst[:], cst_d, "cst", w=["cst"])
            dma("sp", cTf[:], cT_d.rearrange("(k p) c -> p k c", p=128), "cT", w=["cTf"])
            dma("sp", xs[:], xsT_d.rearrange("(k p) c -> p k c", p=128), "xs", w=["xs"])
            dma("sp", stc[:].rearrange("p a b c -> p (a b c)"), stc_d, "stc", w=["stc"])
            dma("sp", stsc[:].rearrange("p a b c -> p (a b c)"), stsc_d, "stsc", w=["stsc"])
            dma("pool", wdt[:], w_in_d[:, 5632:5648].rearrange("(k p) c -> p k c", p=128), "wdt", w=["wdt"])
            act(cTb[:], cTf[:], r=["cTf"], w=["cTb"])
            act(identb[:], cst[:, C_ID:C_ID + 128], r=["cst"], w=["identb"])
            act(bonesb[:], cst[:, C_BONES:C_BONES + 128], r=["cst"], w=["bonesb"])
            act(onesb[:], cst[:, C_ONES:C_ONES + 128], r=["cst"], w=["onesb"])
            act(aneg[:], vecs[:, V_ALB:V_ALB + 16], AF.Exp, r=["vecs"], w=["aneg"])
            ts(aneg[:], aneg[:], -1.0, r=["aneg"], w=["aneg"])
            act(anegch[:], vecs[:, V_ALCH:V_ALCH + 8], AF.Exp, r=["vecs"], w=["anegch"])
            ts(anegch[:], anegch[:], -1.0, r=["anegch"], w=["anegch"])

            if KSTOP < 1:
                return
            def ada_chunks(c0, ncks):
                bk = nb()
                for cl in range(ncks):
                    c = c0 + cl
                    slot, key = wneed(B_ADA[c // 2])
                    cc = c % 2
                    mmg(pb[bk][:, cl * 32:cl * 32 + 17],
                        [(slot[:, kk, cc * 128:(cc + 1) * 128], cTb[:, kk, :]) for kk in range(8)],
                        r=[key, "cTb"], w=[f"pb{bk}"])
                tt(mod[:, c0:c0 + ncks, :],
                   pb[bk][:, 0:ncks * 32].rearrange("p (c x) -> p c x", x=32)[:, :, 0:17],
                   bc(vecs[:, V_BADA + c0:V_BADA + c0 + ncks].unsqueeze(2), [128, ncks, 17]),
                   ALU.add, r=[f"pb{bk}", "vecs"], w=["mod" if c0 < 16 else "mod2"])
            wa32 = R1.bitcast(F32)[:, 0:16384].rearrange("p (k c) -> p k c", c=2048)
            for q4 in range(4):
                dma("sp", wa32[:, :, q4 * 512:(q4 + 1) * 512], w_ada_d[:, q4 * 512:(q4 + 1) * 512].rearrange("(k p) c -> p k c", p=128),
                    f"wa{q4}", w=[f"wa{q4}"])
            bk = nb()
            for c in range(16):
                mmg(pb[bk][:, c * 32:c * 32 + 17], [(wa32[:, kk, c * 128:(c + 1) * 128], cTf[:, kk, :]) for kk in range(8)],
                    r=[f"wa{c // 4}", "cTf"], w=[f"pb{bk}"])
            tt(mod[:, 0:16, :], pb[bk][:, :].rearrange("p (c x) -> p c x", x=32)[:, :, 0:17],
               bc(vecs[:, V_BADA:V_BADA + 16].unsqueeze(2), [128, 16, 17]), ALU.add, r=[f"pb{bk}", "vecs"], w=["mod"])
            ts(mod[:, 8:16, :], mod[:, 8:16, :], 1.0, op0=ALU.add, r=["mod"], w=["mod"])

            def ada_finish():
                ts(mod[:, 32:40, :], mod[:, 32:40, :], 1.0, op0=ALU.add, r=["mod2"], w=["mod2"])
                ts(mod[:, 16:24, :], mod[:, 16:24, :], 1.0, 1.0 / ALPHA, op0=ALU.add, op1=ALU.mult, r=["mod2"], w=["mod2"])
                ts(mod[:, 40:48, :], mod[:, 40:48, :], 1.0, 1.0 / ALPHA, op0=ALU.add, op1=ALU.mult, r=["mod2"], w=["mod2"])
                tt(A2[:], vecs[:, V_L1G:V_L1G + 8], mod[:, 32:40, 0], ALU.mult, r=["vecs", "mod2"], w=["A2"])
                tt(B2[:], vecs[:, V_L1B:V_L1B + 8], mod[:, 32:40, 0], ALU.mult, r=["vecs", "mod2"], w=["B2"])
                tt(B2[:], B2[:], mod[:, 24:32, 0], ALU.add, r=["B2", "mod2"], w=["B2"])

            if KSTOP < 2:
                return
            xTr = xT_d.rearrange("(k p) t -> p k t", p=128)
            xin = [r2f(i * 4096, 4096).rearrange("p (k t) -> p k t", t=512) for i in range(2)]
            for i in range(4):
                t0 = i * 512
                b = i % 2
                dma("sp", xin[b], xTr[:, :, t0:t0 + 512], f"xin{b}", w=[f"xin{b}"])
                for kk in range(8):
                    if kk % 2 == 0:
                        act(Uv[:, kk, t0:t0 + 512], xin[b][:, kk, :], AF.Identity, r=[f"xin{b}", "mod"], w=[f"U{i}"],
                            scale=mod[:, 8 + kk, 0:1], bias=mod[:, kk, 0:1])
                    else:
                        ts(Uv[:, kk, t0:t0 + 512], xin[b][:, kk, :], mod[:, 8 + kk, 0:1], mod[:, kk, 0:1], op0=ALU.mult, op1=ALU.add,
                           r=[f"xin{b}", "mod"], w=[f"U{i}"])
            us_tmp = small[:, 0:0]
            ustmp = xcs[:, 0:8, :]
            tt(ustmp, xs[:], mod[:, 8:16, 1:17], ALU.mult, r=["xs", "mod"], w=["ustmp"])
            tt(Uv[:, :, T:NTOK], ustmp, mod[:, 0:8, 1:17], ALU.add, r=["ustmp", "mod"], w=["U4"])
            S.fence()

            if KSTOP < 3:
                return
            BCT = r2b(0, 4 * T).rearrange("p (k t) -> p k t", t=T)
            wz = R1[:, 0:8192].rearrange("p (k c) -> p k c", c=1024)
            for i in range(4):
                dma("pool", wz[:, :, i * 256:(i + 1) * 256], w_in_d[:, 3072 + i * 256:3072 + (i + 1) * 256].rearrange("(k p) c -> p k c", p=128),
                    f"wz{i}", w=["wz"])
            pre = [r2f(4096 + i * 2064, 2051) for i in range(2)]
            ctmp = [[r2f(8224 + (i * 2 + j) * 512, 512) for j in range(2)] for i in range(4)]
            for i in range(2):
                S.op("dve", lambda e, i=i: e.memset(pre[i][:, 0:3], 0.0), w=[f"pre{i}z"])
            it = 0
            pend = []
            for blk in range(6):
                for cc in range(2):
                    kx = blk * 2 + cc
                    slot, key = wneed(B_XBC[blk])
                    pbuf = kx % 2
                    prb = pre[pbuf]
                    wcol = lambda tap, kx=kx: vecs[:, V_SCW + tap * 12 + kx:V_SCW + tap * 12 + kx + 1]
                    for i, (t0, n) in enumerate(TILES):
                        bk = nb()
                        mmg(pb[bk][:, 0:n], [(slot[:, kk, cc * 128:(cc + 1) * 128], Uv[:, kk, t0:t0 + n]) for kk in range(8)],
                            r=[key, f"U{i}"], w=[f"pb{bk}"])
                        if i < 4:
                            tb = it % 4
                            it += 1
                            c0, c1 = ctmp[tb]
                            act(prb[:, 3 + t0:3 + t0 + n], pb[bk][:, 0:n], r=[f"pb{bk}", f"pre{pbuf}z"], w=[f"pre{pbuf}_{i}"])
                            rd = [f"pre{pbuf}_{i}"] + ([f"pre{pbuf}_{i - 1}"] if i > 0 else [f"pre{pbuf}z"])
                            act(c0, prb[:, t0:t0 + n], AF.Identity, r=rd + ["vecs"], w=[f"c0_{tb}"], scale=wcol(0))
                            if pend:
                                pend.pop()()
                            stt(c1, prb[:, t0 + 1:t0 + 1 + n], wcol(1), c0, ALU.mult, ALU.add, r=rd + [f"c0_{tb}"], w=[f"c1_{tb}"])
                            stt(c0, prb[:, t0 + 2:t0 + 2 + n], wcol(2), c1, ALU.mult, ALU.add, r=rd + [f"c1_{tb}"], w=[f"c0_{tb}"])
                            stt(c1, pb[bk][:, 0:n], wcol(3), c0, ALU.mult, ALU.add, r=[f"pb{bk}", f"c0_{tb}"], w=[f"c1_{tb}"])
                            if kx < 8:
                                dst = ymix[:, 8 + kx, t0:t0 + n]
                                wk = [f"xbf{kx}_{i}"]
                            else:
                                dst = BCT[:, kx - 8, t0:t0 + n]
                                wk = [f"BCT{i}"]
                            pend.append(lambda dst=dst, c1=c1, tb=tb, wk=wk, kx=kx: act(dst, c1, AF.Silu, r=[f"c1_{tb}", "vecs"], w=wk,
                                                                                         bias=vecs[:, V_SCB + kx:V_SCB + kx + 1]))
                        else:
                            if pend:
                                pend.pop()()
                            act(nscs_sb[:, kx, :, 2], pb[bk][:, 0:n], r=[f"pb{bk}"], w=["xbcs"])
                            cs = small[:, 0:16]
                            ts(cs, stsc[:, kx, :, 0], wcol(0), r=["stsc", "vecs"], w=["cs"])
                            stt(cs, stsc[:, kx, :, 1], wcol(1), cs, ALU.mult, ALU.add, r=["cs", "stsc"], w=["cs"])
                            stt(cs, stsc[:, kx, :, 2], wcol(2), cs, ALU.mult, ALU.add, r=["cs", "stsc"], w=["cs"])
                            stt(cs, nscs_sb[:, kx, :, 2], wcol(3), cs, ALU.mult, ALU.add, r=["cs", "xbcs"], w=["cs"])
                            act(xcs[:, kx, :], cs, AF.Silu, r=["cs", "vecs"], w=["xcs"], bias=vecs[:, V_SCB + kx:V_SCB + kx + 1])
                            S.op("pool", lambda e, kx=kx: e.tensor_copy(out=nscs_sb[:, kx, :, 0:2], in_=stsc[:, kx, :, 1:3]), r=["stsc"], w=["nscs_a"])
                    S.op("pool", lambda e, kx=kx, prb=prb: e.tensor_copy(out=nscp_sb[:, kx, :], in_=prb[:, T:T + 3]),
                         r=[f"pre{pbuf}_3"], w=["nscp_sb"])
            if pend:
                pend.pop()()
            dma("sp", nscp_d, nscp_sb[:].rearrange("p a b -> p (a b)"), "nscp", r=["nscp_sb"])
            dma("sp", nscs_d, nscs_sb[:].rearrange("p a b c -> p (a b c)"), "nscs", r=["nscs_a", "xbcs"])

            dts = sb("dts", [128, 8, NS])
            bk = nb()
            for k in range(8):
                slot, key = wneed(B_DTX[k // 2])
                cc = k % 2
                mmg(pb[bk][:, k * 16:(k + 1) * 16], [(slot[:, kk, cc * 128:(cc + 1) * 128], Uv[:, kk, T:NTOK]) for kk in range(8)],
                    r=[key, "U4"], w=[f"pb{bk}"])
            act(dts[:].rearrange("p a b -> p (a b)"), pb[bk][:, 0:128], r=[f"pb{bk}"], w=["dts"])
            S.fence()

            if KSTOP < 4:
                return
            if KSUB < -3:
                return
            o = 4096
            def alloc_f(n):
                nonlocal o
                a = o
                o += n
                return a
            Sst = r2f(alloc_f(1024), 1024)
            Sbf = r2b(alloc_f(512), 1024)
            dtall = r2f(alloc_f(256), 256)
            dta = r2f(alloc_f(256), 256)
            acs = r2f(alloc_f(256), 256)
            Ecol = r2f(alloc_f(256), 256)
            dec = r2f(alloc_f(256), 256)
            wst = r2f(alloc_f(256), 256)
            r1free = 8192
            def r1b(n):
                nonlocal r1free
                a = r1free
                r1free += n
                return R1[:, a:a + n]
            xTb = [r1b(1024) for _ in range(2)]
            xdt = [r1b(1024) for _ in range(2)]
            xdtw = [r1b(1024) for _ in range(2)]
            Mh = [r1b(2048).rearrange("p (h s) -> p h s", s=128), r2b(alloc_f(1024), 2048).rearrange("p (h s) -> p h s", s=128)]
            assert r1free <= 8 * NTOK
            Btok = [r2b(alloc_f(128), 256) for _ in range(2)]
            Ah4 = [r2f(alloc_f(512), 512) for _ in range(2)]
            Lh4 = [r2f(alloc_f(512), 512) for _ in range(2)]
            CBm = [r2f(alloc_f(256), 256).rearrange("p (g s) -> p g s", s=128) for _ in range(2)]
            tA = [r2f(alloc_f(1024), 1024) for _ in range(2)]
            tB = r2f(alloc_f(1024), 1024)
            yn = [r2b(alloc_f(512), 1024) for _ in range(2)]
            DI = r2b(alloc_f(1024), 2048).rearrange("p (h s) -> p h s", s=128)
            for h in range(16):
                ts(DI[:, h, :], cst[:, C_ID:C_ID + 128], vecs[:, V_DSB + h:V_DSB + h + 1], r=["cst", "vecs"], w=["DI"])
            assert o <= 8 * NTOK, o
            mle = cst[:, C_MLE:C_MLE + 128]
            mgt = cst[:, C_MGT:C_MGT + 128]

            def softplus(dst, src, bias_bc, inner, keys_r, key_w, tmp1, tmp2):
                v3 = lambda a: a.rearrange("p (a b) -> p a b", b=inner)
                tt(v3(tmp1), src, bias_bc, ALU.add, r=keys_r, w=[key_w + "_t1"])
                act(tmp2, tmp1, AF.Abs, r=[key_w + "_t1"], w=[key_w + "_t2"])
                act(tmp2, tmp2, AF.Exp, r=[key_w + "_t2"], w=[key_w + "_t2"], scale=-1.0)
                act(tmp2, tmp2, AF.Ln, r=[key_w + "_t2"], w=[key_w + "_t2"], bias=1.0)
                ts(tmp1, tmp1, 0.0, op0=ALU.max, r=[key_w + "_t1"], w=[key_w + "_t1"])
                tt(dst, tmp1, tmp2, ALU.add, r=[key_w + "_t1", key_w + "_t2"], w=[key_w])

            bk = nb()
            for c in range(16):
                t0 = c * 128
                mmg(pb[bk][:, c * 16:(c + 1) * 16], [(Uv[:, kk, t0:t0 + 128], wdt[:, kk, :]) for kk in range(8)],
                    r=[f"U{c // 4}", "wdt"], w=[f"pb{bk}"])
            softplus(dtall, pb[bk][:, 0:256].rearrange("p (c h) -> p c h", h=16),
                     bc(vecs[:, V_DTB:V_DTB + 16].unsqueeze(1), [128, 16, 16]), 16, [f"pb{bk}", "vecs"], "dtall",
                     tA[0][:, 0:256], tA[1][:, 0:256])
            if KSUB < -2:
                return
            S.op("dve", lambda e: e.memset(Sst, 0.0), w=["Sst"])
            S.op("dve", lambda e: e.memset(Sbf, 0.0), w=["Sbf"])
            tt(dta.rearrange("p (c h) -> p c h", h=16), dtall.rearrange("p (c h) -> p c h", h=16),
               bc(aneg[:, :].unsqueeze(1), [128, 16, 16]), ALU.mult, r=["dtall", "aneg"], w=["dta"])
            bk1, bk2 = nb(), nb()
            mmg(pb[bk1][:, 0:256], [(mle, dta)], r=["cst", "dta"], w=[f"pb{bk1}"])
            mmg(pb[bk2][:, 0:256], [(cst[:, C_ONES:C_ONES + 128], dta)], r=["cst", "dta"], w=[f"pb{bk2}"])
            if KSUB < -1:
                return
            if KI < 0:
                return
            act(acs, pb[bk1][:, 0:256], r=[f"pb{bk1}"], w=["acs"])
            if KI < 1:
                return
            act(Ecol, pb[bk1][:, 0:256], AF.Exp, r=[f"pb{bk1}"], w=["Ecol"])
            if KI < 2:
                return
            act(dec, pb[bk2][:, 0:256], AF.Exp, r=[f"pb{bk2}"], w=["dec"])
            if KI < 3:
                return
            tt(wst, acs, pb[bk2][:, 0:256], ALU.subtract, r=[f"pb{bk2}", "acs", "dec", "Ecol"], w=["wst"])
            if KI < 4:
                return
            act(wst, wst, AF.Exp, r=["wst"], w=["wst"], scale=-1.0)
            if KI < 5:
                return
            tt(wst, wst, dtall, ALU.mult, r=["wst", "dtall"], w=["wst"])
            if KSUB < 1:
                return
            PT, PC, PSEG, PY, PO = 0, 1, (2, 3), (4, 5), (6, 7)
            S.op("dve", lambda e: e.memset(small[:, 20:22], -0.5), w=["negh"])

            def stageA(c):
                t0 = c * 128
                pa = c % 2
                trg([(pbb[PT][:, kx * 128:(kx + 1) * 128], ymix[:, 8 + kx, t0:t0 + 128], identb[:]) for kx in range(8)],
                    r=[f"xbf{kx}_{c // 4}" for kx in range(8)] + ["identb"], w=["pb0"])
                yield
                psT3 = pbb[PT][:, 0:1024].rearrange("p (h x) -> p h x", x=64)
                act(xTb[pa], pbb[PT][:, 0:1024], r=["pb0"], w=[f"xTb{pa}"])
                yield
                tt(xdt[pa].rearrange("p (h x) -> p h x", x=64), psT3, bc(dtall[:, c * 16:(c + 1) * 16].unsqueeze(2), [128, 16, 64]),
                   ALU.mult, r=["pb0", "dtall"], w=[f"xdt{pa}"])
                yield
                tt(xdtw[pa].rearrange("p (h x) -> p h x", x=64), psT3, bc(wst[:, c * 16:(c + 1) * 16].unsqueeze(2), [128, 16, 64]),
                   ALU.mult, r=["pb0", "wst"], w=[f"xdtw{pa}"])
                yield
                trg([(pbb[PC][:, 512 + g * 128:512 + (g + 1) * 128], BCT[:, g, t0:t0 + 128], identb[:]) for g in range(2)],
                    r=[f"BCT{c // 4}", "identb"], w=["pb1"])
                yield
                act(Btok[pa], pbb[PC][:, 512:768], r=["pb1"], w=[f"Btok{pa}"])
                yield
                def cbf(e, t0=t0):
                    last = None
                    for g in range(2):
                        last = e.matmul(pb[PC][:, g * 128:(g + 1) * 128], lhsT=BCT[:, g, t0:t0 + 128], rhs=BCT[:, 2 + g, t0:t0 + 128],
                                        start=True, stop=True)
                    return last
                S.op("pe", cbf, r=[f"BCT{c // 4}"], w=["pb1"])
                yield
                tt(CBm[pa], pb[PC][:, 0:256].rearrange("p (g s) -> p g s", s=128), bc(mle.unsqueeze(1), [128, 2, 128]),
                   ALU.mult, r=["pb1", "cst"], w=[f"CBm{pa}"])
                yield
                for q in range(4):
                    qb = q % 2
                    sbk = PSEG[qb]
                    a3 = Ah4[qb].rearrange("p (h t) -> p h t", t=128)
                    tt(a3, bc(mgt.unsqueeze(1), [128, 4, 128]), bc(dta[:, c * 16 + 4 * q:c * 16 + 4 * q + 4].unsqueeze(2), [128, 4, 128]),
                       ALU.mult, r=["cst", "dta"], w=[f"Ah{qb}"])
                    yield
                    def segf(e, qb=qb, sbk=sbk):
                        last = None
                        for j in range(4):
                            last = e.matmul(pb[sbk][:, j * 128:(j + 1) * 128], lhsT=Ah4[qb][:, j * 128:(j + 1) * 128], rhs=mle,
                                            start=True, stop=True)
                        return last
                    S.op("pe", segf, r=[f"Ah{qb}", "cst"], w=[f"pb{sbk}"])
                    yield
                    act(Lh4[qb], pb[sbk][:, :], AF.Exp, r=[f"pb{sbk}"], w=[f"Lh{qb}"])
                    yield
                    tt(Mh[pa][:, 4 * q:4 * q + 4, :], Lh4[qb].rearrange("p (h t) -> p h t", t=128),
                       bc(CBm[pa][:, q // 2, :].unsqueeze(1), [128, 4, 128]), ALU.mult,
                       r=[f"Lh{qb}", f"CBm{pa}"], w=[f"Mh{pa}_{q}"], eng="pool")
                    yield

            def stageB(c):
                t0 = c * 128
                pa = c % 2
                ab = c % 2
                for g in range(2):
                    mmg(pb[PO[g]][:, :], [(BCT[:, 2 + g, t0:t0 + 128], Sbf[:, g * 512:(g + 1) * 512])],
                        r=[f"BCT{c // 4}", "Sbf"], w=[f"pb{6 + g}"])
                    yield
                for h in range(16):
                    yb = PY[h // 8]
                    mmg(pb[yb][:, (h % 8) * 64:(h % 8 + 1) * 64],
                        [(Mh[pa][:, h, :], xdt[pa][:, h * 64:(h + 1) * 64]), (DI[:, h, :], xTb[pa][:, h * 64:(h + 1) * 64])],
                        r=[f"Mh{pa}_{h // 4}", f"xdt{pa}", f"xTb{pa}", "DI"], w=[f"pb{4 + h // 8}"])
                    yield
                for g in range(2):
                    tt(tA[ab][:, g * 512:(g + 1) * 512].rearrange("p (h x) -> p h x", x=64),
                       pb[PO[g]][:, :].rearrange("p (h x) -> p h x", x=64),
                       bc(Ecol[:, c * 16 + g * 8:c * 16 + g * 8 + 8].unsqueeze(2), [128, 8, 64]), ALU.mult,
                       r=[f"pb{6 + g}", "Ecol"], w=[f"tA{ab}_{g}"])
                    yield
                for g in range(2):
                    mmg(pb[PO[g]][:, :], [(Btok[pa][:, g * 128:(g + 1) * 128], xdtw[pa][:, g * 512:(g + 1) * 512])],
                        r=[f"Btok{pa}", f"xdtw{pa}"], w=[f"pb{6 + g}"])
                    yield
                tt(Sst.rearrange("p (h x) -> p h x", x=64), Sst.rearrange("p (h x) -> p h x", x=64),
                   bc(dec[:, c * 16:(c + 1) * 16].unsqueeze(2), [128, 16, 64]), ALU.mult, r=["Sst", "dec"], w=["Sst"], eng="pool")
                yield
                for g in range(2):
                    tt(Sst[:, g * 512:(g + 1) * 512], pb[PO[g]][:, :], Sst[:, g * 512:(g + 1) * 512], ALU.add,
                       r=[f"pb{6 + g}", "Sst"], w=["Sst"])
                    yield
                act(Sbf, Sst, r=["Sst"], w=["Sbf"])
                yield
                for g in range(2):
                    tt(tA[ab][:, g * 512:(g + 1) * 512], pb[PY[g]][:, :], tA[ab][:, g * 512:(g + 1) * 512], ALU.add,
                       r=[f"pb{4 + g}", f"tA{ab}_{g}"], w=[f"tA{ab}_{g}"])
                    yield
                for g in range(2):
                    mmg(pb[PO[g]][:, :], [(Uv[:, kk, t0:t0 + 128], wz[:, kk, g * 512:(g + 1) * 512]) for kk in range(8)],
                        r=[f"U{c // 4}", "wz"], w=[f"pb{6 + g}"])
                    yield
                    tbg = tB[:, g * 512:(g + 1) * 512]
                    act(tbg, pb[PO[g]][:, :], AF.Tanh, r=[f"pb{6 + g}"], w=[f"tB{g}"], scale=0.5)
                    yield
                    stt(tbg, tbg, 1.0, pb[PO[g]][:, :], ALU.add, ALU.mult, r=[f"pb{6 + g}", f"tB{g}"], w=[f"tB{g}"])
                    yield
                    stt(tA[ab][:, g * 512:(g + 1) * 512], tA[ab][:, g * 512:(g + 1) * 512], 0.5, tbg, ALU.mult, ALU.mult,
                        r=[f"tA{ab}_{g}", f"tB{g}"], w=[f"tA{ab}_{g}"])
                    yield
                    act(tbg, tA[ab][:, g * 512:(g + 1) * 512], AF.Square, r=[f"tA{ab}_{g}"], w=[f"tB{g}", f"ssq{g}"],
                        accum_out=small[:, 16 + g:17 + g])
                    yield
                ts(small[:, 18:20], small[:, 16:18], 1.0 / 512, RMS_EPS, op0=ALU.mult, op1=ALU.add, r=["ssq0", "ssq1"], w=["rstd"])
                yield
                S.op("pool", lambda e: e.tensor_tensor(out=small[:, 18:20], in0=small[:, 18:20], in1=small[:, 20:22], op=ALU.pow),
                     r=["rstd", "negh"], w=["rstd"])
                yield
                for g in range(2):
                    stt(yn[ab][:, g * 512:(g + 1) * 512], tA[ab][:, g * 512:(g + 1) * 512], small[:, 18 + g:19 + g],
                        vecs[:, V_SNWB + g * 512:V_SNWB + (g + 1) * 512], ALU.mult, ALU.mult,
                        r=[f"tA{ab}_{g}", "rstd", "vecs"], w=[f"yn{ab}"])
                    yield

            def stageC(c):
                t0 = c * 128
                ab = c % 2
                trg([(pbb[PT][:, kx * 128:(kx + 1) * 128], yn[ab][:, kx * 128:(kx + 1) * 128], identb[:]) for kx in range(8)],
                    r=[f"yn{ab}", "identb"], w=["pb0"])
                yield
                act(ymix[:, 8:16, t0:t0 + 128], pbb[PT][:, 0:1024].rearrange("p (k t) -> p k t", t=128),
                    r=["pb0"], w=[f"xbf{kx}_{c // 4}" for kx in range(8)] + ["ymix_ssm"])
                yield

            def interleave(gens_w):
                live = [[g, w] for g, w in gens_w]
                while live:
                    for ent in list(live):
                        g, w = ent
                        for _ in range(w):
                            try:
                                next(g)
                            except StopIteration:
                                live.remove(ent)
                                break

            for c in range(18):
                gd = {}
                if c < 16:
                    gd["A"] = (stageA(c), QA)
                if 1 <= c <= 16:
                    gd["B"] = (stageB(c - 1), QB)
                if c >= 2:
                    gd["C"] = (stageC(c - 2), QC)
                interleave([gd[k] for k in QORD if k in gd])
            dma("sp", nsp_d, Sst, "nsp", r=["Sst"])
            S.fence()

            if KSTOP < 5:
                return
            o = 0
            dtch = r2f(alloc_f(128), 128).rearrange("p (k j) -> p k j", j=NS)
            dAch = r2f(alloc_f(128), 128).rearrange("p (k j) -> p k j", j=NS)
            xdts = r2f(alloc_f(128), 128).rearrange("p (k j) -> p k j", j=NS)
            ysm = r2f(alloc_f(128), 128).rearrange("p (k j) -> p k j", j=NS)
            zs = r2f(alloc_f(128), 128).rearrange("p (k j) -> p k j", j=NS)
            gs = r2f(alloc_f(128), 128).rearrange("p (k j) -> p k j", j=NS)
            t1s = r2f(alloc_f(128), 128)
            t2s = r2f(alloc_f(128), 128)
            sqs = r2b(alloc_f(64), 128).rearrange("p (k j) -> p k j", j=NS)
            BCtok = r2f(alloc_f(512), 512)
            rhsj = [r2f(alloc_f(512), 512) for _ in range(2)]
            T1s = r2f(alloc_f(1024), 1024)
            Stb = [r2f(alloc_f(1024), 1024).rearrange("p (k n) -> p k n", n=128) for _ in range(4)]
            softplus(dtch.rearrange("p k j -> p (k j)"), dts[:],
                     bc(vecs[:, V_DTBCH:V_DTBCH + 8].unsqueeze(2), [128, 8, NS]), NS, ["dts", "vecs"], "dtch", t1s, t2s)
            tt(dAch, dtch, bc(anegch[:, :].unsqueeze(2), [128, 8, NS]), ALU.mult, r=["dtch", "anegch"], w=["dAch"])
            act(dAch, dAch, AF.Exp, r=["dAch"], w=["dAch"])
            tt(xdts, xcs[:, 0:8, :], dtch, ALU.mult, r=["xcs", "dtch"], w=["xdts"])
            bkT = nb()
            trg([(pb[bkT][0:16, m * 128:(m + 1) * 128], xcs[:, 8 + m, :], cst[:, C_ID:C_ID + 128]) for m in range(4)],
                r=["xcs", "cst"], w=[f"pb{bkT}"])
            act(BCtok[0:16, :], pb[bkT][0:16, :], r=[f"pb{bkT}"], w=["BCtok"])
            def ld_state(j):
                si = j % 4
                for h2 in range(2):
                    dma("sp", Stb[si][h2 * 64:(h2 + 1) * 64, :, :], sts_d[j, h2::2, :, :].rearrange("k p n -> p k n"),
                        f"St{si}_{h2}", w=[f"St{si}"])
            def decay_state(j):
                si = j % 4
                for k in range(8):
                    act(Stb[si][:, k, :], Stb[si][:, k, :], AF.Identity, r=[f"St{si}", "dAch"], w=[f"St{si}"], scale=dAch[:, k, j:j + 1])
            ld_state(0)
            ld_state(1)
            decay_state(0)
            for j in range(NS):
                sbuf_i = j % 4
                St = Stb[sbuf_i]
                if j + 2 < NS:
                    ld_state(j + 2)
                if j + 1 < NS:
                    decay_state(j + 1)
                rj = rhsj[j % 2]
                ts(rj[0:16, :], BCtok[0:16, :], cst[0:16, C_ID + j:C_ID + j + 1], r=["BCtok", "cst"], w=[f"rhsj{j % 2}"])
                bkj = nb()
                mmg(pb[bkj][:, :], [(cst[0:16, C_ONES:C_ONES + 128], rj[0:16, :])], r=[f"rhsj{j % 2}", "cst"], w=[f"pb{bkj}"])
                St2 = St.rearrange("p k n -> p (k n)")
                T1v = T1s.rearrange("p (g k n) -> p g k n", g=2, k=4)
                Bv = bc(pb[bkj][:, 0:256].rearrange("p (g n) -> p g n", n=128).unsqueeze(2), [128, 2, 4, 128])
                Cv = bc(pb[bkj][:, 256:512].rearrange("p (g n) -> p g n", n=128).unsqueeze(2), [128, 2, 4, 128])
                xv = bc(xdts[:, :, j].rearrange("p (g k) -> p g k", k=4).unsqueeze(3), [128, 2, 4, 128])
                tt(T1v, Bv, xv, ALU.mult, r=[f"pb{bkj}", "xdts"], w=["T1s"])
                tt(St2, St2, T1s, ALU.add, r=[f"St{sbuf_i}", "T1s"], w=[f"St{sbuf_i}"])
                tt(T1v, St.rearrange("p (g k) n -> p g k n", k=4), Cv, ALU.mult, r=[f"pb{bkj}", f"St{sbuf_i}"], w=["T1s"])
                S.op("dve", lambda e, j=j: e.tensor_reduce(out=ysm[:, :, j], in_=T1s.rearrange("p (k n) -> p k n", n=128),
                                                           axis=mybir.AxisListType.X, op=ALU.add), r=["T1s"], w=["ysm"])
                for h2 in range(2):
                    dma("act", nss_d[j, h2::2, :, :].rearrange("k p n -> p k n"), St[h2 * 64:(h2 + 1) * 64, :, :],
                        f"nss{sbuf_i}_{h2}", r=[f"St{sbuf_i}"])
                ada_chunks(16 + 2 * j, 2)
            ada_finish()
            tt(gs, xcs[:, 0:8, :], bc(vecs[:, V_DCH:V_DCH + 8].unsqueeze(2), [128, 8, NS]), ALU.mult, r=["xcs", "vecs"], w=["gs"])
            tt(gs, gs, ysm, ALU.add, r=["gs", "ysm"], w=["gs"])
            bkz = nb()
            for k in range(8):
                mmg(pb[bkz][:, k * 16:(k + 1) * 16], [(wz[:, kk, k * 128:(k + 1) * 128], Uv[:, kk, T:NTOK]) for kk in range(8)],
                    r=["wz", "U4"], w=[f"pb{bkz}"])
            zs2 = zs.rearrange("p k j -> p (k j)")
            act(zs2, pb[bkz][:, 0:128], r=[f"pb{bkz}"], w=["zs"])
            act(t1s, zs2, AF.Exp, r=["zs"], w=["t1s"], scale=-1.0)
            act(t1s, t1s, AF.Ln, r=["t1s"], w=["t1s"], bias=1.0)
            act(t1s, t1s, AF.Exp, r=["t1s"], w=["t1s"], scale=-1.0)
            tt(t1s, t1s, zs2, ALU.mult, r=["t1s", "zs"], w=["t1s"])
            gs2 = gs.rearrange("p k j -> p (k j)")
            tt(gs2, gs2, t1s, ALU.mult, r=["gs", "t1s"], w=["gs"])
            act(sqs.rearrange("p k j -> p (k j)"), gs2, AF.Square, r=["gs"], w=["sqs"])
            bkn = nb()
            for g in range(2):
                mmg(pb[bkn][:, g * 16:(g + 1) * 16], [(onesb[:], sqs[:, 4 * g + kk, :]) for kk in range(4)],
                    r=["onesb", "sqs"], w=[f"pb{bkn}"])
            act(t2s[:, 0:32], pb[bkn][:, 0:32], AF.Ln, r=[f"pb{bkn}"], w=["t2s"], scale=1.0 / 512, bias=RMS_EPS)
            act(t2s[:, 0:32], t2s[:, 0:32], AF.Exp, r=["t2s"], w=["t2s"], scale=-0.5)
            for g in range(2):
                tt(gs[:, 4 * g:4 * g + 4, :], gs[:, 4 * g:4 * g + 4, :], bc(t2s[:, g * 16:(g + 1) * 16].unsqueeze(1), [128, 4, NS]),
                   ALU.mult, r=["gs", "t2s"], w=["gs"])
            tt(ymix[:, 8:16, T:NTOK], gs, bc(vecs[:, V_SNWCH:V_SNWCH + 8].unsqueeze(2), [128, 8, NS]), ALU.mult,
               r=["gs", "vecs"], w=["ymix_ssm_s"])
            S.fence()

            if KSTOP < 6:
                return
            o = 0
            pful = [r2f(alloc_f(2052), 2050) for _ in range(2)]
            gcs = [r2f(alloc_f(512), 512) for _ in range(2)]
            cA = [r2f(alloc_f(512), 512) for _ in range(3)]
            cB = [r2f(alloc_f(512), 512) for _ in range(2)]
            gbcv = [r2f(alloc_f(512), 512) for _ in range(4)]
            sqb = [r2b(alloc_f(256), 512) for _ in range(4)]
            rs = [r2f(alloc_f(512), 512) for _ in range(2)]
            ps_s = r2f(alloc_f(16), 16)
            for i in range(2):
                S.op("dve", lambda e, i=i: e.memset(pful[i][:, 0:2], 0.0), w=[f"pf{i}z"])
            iters = []
            for kb in range(4):
                for cc in range(2):
                    for i, (t0, n) in enumerate(TILES):
                        iters.append((kb, cc, i, t0, n))
            slot_of = {}
            BG, BH, BB, BQ = (0, 1), (2, 3), (4, 5), (6, 7)

            def get_slots(kb):
                if kb not in slot_of:
                    base = B_CONV[kb - 1][2] if kb > 0 else B_CONV[kb][0]
                    slot_of[kb] = [wneed(b, base) for b in B_CONV[kb]]
                return slot_of[kb]

            def cS1(it):
                kb, cc, i, t0, n = iters[it]
                k = kb * 2 + cc
                slots = get_slots(kb)
                pbuf = k % 2
                pf = pful[pbuf]
                cw = lambda tap, k=k: vecs[:, V_CW + tap * 8 + k:V_CW + tap * 8 + k + 1]
                bg, bh = BG[it % 2], BH[it % 2]
                for (bank, gi) in ((bg, 0), (bh, 1)):
                    slot, key = slots[gi]
                    mmg(pb[bank][:, 0:n], [(slot[:, kk, cc * 128:(cc + 1) * 128], Uv[:, kk, t0:t0 + n]) for kk in range(8)],
                        r=[key, f"U{i}"], w=[f"pb{bank}"])
                tb = it % 2
                t3 = it % 3
                act(gcs[tb][:, 0:n], pb[bg][:, 0:n], r=[f"pb{bg}"], w=[f"gcs{tb}"])
                if i < 4:
                    tt(pf[:, 2 + t0:2 + t0 + n], pb[bh][:, 0:n], gcs[tb][:, 0:n], ALU.mult,
                       r=[f"pb{bh}", f"gcs{tb}", f"pf{pbuf}z"], w=[f"pf{pbuf}_{i}"])
                    rd = [f"pf{pbuf}_{i}"] + ([f"pf{pbuf}_{i - 1}"] if i > 0 else [f"pf{pbuf}z"])
                    act(cB[tb], pf[:, t0:t0 + n], AF.Identity, r=rd + ["vecs"], w=[f"cB{tb}"], scale=cw(0))
                    stt(cA[t3], pf[:, t0 + 1:t0 + 1 + n], cw(1), cB[tb], ALU.mult, ALU.add, r=rd + [f"cB{tb}"], w=[f"cA{t3}"])
                    stt(cA[t3], pf[:, t0 + 2:t0 + 2 + n], cw(2), cA[t3], ALU.mult, ALU.add, r=rd + [f"cA{t3}"], w=[f"cA{t3}"])
                    if i == 3:
                        S.op("pool", lambda e, k=k, pf=pf: e.tensor_copy(out=ncp_sb[:, k, :], in_=pf[:, T:T + 2]),
                             r=[f"pf{pbuf}_3"], w=["ncp_sb"])
                else:
                    tt(ps_s, pb[bh][:, 0:n], gcs[tb][:, 0:n], ALU.mult, r=[f"pb{bh}", f"gcs{tb}"], w=["ps_s"])
                    ts(cA[t3][:, 0:n], stc[:, k, :, 0], cw(0), r=["stc", "vecs"], w=[f"cA{t3}"])
                    stt(cA[t3][:, 0:n], stc[:, k, :, 1], cw(1), cA[t3][:, 0:n], ALU.mult, ALU.add, r=["stc", f"cA{t3}"], w=[f"cA{t3}"])
                    stt(cA[t3][:, 0:n], ps_s, cw(2), cA[t3][:, 0:n], ALU.mult, ALU.add, r=["ps_s", f"cA{t3}"], w=[f"cA{t3}"])
                    S.op("pool", lambda e, k=k: e.tensor_copy(out=ncs_sb[:, k, :, 0], in_=stc[:, k, :, 1]), r=["stc"], w=["ncs_a"])
                    S.op("pool", lambda e, k=k: e.tensor_copy(out=ncs_sb[:, k, :, 1], in_=ps_s), r=["ps_s"], w=["ncs_b"])

            def cS2(it):
                kb, cc, i, t0, n = iters[it]
                slot, key = get_slots(kb)[2]
                bb = BB[it % 2]
                t3 = it % 3
                t4 = it % 4
                mmg(pb[bb][:, 0:n], [(slot[:, kk, cc * 128:(cc + 1) * 128], Uv[:, kk, t0:t0 + n]) for kk in range(8)],
                    r=[key, f"U{i}"], w=[f"pb{bb}"])
                tt(gbcv[t4][:, 0:n], pb[bb][:, 0:n], cA[t3][:, 0:n], ALU.mult, r=[f"pb{bb}", f"cA{t3}"], w=[f"gbcv{t4}"])
                act(sqb[t4][:, 0:n], gbcv[t4][:, 0:n], AF.Square, r=[f"gbcv{t4}"], w=[f"sqb{t4}"])

            def cS3(it):
                kb, cc, i, t0, n = iters[it]
                k = kb * 2 + cc
                bq = BQ[it % 2]
                t3 = it % 4
                tb = it % 2
                mmg(pb[bq][:, 0:n], [(bonesb[:], sqb[t3][:, 0:n])], r=["bonesb", f"sqb{t3}"], w=[f"pb{bq}"])
                act(rs[tb][:, 0:n], pb[bq][:, 0:n], AF.Ln, r=[f"pb{bq}"], w=[f"rs{tb}"], scale=1.0 / 64, bias=RMS_EPS)
                act(rs[tb][:, 0:n], rs[tb][:, 0:n], AF.Exp, r=[f"rs{tb}"], w=[f"rs{tb}"], scale=-0.5)
                stt(ymix[:, k, t0:t0 + n], gbcv[t3][:, 0:n], vecs[:, V_CNW + k:V_CNW + k + 1], rs[tb][:, 0:n], ALU.mult, ALU.mult,
                    r=[f"gbcv{t3}", f"rs{tb}", "vecs"], w=[f"ymc{k}_{i}"])

            NI = len(iters)
            for s_ in range(NI + 3):
                if s_ < NI:
                    cS1(s_)
                if 0 <= s_ - 1 < NI:
                    cS2(s_ - 1)
                if 0 <= s_ - 3 < NI:
                    cS3(s_ - 3)
            dma("sp", ncp_d, ncp_sb[:].rearrange("p a b -> p (a b)"), "ncp", r=["ncp_sb"])
            dma("sp", ncs_d, ncs_sb[:].rearrange("p a b c -> p (a b c)"), "ncs", r=["ncs_a", "ncs_b"])
            S.fence()

            if KSTOP < 7:
                return
            X1 = R2[:, :].rearrange("p (k t) -> p k t", t=NTOK)
            xk = [Ub32[:, i * 2048:(i + 1) * 2048] for i in range(2)]
            o32 = 4096
            sqt1 = U[:, 2048:2048 + 4096].rearrange("p (k t) -> p k t", t=512)
            sqt2 = [sqt1, sqt1]
            _mean = Ub32[:, 3072:3584]
            _msq = Ub32[:, 3584:4096]
            st4 = [[_mean, _msq, Ub32[:, 4096 + j * 1024:4608 + j * 1024], Ub32[:, 4608 + j * 1024:5120 + j * 1024]] for j in range(2)]
            lt1 = [Ub32[:, 6144 + i * 512:6656 + i * 512] for i in range(2)]
            assert 7168 <= 4 * NTOK
            Vv = R1[:, 0:8 * NTOK].rearrange("p (k t) -> p k t", t=NTOK)
            HQ = R1[:, 8 * NTOK:16 * NTOK].rearrange("p (k t) -> p k t", t=NTOK)

            def layer_norm_all(outs, post, inline=False):
                def stats(i):
                    t0, n = TILES[i]
                    sb_ = i % 2
                    for kk in range(8):
                        act(sqt2[sb_][:, kk, 0:n], X1[:, kk, t0:t0 + n], AF.Square, r=[f"X1_{kk}_{i}"], w=[f"sqt_{kk}"])
                    b1, b2 = nb(), nb()
                    mmg(pb[b1][:, 0:n], [(cst[:, C_ONES:C_ONES + 128], X1[:, kk, t0:t0 + n]) for kk in range(8)],
                        r=["cst"] + [f"X1_{kk}_{i}" for kk in range(8)], w=[f"pb{b1}"])
                    mmg(pb[b2][:, 0:n], [(onesb[:], sqt2[sb_][:, kk, 0:n]) for kk in range(8)],
                        r=["onesb"] + [f"sqt_{kk}" for kk in range(8)], w=[f"pb{b2}"])
                    mean, msq, rstd, nmr = st4[sb_]
                    ts(mean[:, 0:n], pb[b1][:, 0:n], 1.0 / 1024, r=[f"pb{b1}"], w=["st_mean"])
                    tt(msq[:, 0:n], mean[:, 0:n], mean[:, 0:n], ALU.mult, r=["st_mean"], w=["st_msq"])
                    stt(msq[:, 0:n], pb[b2][:, 0:n], 1.0 / 1024, msq[:, 0:n], ALU.mult, ALU.subtract, r=[f"pb{b2}", "st_msq"], w=["st_msq"])
                    act(rstd[:, 0:n], msq[:, 0:n], AF.Ln, r=["st_msq"], w=[f"st_rstd{sb_}"], bias=LN_EPS)
                    act(rstd[:, 0:n], rstd[:, 0:n], AF.Exp, r=[f"st_rstd{sb_}"], w=[f"st_rstd{sb_}"], scale=-0.5)
                    stt(nmr[:, 0:n], mean[:, 0:n], -1.0, rstd[:, 0:n], ALU.mult, ALU.mult, r=["st_mean", f"st_rstd{sb_}"], w=[f"st_nmr{sb_}"])

                def norm(i):
                    t0, n = TILES[i]
                    sb_ = i % 2
                    mean, msq, rstd, nmr = st4[sb_]
                    for kk in range(8):
                        lb = kk % 2
                        tt(lt1[lb][:, 0:n], X1[:, kk, t0:t0 + n], rstd[:, 0:n], ALU.mult, r=[f"X1_{kk}_{i}", f"st_rstd{sb_}"], w=[f"lt1{lb}"])
                        tt(lt1[lb][:, 0:n], lt1[lb][:, 0:n], nmr[:, 0:n], ALU.add, r=[f"lt1{lb}", f"st_nmr{sb_}"], w=[f"lt1{lb}"])
                        outs(i, t0, n, kk, lt1[lb][:, 0:n], f"lt1{lb}")
                    post(i, t0, n)

                if inline:
                    return lambda i: (stats(i), norm(i))
                stats(0)
                for i in range(len(TILES)):
                    if i + 1 < len(TILES):
                        stats(i + 1)
                    norm(i)

            xTrk = xT_d.rearrange("(k p) t -> p k t", p=128)
            for cb in range(4):
                slots = [wneed(b, B_OUT[cb][0]) for b in B_OUT[cb]]
                for cc in range(2):
                    kd = cb * 2 + cc
                    xb_i = kd % 2
                    dma("sp", xk[xb_i], xTrk[:, kd, :], f"xk{xb_i}", w=[f"xk{xb_i}"])
                    for i, (t0, n) in enumerate(TILES):
                        bk = nb()
                        pairs = []
                        for hh in range(2):
                            slot, key = slots[hh]
                            pairs += [(slot[:, kk, cc * 128:(cc + 1) * 128], ymix[:, hh * 8 + kk, t0:t0 + n]) for kk in range(8)]
                        rk = [slots[0][1], slots[1][1]] + [f"ymc{kk}_{i}" for kk in range(8)] + (["ymix_ssm"] if i < 4 else ["ymix_ssm_s"])
                        mmg(pb[bk][:, 0:n], pairs, r=rk, w=[f"pb{bk}"])
                        if i < 4:
                            stt(X1[:, kd, t0:t0 + n], pb[bk][:, 0:n], mod[:, 16 + kd, 0:1], xk[xb_i][:, t0:t0 + n], ALU.mult, ALU.add,
                                r=[f"pb{bk}", "mod", f"xk{xb_i}"], w=[f"X1_{kd}_{i}"])
                        else:
                            tt(X1[:, kd, t0:t0 + n], pb[bk][:, 0:n], mod[:, 16 + kd, 1:17], ALU.mult, r=[f"pb{bk}", "mod"], w=[f"X1_{kd}_{i}"])
                            tt(X1[:, kd, t0:t0 + n], X1[:, kd, t0:t0 + n], xs[:, kd, :], ALU.add, r=[f"X1_{kd}_{i}", "xs"], w=[f"X1_{kd}_{i}"])
            S.fence()
            def outs1(i, t0, n, kk, xn, xkey):
                if i < 4:
                    act(X1[:, kk, t0:t0 + n], xn, AF.Identity, r=[xkey, "vecs"], w=[f"X1_{kk}_{i}"],
                        scale=vecs[:, V_L1G + kk:V_L1G + kk + 1], bias=vecs[:, V_L1B + kk:V_L1B + kk + 1])
                    if kk in (3, 7):
                        ts(Vv[:, kk, t0:t0 + n], xn, A2[:, kk:kk + 1], B2[:, kk:kk + 1], op0=ALU.mult, op1=ALU.add,
                           r=[xkey, "A2", "B2"], w=[f"V{i}"])
                    else:
                        act(Vv[:, kk, t0:t0 + n], xn, AF.Identity, r=[xkey, "A2", "B2"], w=[f"V{i}"],
                            scale=A2[:, kk:kk + 1], bias=B2[:, kk:kk + 1])
                else:
                    act(X1[:, kk, t0:t0 + n], xn, AF.Identity, r=[xkey, "vecs"], w=[f"X1_{kk}_{i}"],
                        scale=vecs[:, V_L1G + kk:V_L1G + kk + 1], bias=vecs[:, V_L1B + kk:V_L1B + kk + 1])
                    tt(xn, X1[:, kk, t0:t0 + n], mod[:, 32 + kk, 1:17], ALU.mult, r=[f"X1_{kk}_{i}", "mod"], w=[xkey])
                    tt(Vv[:, kk, t0:t0 + n], xn, mod[:, 24 + kk, 1:17], ALU.add, r=[xkey, "mod"], w=[f"V{i}"])
            layer_norm_all(outs1, lambda i, t0, n: None)

            if KSTOP < 8:
                return
            rl = [Ub32[:, i * 512:(i + 1) * 512] for i in range(2)]
            yo = [Ub32[:, 1024 + i * 512:1536 + i * 512] for i in range(2)]
            yTr = yT_d.rearrange("(k p) t -> p k t", p=128)
            ysTr = ysT_d.rearrange("(k p) t -> p k t", p=128)

            def outs2(i, t0, n, kk, xn, xkey):
                act(X1[:, kk, t0:t0 + n], xn, AF.Identity, r=[xkey, "vecs"], w=[f"X1_{kk}_{i}"],
                    scale=vecs[:, V_L2G + kk:V_L2G + kk + 1], bias=vecs[:, V_L2B + kk:V_L2B + kk + 1])

            def post2(i, t0, n):
                if i < 4:
                    dma("sp", yTr[:, :, t0:t0 + n], X1[:, :, t0:t0 + n], f"yout{i}", r=[f"X1_{kk}_{i}" for kk in range(8)])
                else:
                    dma("sp", ysTr, X1[:, :, t0:t0 + n], f"yout{i}", r=[f"X1_{kk}_{i}" for kk in range(8)])
            ln2_tile = layer_norm_all(outs2, post2, inline=True)
            it = 0
            for q in range(4):
                for bi in range(4):
                    slot, key = wneed(B_UP[q][bi])
                    for cc in range(2):
                        f = bi * 2 + cc
                        for i, (t0, n) in enumerate(TILES):
                            bk = nb()
                            mmg(pb[bk][:, 0:n], [(slot[:, kk, cc * 128:(cc + 1) * 128], Vv[:, kk, t0:t0 + n]) for kk in range(8)],
                                r=[key, f"V{i}"], w=[f"pb{bk}"])
                            tb = it % 2
                            it += 1
                            act(rl[tb][:, 0:n], pb[bk][:, 0:n], AF.Relu, r=[f"pb{bk}"], w=[f"rl{tb}"])
                            tt(HQ[:, f, t0:t0 + n], pb[bk][:, 0:n], rl[tb][:, 0:n], ALU.mult, r=[f"pb{bk}", f"rl{tb}"], w=[f"HQ{f}_{i}"])
                def down_tile(slot, key, cc, kd, i, t0, n):
                    bk = nb()
                    mmg(pb[bk][:, 0:n], [(slot[:, kk, cc * 128:(cc + 1) * 128], HQ[:, kk, t0:t0 + n]) for kk in range(8)],
                        r=[key] + [f"HQ{kk}_{i}" for kk in range(8)], w=[f"pb{bk}"])
                    if i < 4:
                        stt(X1[:, kd, t0:t0 + n], pb[bk][:, 0:n], mod[:, 40 + kd, 0:1], X1[:, kd, t0:t0 + n], ALU.mult, ALU.add,
                            r=[f"pb{bk}", "mod", f"X1_{kd}_{i}"], w=[f"X1_{kd}_{i}"])
                    else:
                        tt(rl[0][:, 0:n], pb[bk][:, 0:n], mod[:, 40 + kd, 1:17], ALU.mult, r=[f"pb{bk}", "mod"], w=["rl0"])
                        tt(X1[:, kd, t0:t0 + n], X1[:, kd, t0:t0 + n], rl[0][:, 0:n], ALU.add, r=[f"X1_{kd}_{i}", "rl0"], w=[f"X1_{kd}_{i}"])

                if q < 3:
                    for bi in range(4):
                        slot, key = wneed(B_DN[q][bi])
                        for cc in range(2):
                            for i, (t0, n) in enumerate(TILES):
                                down_tile(slot, key, cc, bi * 2 + cc, i, t0, n)
                else:
                    slots = [wneed(b_, B_DN[q][0]) for b_ in B_DN[q]]
                    for i, (t0, n) in enumerate(TILES):
                        for bi in range(4):
                            slot, key = slots[bi]
                            for cc in range(2):
                                down_tile(slot, key, cc, bi * 2 + cc, i, t0, n)
                        ln2_tile(i)

        o = 0
        phases()
        with nc.Block() as block:
            S.emit(block)
    return nc


def _fm(v, nchunk):
    return np.ascontiguousarray(v.reshape(nchunk, 128).T)


_CACHE = {}


def kernel(x_prompt, x_sample, state_conv, state_ssm_conv, state_ssm, c_prompt, c_sample,
           w_ada, b_ada, w_in, conv_w, conv_norm_w, ssm_conv_w, ssm_conv_b, dt_bias, a_log, d_skip,
           ssm_norm_w, w_out, ln1_g, ln1_b, w_up, w_down, ln2_g, ln2_b):
    f = lambda a: np.ascontiguousarray(np.asarray(a, dtype=np.float32))
    x_prompt, x_sample, state_conv, state_ssm_conv, state_ssm = map(f, (x_prompt, x_sample, state_conv, state_ssm_conv, state_ssm))
    c_prompt, c_sample = f(c_prompt), f(c_sample)
    w_ada, w_in, w_out, w_up, w_down = f(w_ada)[0], f(w_in)[0], f(w_out)[0], f(w_up)[0], f(w_down)[0]
    b_ada, conv_w, conv_norm_w, ssm_conv_w, ssm_conv_b = f(b_ada)[0], f(conv_w)[0], f(conv_norm_w)[0], f(ssm_conv_w)[0], f(ssm_conv_b)[0]
    dt_bias, a_log, d_skip, ssm_norm_w = f(dt_bias)[0], f(a_log)[0], f(d_skip)[0], f(ssm_norm_w)[0]
    ln1_g, ln1_b, ln2_g, ln2_b = f(ln1_g)[0], f(ln1_b)[0], f(ln2_g)[0], f(ln2_b)[0]

    vecs = np.zeros((128, NV), np.float32)
    vecs[:, V_BADA:V_BADA + 48] = _fm(b_ada, 48)
    for tap in range(3):
        vecs[:, V_CW + tap * 8:V_CW + tap * 8 + 8] = _fm(conv_w[tap], 8)
    vecs[:, V_CNW:V_CNW + 8] = _fm(conv_norm_w, 8)
    for tap in range(4):
        vecs[:, V_SCW + tap * 12:V_SCW + tap * 12 + 12] = _fm(ssm_conv_w[tap], 12)
    vecs[:, V_SCB:V_SCB + 12] = _fm(ssm_conv_b, 12)
    vecs[:, V_L1G:V_L1G + 8] = _fm(ln1_g, 8)
    vecs[:, V_L1B:V_L1B + 8] = _fm(ln1_b, 8)
    vecs[:, V_L2G:V_L2G + 8] = _fm(ln2_g, 8)
    vecs[:, V_L2B:V_L2B + 8] = _fm(ln2_b, 8)
    vecs[:, V_DCH:V_DCH + 8] = _fm(np.repeat(d_skip, 64), 8)
    vecs[:, V_ALCH:V_ALCH + 8] = _fm(np.repeat(a_log, 64), 8)
    vecs[:, V_DTBCH:V_DTBCH + 8] = _fm(np.repeat(dt_bias, 64), 8)
    vecs[:, V_SNWCH:V_SNWCH + 8] = _fm(ssm_norm_w, 8)
    vecs[:, V_ALB:V_ALB + 16] = a_log[None, :]
    vecs[:, V_DTB:V_DTB + 16] = dt_bias[None, :]
    vecs[:, V_DSB:V_DSB + 16] = d_skip[None, :]
    vecs[:, V_SNWB:V_SNWB + 1024] = ssm_norm_w[None, :]
    consts = np.zeros((128, NC_), np.float32)
    idx = np.arange(128)
    consts[:, C_ID:C_ID + 128] = np.eye(128, dtype=np.float32)
    consts[:, C_MLE:C_MLE + 128] = (idx[:, None] <= idx[None, :])
    consts[:, C_MGT:C_MGT + 128] = (idx[:, None] > idx[None, :])
    consts[:, C_BONES:C_BONES + 128] = ((idx[:, None] // 64) == (idx[None, :] // 64))
    consts[:, C_ONES:C_ONES + 128] = 1.0
    wdtx = np.ascontiguousarray(np.repeat(w_in[:, 5632:5648], 64, axis=1))

    in_maps = []
    for b in range(8):
        js = slice(16 * b, 16 * b + 16)
        stc = state_conv[0, js]
        stsc = state_ssm_conv[0, js]
        in_maps.append({
            "xT": np.ascontiguousarray(x_prompt[b].T),
            "xsT": np.ascontiguousarray(x_sample[js, 0, :].T),
            "cT": np.ascontiguousarray(np.concatenate([c_prompt[b:b + 1], c_sample[js]], axis=0).T),
            "stc": np.ascontiguousarray(stc.reshape(16, 2, 8, 128).transpose(3, 2, 0, 1).reshape(128, -1)),
            "stsc": np.ascontiguousarray(stsc.reshape(16, 3, 12, 128).transpose(3, 2, 0, 1).reshape(128, -1)),
            "sts": np.ascontiguousarray(state_ssm[0, js]),
            "w_ada": w_ada, "w_in": w_in, "wdtx": wdtx, "w_out": w_out, "w_up": w_up, "w_down": w_down,
            "vecs": vecs, "consts": consts,
        })
    if "nc" not in _CACHE:
        _CACHE["nc"] = build_program()
    res = run_bass_kernel_spmd(_CACHE["nc"], in_maps, core_ids=list(range(8)))
    R = res.results
    y_prompt = np.stack([R[b]["yT"].T for b in range(8)]).astype(np.float32)
    y_sample = np.concatenate([R[b]["ysT"].T for b in range(8)], axis=0)[:, None, :].astype(np.float32)
    ncp = np.stack([R[b]["ncp"].reshape(128, 8, 2).transpose(2, 1, 0).reshape(2, 1024) for b in range(8)])[None]
    nscp = np.stack([R[b]["nscp"].reshape(128, 12, 3).transpose(2, 1, 0).reshape(3, 1536) for b in range(8)])[None]
    nsp = np.stack([R[b]["nsp"].reshape(128, 16, 64).transpose(1, 2, 0) for b in range(8)])[None]
    ncs = np.concatenate([R[b]["ncs"].reshape(128, 8, 16, 2).transpose(2, 3, 1, 0).reshape(16, 2, 1024) for b in range(8)], axis=0)[None]
    nscs = np.concatenate([R[b]["nscs"].reshape(128, 12, 16, 3).transpose(2, 3, 1, 0).reshape(16, 3, 1536) for b in range(8)], axis=0)[None]
    nss = np.concatenate([R[b]["nss"] for b in range(8)], axis=0)[None]
    c = lambda a: np.ascontiguousarray(a, dtype=np.float32)
    return (c(y_prompt), c(y_sample), c(ncp), c(nscp), c(nsp), c(ncs), c(nscs), c(nss))
```

```python
import os
import numpy as np
from contextlib import ExitStack
import concourse.bass as bass
import concourse.mybir as mybir
from concourse.bass_utils import run_bass_kernel_spmd

F32 = mybir.dt.float32
BF16 = mybir.dt.bfloat16
AF = mybir.ActivationFunctionType
ALU = mybir.AluOpType

ENGS = ("pe", "act", "dve", "pool", "sp")

T = 2048
NS = 16
NTOK = T + NS
ALPHA = 2.0 ** 0.25
LN_EPS = 1e-5 / (ALPHA * ALPHA)
RMS_EPS = 1e-5
NSLOT = 6
TILES = [(0, 512), (512, 512), (1024, 512), (1536, 512), (2048, 16)]

V_BADA, V_CW, V_CNW, V_SCW, V_SCB = 0, 48, 72, 80, 128
V_L1G, V_L1B, V_L2G, V_L2B = 140, 148, 156, 164
V_DCH, V_ALCH, V_DTBCH, V_SNWCH = 172, 180, 188, 196
V_ALB, V_DTB, V_DSB, V_SNWB = 204, 220, 236, 252
NV = 252 + 1024
C_ID, C_MLE, C_MGT, C_BONES, C_ONES = 0, 128, 256, 384, 512
NC_ = 640


KSTOP = int(os.environ.get('KSTOP', '99'))
KSUB = int(os.environ.get('KSUB', '99'))
QA = int(os.environ.get('QA', '8'))
QB = int(os.environ.get('QB', '12'))
QC = int(os.environ.get('QC', '2'))
QORD = os.environ.get('QORD', 'BAC')
KI = int(os.environ.get('KI', '99'))


class _Stop(Exception):
    pass


class Sched:
    def __init__(self, nc, es):
        self.nc = nc
        self.es = es
        self.q = {e: [] for e in ENGS}
        self.sems = {}
        self.cnt = {}
        self.waited = {e: {} for e in ENGS}
        self.lastw = {}
        self.readers = {}
        for e in ENGS:
            self._sem("E_" + e)

    def _sem(self, name):
        if name not in self.sems:
            self.sems[name] = self.es.enter_context(self.nc.semaphore(name))
            self.cnt[name] = 0
        return self.sems[name]

    def op(self, eng, fn, r=(), w=(), dma=None):
        deps = {}

        def add(d):
            if d is None:
                return
            s, v, e2 = d
            if e2 == "pe" and eng == "pe" and dma is None:
                return
            if deps.get(s, 0) < v:
                deps[s] = v

        w = list(w) + [k for k in r if k.startswith("pb") and k not in w]
        for k in r:
            add(self.lastw.get(k))
        for k in w:
            add(self.lastw.get(k))
            for d in self.readers.get(k, ()):
                add(d)
        waits = []
        for s, v in deps.items():
            if self.waited[eng].get(s, 0) < v:
                self.waited[eng][s] = v
                waits.append((s, v))
        if dma is not None:
            sname = "D_" + dma
            self._sem(sname)
            self.cnt[sname] += 16
            me = (sname, self.cnt[sname], "dma")
            inc = 16
        else:
            sname = "E_" + eng
            self.cnt[sname] += 1
            me = (sname, self.cnt[sname], eng)
            inc = 1
        self.q[eng].append((waits, fn, sname, inc))
        for k in w:
            self.lastw[k] = me
            self.readers[k] = []
        for k in r:
            self.readers.setdefault(k, []).append(me)
        return me

    def fence(self):
        snap = dict(self.cnt)
        for e in ENGS:
            waits = []
            for s, v in snap.items():
                if v > 0 and self.waited[e].get(s, 0) < v and s != "E_" + e:
                    self.waited[e][s] = v
                    waits.append((s, v))
            if waits:
                self.q[e].append((waits, None, None, 0))

    def emit(self, block):
        sems = self.sems
        fin = [(s, v) for s, v in self.cnt.items() if v > 0]

        def make(ename):
            def body(eng):
                for waits, fn, sname, inc in self.q[ename]:
                    for s, v in waits:
                        eng.wait_ge(sems[s], v)
                    if fn is None:
                        continue
                    inst = fn(eng)
                    inst.then_inc(sems[sname], inc)
                if ename == "sp":
                    for s, v in fin:
                        eng.wait_ge(sems[s], v)
            return body

        block.tensor(make("pe"))
        block.scalar(make("act"))
        block.vector(make("dve"))
        block.gpsimd(make("pool"))
        block.sync(make("sp"))


def build_program():
    nc = bass.Bass("TRN2", target_bir_lowering=False)
    din = lambda n, sh: nc.dram_tensor(n, sh, F32, kind="ExternalInput").ap()
    dout = lambda n, sh: nc.dram_tensor(n, sh, F32, kind="ExternalOutput").ap()
    xT_d = din("xT", [1024, T])
    xsT_d = din("xsT", [1024, NS])
    cT_d = din("cT", [1024, 17])
    stc_d = din("stc", [128, 8 * NS * 2])
    stsc_d = din("stsc", [128, 12 * NS * 3])
    sts_d = din("sts", [NS, 16, 64, 128])
    w_ada_d = din("w_ada", [1024, 6144])
    w_in_d = din("w_in", [1024, 5648])
    wdtx_d = din("wdtx", [1024, 1024])
    w_out_d = din("w_out", [2048, 1024])
    w_up_d = din("w_up", [1024, 4096])
    w_down_d = din("w_down", [4096, 1024])
    vecs_d = din("vecs", [128, NV])
    cst_d = din("consts", [128, NC_])
    yT_d = dout("yT", [1024, T])
    ysT_d = dout("ysT", [1024, NS])
    ncp_d = dout("ncp", [128, 16])
    nscp_d = dout("nscp", [128, 36])
    nsp_d = dout("nsp", [128, 1024])
    ncs_d = dout("ncs", [128, 8 * NS * 2])
    nscs_d = dout("nscs", [128, 12 * NS * 3])
    nss_d = dout("nss", [NS, 16, 64, 128])

    with ExitStack() as es:
        S = Sched(nc, es)
        sb = lambda n, sh, dt=F32: es.enter_context(nc.sbuf_tensor("s_" + n, sh, dt))
        R1 = sb("R1", [128, 16 * NTOK], BF16)
        R2 = sb("R2", [128, 8 * NTOK], F32)
        U = sb("U", [128, 8 * NTOK], BF16)
        ring = [sb(f"wr{i}", [128, 8, 256], BF16) for i in range(NSLOT)]
        vecs = sb("vecs", [128, NV])
        cst = sb("cst", [128, NC_])
        mod = sb("mod", [128, 48, 17])
        cTf = sb("cTf", [128, 8, 17])
        cTb = sb("cTb", [128, 8, 17], BF16)
        xs = sb("xs", [128, 8, NS])
        identb = sb("identb", [128, 128], BF16)
        bonesb = sb("bonesb", [128, 128], BF16)
        onesb = sb("onesb", [128, 128], BF16)
        aneg = sb("aneg", [128, 16])
        anegch = sb("anegch", [128, 8])
        wdt = sb("wdt", [128, 8, 16], BF16)
        A2 = sb("A2", [128, 8])
        B2 = sb("B2", [128, 8])
        ncp_sb = sb("ncp_sb", [128, 8, 2])
        nscp_sb = sb("nscp_sb", [128, 12, 3])
        stc = sb("stc", [128, 8, NS, 2])
        stsc = sb("stsc", [128, 12, NS, 3])
        ncs_sb = sb("ncs_sb", [128, 8, NS, 2])
        nscs_sb = sb("nscs_sb", [128, 12, NS, 3])
        xcs = sb("xcs", [128, 12, NS])
        small = sb("small", [128, 64])
        pb = [es.enter_context(nc.psum_tensor(f"pb{i}", [128, 512], F32)) for i in range(8)]
        pbb = [p.bitcast(BF16) for p in pb]

        R1b = R1
        ymix = R1[:, :].rearrange("p (k t) -> p k t", t=NTOK)
        Uv = U[:, :].rearrange("p (k t) -> p k t", t=NTOK)
        R2b = R2.bitcast(BF16)
        Ub32 = U.bitcast(F32)

        def r2f(off, n):
            return R2[:, off:off + n]

        def r2b(off_f32, n):
            return R2b[:, 2 * off_f32:2 * off_f32 + n]

        def act(out, in_, func=AF.Copy, r=(), w=(), **kw):
            S.op("act", lambda e: e.activation(out=out, in_=in_, func=func, **kw), r=r, w=w)

        def tt(out, in0, in1, op, r=(), w=(), eng="dve"):
            S.op(eng, lambda e: e.tensor_tensor(out=out, in0=in0, in1=in1, op=op), r=r, w=w)

        def ts(out, in0, s1, s2=None, op0=ALU.mult, op1=None, r=(), w=(), eng="dve"):
            if op1 is None:
                S.op(eng, lambda e: e.tensor_scalar(out=out, in0=in0, scalar1=s1, scalar2=None, op0=op0), r=r, w=w)
            else:
                S.op(eng, lambda e: e.tensor_scalar(out=out, in0=in0, scalar1=s1, scalar2=s2, op0=op0, op1=op1), r=r, w=w)

        def stt(out, in0, scalar, in1, op0, op1, r=(), w=(), accum=None):
            if accum is None:
                S.op("dve", lambda e: e.scalar_tensor_tensor(out=out, in0=in0, scalar=scalar, in1=in1, op0=op0, op1=op1), r=r, w=w)
            else:
                S.op("dve", lambda e: e.scalar_tensor_tensor(out=out, in0=in0, scalar=scalar, in1=in1, op0=op0, op1=op1, accum_out=accum), r=r, w=w)

        def mmg(out, pairs, r=(), w=()):
            def f(e):
                n = len(pairs)
                last = None
                for i, (l, rr) in enumerate(pairs):
                    last = e.matmul(out, lhsT=l, rhs=rr, start=(i == 0), stop=(i == n - 1))
                return last
            S.op("pe", f, r=r, w=w)

        def trg(items, r=(), w=()):
            def f(e):
                last = None
                for (o, i_, idn) in items:
                    last = e.transpose(out=o, in_=i_, identity=idn)
                return last
            S.op("pe", f, r=r, w=w)

        def dma(eng, out, in_, key, r=(), w=()):
            S.op(eng, lambda e: e.dma_start(out=out, in_=in_), r=r, w=w, dma=key)

        def bc(ap, shape):
            return ap.broadcast_to(shape)

        blocks = []

        def addblk(wd, r0, c0, ncols=256):
            blocks.append((wd[r0:r0 + 1024, c0:c0 + ncols].rearrange("(k p) c -> p k c", p=128), ncols))
            return len(blocks) - 1

        wstate = {"next": 0}

        def wneed(i, base=None):
            lim = min(len(blocks), (i if base is None else base) + NSLOT)
            while wstate["next"] < lim:
                j = wstate["next"]
                src, ncols = blocks[j]
                sl = j % NSLOT
                dma("pool", ring[sl][:, :, 0:ncols], src, f"wr{sl}", w=[f"wr{sl}"])
                wstate["next"] += 1
            assert wstate["next"] > i, ("weight block not loaded", i, base)
            return ring[i % NSLOT], f"wr{i % NSLOT}"

        B_XBC = [addblk(w_in_d, 0, 4096 + c * 256) for c in range(6)]
        B_DTX = [addblk(wdtx_d, 0, c * 256) for c in range(4)]
        B_ADA = [None] * 8 + [addblk(w_ada_d, 0, c * 256) for c in range(8, 24)]
        B_CONV = []
        for kb in range(4):
            B_CONV.append([addblk(w_in_d, 0, g * 1024 + kb * 256) for g in (1, 2, 0)])
        B_OUT = []
        for cb in range(4):
            B_OUT.append([addblk(w_out_d, h * 1024, cb * 256) for h in range(2)])
        B_UP, B_DN = [], []
        for q in range(4):
            B_UP.append([addblk(w_up_d, 0, q * 1024 + i * 256) for i in range(4)])
            B_DN.append([addblk(w_down_d, q * 1024, i * 256) for i in range(4)])

        bank_rr = {"i": 0}

        def nb():
            i = bank_rr["i"] % 8
            bank_rr["i"] += 1
            return i

        def phases():
            nonlocal o
            if KSTOP < 0:
                return
            dma("sp", vecs[:], vecs_d, "vecs", w=["vecs"])
            dma("sp", cst[:], cst_d, "cst", w=["cst"])
            dma("sp", cTf[:], cT_d.rearrange("(k p) c -> p k c", p=128), "cT", w=["cTf"])
            dma("sp", xs[:], xsT_d.rearrange("(k p) c -> p k c", p=128), "xs", w=["xs"])
            dma("sp", stc[:].rearrange("p a b c -> p (a b c)"), stc_d, "stc", w=["stc"])
            dma("sp", stsc[:].rearrange("p a b c -> p (a b c)"), stsc_d, "stsc", w=["stsc"])
            dma("pool", wdt[:], w_in_d[:, 5632:5648].rearrange("(k p) c -> p k c", p=128), "wdt", w=["wdt"])
            act(cTb[:], cTf[:], r=["cTf"], w=["cTb"])
            act(identb[:], cst[:, C_ID:C_ID + 128], r=["cst"], w=["identb"])
            act(bonesb[:], cst[:, C_BONES:C_BONES + 128], r=["cst"], w=["bonesb"])
            act(onesb[:], cst[:, C_ONES:C_ONES + 128], r=["cst"], w=["onesb"])
            act(aneg[:], vecs[:, V_ALB:V_ALB + 16], AF.Exp, r=["vecs"], w=["aneg"])
            ts(aneg[:], aneg[:], -1.0, r=["aneg"], w=["aneg"])
            act(anegch[:], vecs[:, V_ALCH:V_ALCH + 8], AF.Exp, r=["vecs"], w=["anegch"])
            ts(anegch[:], anegch[:], -1.0, r=["anegch"], w=["anegch"])

            if KSTOP < 1:
                return
            def ada_chunks(c0, ncks):
                bk = nb()
                for cl in range(ncks):
                    c = c0 + cl
                    slot, key = wneed(B_ADA[c // 2])
                    cc = c % 2
                    mmg(pb[bk][:, cl * 32:cl * 32 + 17],
                        [(slot[:, kk, cc * 128:(cc + 1) * 128], cTb[:, kk, :]) for kk in range(8)],
                        r=[key, "cTb"], w=[f"pb{bk}"])
                tt(mod[:, c0:c0 + ncks, :],
                   pb[bk][:, 0:ncks * 32].rearrange("p (c x) -> p c x", x=32)[:, :, 0:17],
                   bc(vecs[:, V_BADA + c0:V_BADA + c0 + ncks].unsqueeze(2), [128, ncks, 17]),
                   ALU.add, r=[f"pb{bk}", "vecs"], w=["mod" if c0 < 16 else "mod2"])
            wa32 = R1.bitcast(F32)[:, 0:16384].rearrange("p (k c) -> p k c", c=2048)
            for q4 in range(4):
                dma("sp", wa32[:, :, q4 * 512:(q4 + 1) * 512], w_ada_d[:, q4 * 512:(q4 + 1) * 512].rearrange("(k p) c -> p k c", p=128),
                    f"wa{q4}", w=[f"wa{q4}"])
            bk = nb()
            for c in range(16):
                mmg(pb[bk][:, c * 32:c * 32 + 17], [(wa32[:, kk, c * 128:(c + 1) * 128], cTf[:, kk, :]) for kk in range(8)],
                    r=[f"wa{c // 4}", "cTf"], w=[f"pb{bk}"])
            tt(mod[:, 0:16, :], pb[bk][:, :].rearrange("p (c x) -> p c x", x=32)[:, :, 0:17],
               bc(vecs[:, V_BADA:V_BADA + 16].unsqueeze(2), [128, 16, 17]), ALU.add, r=[f"pb{bk}", "vecs"], w=["mod"])
            ts(mod[:, 8:16, :], mod[:, 8:16, :], 1.0, op0=ALU.add, r=["mod"], w=["mod"])

            def ada_finish():
                ts(mod[:, 32:40, :], mod[:, 32:40, :], 1.0, op0=ALU.add, r=["mod2"], w=["mod2"])
                ts(mod[:, 16:24, :], mod[:, 16:24, :], 1.0, 1.0 / ALPHA, op0=ALU.add, op1=ALU.mult, r=["mod2"], w=["mod2"])
                ts(mod[:, 40:48, :], mod[:, 40:48, :], 1.0, 1.0 / ALPHA, op0=ALU.add, op1=ALU.mult, r=["mod2"], w=["mod2"])
                tt(A2[:], vecs[:, V_L1G:V_L1G + 8], mod[:, 32:40, 0], ALU.mult, r=["vecs", "mod2"], w=["A2"])
                tt(B2[:], vecs[:, V_L1B:V_L1B + 8], mod[:, 32:40, 0], ALU.mult, r=["vecs", "mod2"], w=["B2"])
                tt(B2[:], B2[:], mod[:, 24:32, 0], ALU.add, r=["B2", "mod2"], w=["B2"])

            if KSTOP < 2:
                return
            xTr = xT_d.rearrange("(k p) t -> p k t", p=128)
            xin = [r2f(i * 4096, 4096).rearrange("p (k t) -> p k t", t=512) for i in range(2)]
            for i in range(4):
                t0 = i * 512
                b = i % 2
                dma("sp", xin[b], xTr[:, :, t0:t0 + 512], f"xin{b}", w=[f"xin{b}"])
                for kk in range(8):
                    if kk % 2 == 0:
                        act(Uv[:, kk, t0:t0 + 512], xin[b][:, kk, :], AF.Identity, r=[f"xin{b}", "mod"], w=[f"U{i}"],
                            scale=mod[:, 8 + kk, 0:1], bias=mod[:, kk, 0:1])
                    else:
                        ts(Uv[:, kk, t0:t0 + 512], xin[b][:, kk, :], mod[:, 8 + kk, 0:1], mod[:, kk, 0:1], op0=ALU.mult, op1=ALU.add,
                           r=[f"xin{b}", "mod"], w=[f"U{i}"])
            us_tmp = small[:, 0:0]
            ustmp = xcs[:, 0:8, :]
            tt(ustmp, xs[:], mod[:, 8:16, 1:17], ALU.mult, r=["xs", "mod"], w=["ustmp"])
            tt(Uv[:, :, T:NTOK], ustmp, mod[:, 0:8, 1:17], ALU.add, r=["ustmp", "mod"], w=["U4"])
            S.fence()

            if KSTOP < 3:
                return
            BCT = r2b(0, 4 * T).rearrange("p (k t) -> p k t", t=T)
            pre = [r2f(4096 + i * 2064, 2051) for i in range(2)]
            ctmp = [[r2f(8224 + (i * 2 + j) * 512, 512) for j in range(2)] for i in range(4)]
            for i in range(2):
                S.op("dve", lambda e, i=i: e.memset(pre[i][:, 0:3], 0.0), w=[f"pre{i}z"])
            it = 0
            pend = []
            for blk in range(6):
                for cc in range(2):
                    kx = blk * 2 + cc
                    slot, key = wneed(B_XBC[blk])
                    pbuf = kx % 2
                    prb = pre[pbuf]
                    wcol = lambda tap, kx=kx: vecs[:, V_SCW + tap * 12 + kx:V_SCW + tap * 12 + kx + 1]
                    for i, (t0, n) in enumerate(TILES):
                        bk = nb()
                        mmg(pb[bk][:, 0:n], [(slot[:, kk, cc * 128:(cc + 1) * 128], Uv[:, kk, t0:t0 + n]) for kk in range(8)],
                            r=[key, f"U{i}"], w=[f"pb{bk}"])
                        if i < 4:
                            tb = it % 4
                            it += 1
                            c0, c1 = ctmp[tb]
                            act(prb[:, 3 + t0:3 + t0 + n], pb[bk][:, 0:n], r=[f"pb{bk}", f"pre{pbuf}z"], w=[f"pre{pbuf}_{i}"])
                            rd = [f"pre{pbuf}_{i}"] + ([f"pre{pbuf}_{i - 1}"] if i > 0 else [f"pre{pbuf}z"])
                            act(c0, prb[:, t0:t0 + n], AF.Identity, r=rd + ["vecs"], w=[f"c0_{tb}"], scale=wcol(0))
                            if pend:
                                pend.pop()()
                            stt(c1, prb[:, t0 + 1:t0 + 1 + n], wcol(1), c0, ALU.mult, ALU.add, r=rd + [f"c0_{tb}"], w=[f"c1_{tb}"])
                            stt(c0, prb[:, t0 + 2:t0 + 2 + n], wcol(2), c1, ALU.mult, ALU.add, r=rd + [f"c1_{tb}"], w=[f"c0_{tb}"])
                            stt(c1, pb[bk][:, 0:n], wcol(3), c0, ALU.mult, ALU.add, r=[f"pb{bk}", f"c0_{tb}"], w=[f"c1_{tb}"])
                            if kx < 8:
                                dst = ymix[:, 8 + kx, t0:t0 + n]
                                wk = [f"xbf{kx}_{i}"]
                            else:
                                dst = BCT[:, kx - 8, t0:t0 + n]
                                wk = [f"BCT{i}"]
                            pend.append(lambda dst=dst, c1=c1, tb=tb, wk=wk, kx=kx: act(dst, c1, AF.Silu, r=[f"c1_{tb}", "vecs"], w=wk,
                                                                                         bias=vecs[:, V_SCB + kx:V_SCB + kx + 1]))
                        else:
                            if pend:
                                pend.pop()()
                            act(nscs_sb[:, kx, :, 2], pb[bk][:, 0:n], r=[f"pb{bk}"], w=["xbcs"])
                            cs = small[:, 0:16]
                            ts(cs, stsc[:, kx, :, 0], wcol(0), r=["stsc", "vecs"], w=["cs"])
                            stt(cs, stsc[:, kx, :, 1], wcol(1), cs, ALU.mult, ALU.add, r=["cs", "stsc"], w=["cs"])
                            stt(cs, stsc[:, kx, :, 2], wcol(2), cs, ALU.mult, ALU.add, r=["cs", "stsc"], w=["cs"])
                            stt(cs, nscs_sb[:, kx, :, 2], wcol(3), cs, ALU.mult, ALU.add, r=["cs", "xbcs"], w=["cs"])
                            act(xcs[:, kx, :], cs, AF.Silu, r=["cs", "vecs"], w=["xcs"], bias=vecs[:, V_SCB + kx:V_SCB + kx + 1])
                            S.op("pool", lambda e, kx=kx: e.tensor_copy(out=nscs_sb[:, kx, :, 0:2], in_=stsc[:, kx, :, 1:3]), r=["stsc"], w=["nscs_a"])
                    S.op("pool", lambda e, kx=kx, prb=prb: e.tensor_copy(out=nscp_sb[:, kx, :], in_=prb[:, T:T + 3]),
                         r=[f"pre{pbuf}_3"], w=["nscp_sb"])
            if pend:
                pend.pop()()
            dma("sp", nscp_d, nscp_sb[:].rearrange("p a b -> p (a b)"), "nscp", r=["nscp_sb"])
            dma("sp", nscs_d, nscs_sb[:].rearrange("p a b c -> p (a b c)"), "nscs", r=["nscs_a", "xbcs"])

            dts = sb("dts", [128, 8, NS])
            bk = nb()
            for k in range(8):
                slot, key = wneed(B_DTX[k // 2])
                cc = k % 2
                mmg(pb[bk][:, k * 16:(k + 1) * 16], [(slot[:, kk, cc * 128:(cc + 1) * 128], Uv[:, kk, T:NTOK]) for kk in range(8)],
                    r=[key, "U4"], w=[f"pb{bk}"])
            act(dts[:].rearrange("p a b -> p (a b)"), pb[bk][:, 0:128], r=[f"pb{bk}"], w=["dts"])
            S.fence()

            if KSTOP < 4:
                return
            wz = R1[:, 0:8192].rearrange("p (k c) -> p k c", c=1024)
            for i in range(4):
                dma("pool", wz[:, :, i * 256:(i + 1) * 256], w_in_d[:, 3072 + i * 256:3072 + (i + 1) * 256].rearrange("(k p) c -> p k c", p=128),
                    f"wz{i}", w=["wz"])
            if KSUB < -3:
                return
            o = 4096
            def alloc_f(n):
                nonlocal o
                a = o
                o += n
                return a
            Sst = r2f(alloc_f(1024), 1024)
            Sbf = r2b(alloc_f(512), 1024)
            dtall = r2f(alloc_f(256), 256)
            dta = r2f(alloc_f(256), 256)
            acs = r2f(alloc_f(256), 256)
            Ecol = r2f(alloc_f(256), 256)
            dec = r2f(alloc_f(256), 256)
            wst = r2f(alloc_f(256), 256)
            r1free = 8192
            def r1b(n):
                nonlocal r1free
                a = r1free
                r1free += n
                return R1[:, a:a + n]
            xTb = [r1b(1024) for _ in range(2)]
            xdt = [r1b(1024) for _ in range(2)]
            xdtw = [r1b(1024) for _ in range(2)]
            Mh = [r1b(2048).rearrange("p (h s) -> p h s", s=128), r2b(alloc_f(1024), 2048).rearrange("p (h s) -> p h s", s=128)]
            assert r1free <= 8 * NTOK
            Btok = [r2b(alloc_f(128), 256) for _ in range(2)]
            Ah4 = [r2f(alloc_f(512), 512) for _ in range(2)]
            Lh4 = [r2f(alloc_f(512), 512) for _ in range(2)]
            CBm = [r2f(alloc_f(256), 256).rearrange("p (g s) -> p g s", s=128) for _ in range(2)]
            tA = [r2f(alloc_f(1024), 1024) for _ in range(2)]
            tB = r2f(alloc_f(1024), 1024)
            yn = [r2b(alloc_f(512), 1024) for _ in range(2)]
            DI = r2b(alloc_f(1024), 2048).rearrange("p (h s) -> p h s", s=128)
            for h in range(16):
                ts(DI[:, h, :], cst[:, C_ID:C_ID + 128], vecs[:, V_DSB + h:V_DSB + h + 1], r=["cst", "vecs"], w=["DI"])
            assert o <= 8 * NTOK, o
            mle = cst[:, C_MLE:C_MLE + 128]
            mgt = cst[:, C_MGT:C_MGT + 128]

            def softplus(dst, src, bias_bc, inner, keys_r, key_w, tmp1, tmp2):
                v3 = lambda a: a.rearrange("p (a b) -> p a b", b=inner)
                tt(v3(tmp1), src, bias_bc, ALU.add, r=keys_r, w=[key_w + "_t1"])
                act(tmp2, tmp1, AF.Abs, r=[key_w + "_t1"], w=[key_w + "_t2"])
                act(tmp2, tmp2, AF.Exp, r=[key_w + "_t2"], w=[key_w + "_t2"], scale=-1.0)
                act(tmp2, tmp2, AF.Ln, r=[key_w + "_t2"], w=[key_w + "_t2"], bias=1.0)
                ts(tmp1, tmp1, 0.0, op0=ALU.max, r=[key_w + "_t1"], w=[key_w + "_t1"])
                tt(dst, tmp1, tmp2, ALU.add, r=[key_w + "_t1", key_w + "_t2"], w=[key_w])

            bk = nb()
            for c in range(16):
                t0 = c * 128
                mmg(pb[bk][:, c * 16:(c + 1) * 16], [(Uv[:, kk, t0:t0 + 128], wdt[:, kk, :]) for kk in range(8)],
                    r=[f"U{c // 4}", "wdt"], w=[f"pb{bk}"])
            softplus(dtall, pb[bk][:, 0:256].rearrange("p (c h) -> p c h", h=16),
                     bc(vecs[:, V_DTB:V_DTB + 16].unsqueeze(1), [128, 16, 16]), 16, [f"pb{bk}", "vecs"], "dtall",
                     tA[0][:, 0:256], tA[1][:, 0:256])
            if KSUB < -2:
                return
            S.op("dve", lambda e: e.memset(Sst, 0.0), w=["Sst"])
            S.op("dve", lambda e: e.memset(Sbf, 0.0), w=["Sbf"])
            tt(dta.rearrange("p (c h) -> p c h", h=16), dtall.rearrange("p (c h) -> p c h", h=16),
               bc(aneg[:, :].unsqueeze(1), [128, 16, 16]), ALU.mult, r=["dtall", "aneg"], w=["dta"])
            bk1, bk2 = nb(), nb()
            mmg(pb[bk1][:, 0:256], [(mle, dta)], r=["cst", "dta"], w=[f"pb{bk1}"])
            mmg(pb[bk2][:, 0:256], [(cst[:, C_ONES:C_ONES + 128], dta)], r=["cst", "dta"], w=[f"pb{bk2}"])
            if KSUB < -1:
                return
            if KI < 0:
                return
            act(acs, pb[bk1][:, 0:256], r=[f"pb{bk1}"], w=["acs"])
            if KI < 1:
                return
            act(Ecol, pb[bk1][:, 0:256], AF.Exp, r=[f"pb{bk1}"], w=["Ecol"])
            if KI < 2:
                return
            act(dec, pb[bk2][:, 0:256], AF.Exp, r=[f"pb{bk2}"], w=["dec"])
            if KI < 3:
                return
            tt(wst, acs, pb[bk2][:, 0:256], ALU.subtract, r=[f"pb{bk2}", "acs", "dec", "Ecol"], w=["wst"])
            if KI < 4:
                return
            act(wst, wst, AF.Exp, r=["wst"], w=["wst"], scale=-1.0)
            if KI < 5:
                return
            tt(wst, wst, dtall, ALU.mult, r=["wst", "dtall"], w=["wst"])
            if KSUB < 1:
                return
            PT, PC, PSEG, PY, PO = 0, 1, (2, 3), (4, 5), (6, 7)
            S.op("dve", lambda e: e.memset(small[:, 20:22], -0.5), w=["negh"])

            def stageA(c):
                t0 = c * 128
                pa = c % 2
                trg([(pbb[PT][:, kx * 128:(kx + 1) * 128], ymix[:, 8 + kx, t0:t0 + 128], identb[:]) for kx in range(8)],
                    r=[f"xbf{kx}_{c // 4}" for kx in range(8)] + ["identb"], w=["pb0"])
                yield
                psT3 = pbb[PT][:, 0:1024].rearrange("p (h x) -> p h x", x=64)
                act(xTb[pa], pbb[PT][:, 0:1024], r=["pb0"], w=[f"xTb{pa}"])
                yield
                tt(xdt[pa].rearrange("p (h x) -> p h x", x=64), psT3, bc(dtall[:, c * 16:(c + 1) * 16].unsqueeze(2), [128, 16, 64]),
                   ALU.mult, r=["pb0", "dtall"], w=[f"xdt{pa}"])
                yield
                tt(xdtw[pa].rearrange("p (h x) -> p h x", x=64), psT3, bc(wst[:, c * 16:(c + 1) * 16].unsqueeze(2), [128, 16, 64]),
                   ALU.mult, r=["pb0", "wst"], w=[f"xdtw{pa}"])
                yield
                trg([(pbb[PC][:, 512 + g * 128:512 + (g + 1) * 128], BCT[:, g, t0:t0 + 128], identb[:]) for g in range(2)],
                    r=[f"BCT{c // 4}", "identb"], w=["pb1"])
                yield
                act(Btok[pa], pbb[PC][:, 512:768], r=["pb1"], w=[f"Btok{pa}"])
                yield
                def cbf(e, t0=t0):
                    last = None
                    for g in range(2):
                        last = e.matmul(pb[PC][:, g * 128:(g + 1) * 128], lhsT=BCT[:, g, t0:t0 + 128], rhs=BCT[:, 2 + g, t0:t0 + 128],
                                        start=True, stop=True)
                    return last
                S.op("pe", cbf, r=[f"BCT{c // 4}"], w=["pb1"])
                yield
                tt(CBm[pa], pb[PC][:, 0:256].rearrange("p (g s) -> p g s", s=128), bc(mle.unsqueeze(1), [128, 2, 128]),
                   ALU.mult, r=["pb1", "cst"], w=[f"CBm{pa}"])
                yield
                for q in range(4):
                    qb = q % 2
                    sbk = PSEG[qb]
                    a3 = Ah4[qb].rearrange("p (h t) -> p h t", t=128)
                    tt(a3, bc(mgt.unsqueeze(1), [128, 4, 128]), bc(dta[:, c * 16 + 4 * q:c * 16 + 4 * q + 4].unsqueeze(2), [128, 4, 128]),
                       ALU.mult, r=["cst", "dta"], w=[f"Ah{qb}"])
                    yield
                    def segf(e, qb=qb, sbk=sbk):
                        last = None
                        for j in range(4):
                            last = e.matmul(pb[sbk][:, j * 128:(j + 1) * 128], lhsT=Ah4[qb][:, j * 128:(j + 1) * 128], rhs=mle,
                                            start=True, stop=True)
                        return last
                    S.op("pe", segf, r=[f"Ah{qb}", "cst"], w=[f"pb{sbk}"])
                    yield
                    act(Lh4[qb], pb[sbk][:, :], AF.Exp, r=[f"pb{sbk}"], w=[f"Lh{qb}"])
                    yield
                    tt(Mh[pa][:, 4 * q:4 * q + 4, :], Lh4[qb].rearrange("p (h t) -> p h t", t=128),
                       bc(CBm[pa][:, q // 2, :].unsqueeze(1), [128, 4, 128]), ALU.mult,
                       r=[f"Lh{qb}", f"CBm{pa}"], w=[f"Mh{pa}_{q}"], eng="pool")
                    yield

            def stageB(c):
                t0 = c * 128
                pa = c % 2
                ab = c % 2
                for g in range(2):
                    mmg(pb[PO[g]][:, :], [(BCT[:, 2 + g, t0:t0 + 128], Sbf[:, g * 512:(g + 1) * 512])],
                        r=[f"BCT{c // 4}", "Sbf"], w=[f"pb{6 + g}"])
                    yield
                for h in range(16):
                    yb = PY[h // 8]
                    mmg(pb[yb][:, (h % 8) * 64:(h % 8 + 1) * 64],
                        [(Mh[pa][:, h, :], xdt[pa][:, h * 64:(h + 1) * 64]), (DI[:, h, :], xTb[pa][:, h * 64:(h + 1) * 64])],
                        r=[f"Mh{pa}_{h // 4}", f"xdt{pa}", f"xTb{pa}", "DI"], w=[f"pb{4 + h // 8}"])
                    yield
                for g in range(2):
                    tt(tA[ab][:, g * 512:(g + 1) * 512].rearrange("p (h x) -> p h x", x=64),
                       pb[PO[g]][:, :].rearrange("p (h x) -> p h x", x=64),
                       bc(Ecol[:, c * 16 + g * 8:c * 16 + g * 8 + 8].unsqueeze(2), [128, 8, 64]), ALU.mult,
                       r=[f"pb{6 + g}", "Ecol"], w=[f"tA{ab}_{g}"])
                    yield
                for g in range(2):
                    mmg(pb[PO[g]][:, :], [(Btok[pa][:, g * 128:(g + 1) * 128], xdtw[pa][:, g * 512:(g + 1) * 512])],
                        r=[f"Btok{pa}", f"xdtw{pa}"], w=[f"pb{6 + g}"])
                    yield
                tt(Sst.rearrange("p (h x) -> p h x", x=64), Sst.rearrange("p (h x) -> p h x", x=64),
                   bc(dec[:, c * 16:(c + 1) * 16].unsqueeze(2), [128, 16, 64]), ALU.mult, r=["Sst", "dec"], w=["Sst"], eng="pool")
                yield
                for g in range(2):
                    tt(Sst[:, g * 512:(g + 1) * 512], pb[PO[g]][:, :], Sst[:, g * 512:(g + 1) * 512], ALU.add,
                       r=[f"pb{6 + g}", "Sst"], w=["Sst"])
                    yield
                act(Sbf, Sst, r=["Sst"], w=["Sbf"])
                yield
                for g in range(2):
                    tt(tA[ab][:, g * 512:(g + 1) * 512], pb[PY[g]][:, :], tA[ab][:, g * 512:(g + 1) * 512], ALU.add,
                       r=[f"pb{4 + g}", f"tA{ab}_{g}"], w=[f"tA{ab}_{g}"])
                    yield
                for g in range(2):
                    mmg(pb[PO[g]][:, :], [(Uv[:, kk, t0:t0 + 128], wz[:, kk, g * 512:(g + 1) * 512]) for kk in range(8)],
                        r=[f"U{c // 4}", "wz"], w=[f"pb{6 + g}"])
                    yield
                    tbg = tB[:, g * 512:(g + 1) * 512]
                    act(tbg, pb[PO[g]][:, :], AF.Tanh, r=[f"pb{6 + g}"], w=[f"tB{g}"], scale=0.5)
                    yield
                    stt(tbg, tbg, 1.0, pb[PO[g]][:, :], ALU.add, ALU.mult, r=[f"pb{6 + g}", f"tB{g}"], w=[f"tB{g}"])
                    yield
                    stt(tA[ab][:, g * 512:(g + 1) * 512], tA[ab][:, g * 512:(g + 1) * 512], 0.5, tbg, ALU.mult, ALU.mult,
                        r=[f"tA{ab}_{g}", f"tB{g}"], w=[f"tA{ab}_{g}"])
                    yield
                    act(tbg, tA[ab][:, g * 512:(g + 1) * 512], AF.Square, r=[f"tA{ab}_{g}"], w=[f"tB{g}", f"ssq{g}"],
                        accum_out=small[:, 16 + g:17 + g])
                    yield
                ts(small[:, 18:20], small[:, 16:18], 1.0 / 512, RMS_EPS, op0=ALU.mult, op1=ALU.add, r=["ssq0", "ssq1"], w=["rstd"])
                yield
                S.op("pool", lambda e: e.tensor_tensor(out=small[:, 18:20], in0=small[:, 18:20], in1=small[:, 20:22], op=ALU.pow),
                     r=["rstd", "negh"], w=["rstd"])
                yield
                for g in range(2):
                    stt(yn[ab][:, g * 512:(g + 1) * 512], tA[ab][:, g * 512:(g + 1) * 512], small[:, 18 + g:19 + g],
                        vecs[:, V_SNWB + g * 512:V_SNWB + (g + 1) * 512], ALU.mult, ALU.mult,
                        r=[f"tA{ab}_{g}", "rstd", "vecs"], w=[f"yn{ab}"])
                    yield

            def stageC(c):
                t0 = c * 128
                ab = c % 2
                trg([(pbb[PT][:, kx * 128:(kx + 1) * 128], yn[ab][:, kx * 128:(kx + 1) * 128], identb[:]) for kx in range(8)],
                    r=[f"yn{ab}", "identb"], w=["pb0"])
                yield
                act(ymix[:, 8:16, t0:t0 + 128], pbb[PT][:, 0:1024].rearrange("p (k t) -> p k t", t=128),
                    r=["pb0"], w=[f"xbf{kx}_{c // 4}" for kx in range(8)] + ["ymix_ssm"])
                yield

            def interleave(gens_w):
                live = [[g, w] for g, w in gens_w]
                while live:
                    for ent in list(live):
                        g, w = ent
                        for _ in range(w):
                            try:
                                next(g)
                            except StopIteration:
                                live.remove(ent)
                                break

            for c in range(18):
                gd = {}
                if c < 16:
                    gd["A"] = (stageA(c), QA)
                if 1 <= c <= 16:
                    gd["B"] = (stageB(c - 1), QB)
                if c >= 2:
                    gd["C"] = (stageC(c - 2), QC)
                interleave([gd[k] for k in QORD if k in gd])
            dma("sp", nsp_d, Sst, "nsp", r=["Sst"])
            S.fence()

            if KSTOP < 5:
                return
            o = 0
            dtch = r2f(alloc_f(128), 128).rearrange("p (k j) -> p k j", j=NS)
            dAch = r2f(alloc_f(128), 128).rearrange("p (k j) -> p k j", j=NS)
            xdts = r2f(alloc_f(128), 128).rearrange("p (k j) -> p k j", j=NS)
            ysm = r2f(alloc_f(128), 128).rearrange("p (k j) -> p k j", j=NS)
            zs = r2f(alloc_f(128), 128).rearrange("p (k j) -> p k j", j=NS)
            gs = r2f(alloc_f(128), 128).rearrange("p (k j) -> p k j", j=NS)
            t1s = r2f(alloc_f(128), 128)
            t2s = r2f(alloc_f(128), 128)
            sqs = r2b(alloc_f(64), 128).rearrange("p (k j) -> p k j", j=NS)
            BCtok = r2f(alloc_f(512), 512)
            rhsj = [r2f(alloc_f(512), 512) for _ in range(2)]
            T1s = r2f(alloc_f(1024), 1024)
            Stb = [r2f(alloc_f(1024), 1024).rearrange("p (k n) -> p k n", n=128) for _ in range(4)]
            softplus(dtch.rearrange("p k j -> p (k j)"), dts[:],
                     bc(vecs[:, V_DTBCH:V_DTBCH + 8].unsqueeze(2), [128, 8, NS]), NS, ["dts", "vecs"], "dtch", t1s, t2s)
            tt(dAch, dtch, bc(anegch[:, :].unsqueeze(2), [128, 8, NS]), ALU.mult, r=["dtch", "anegch"], w=["dAch"])
            act(dAch, dAch, AF.Exp, r=["dAch"], w=["dAch"])
            tt(xdts, xcs[:, 0:8, :], dtch, ALU.mult, r=["xcs", "dtch"], w=["xdts"])
            bkT = nb()
            trg([(pb[bkT][0:16, m * 128:(m + 1) * 128], xcs[:, 8 + m, :], cst[:, C_ID:C_ID + 128]) for m in range(4)],
                r=["xcs", "cst"], w=[f"pb{bkT}"])
            act(BCtok[0:16, :], pb[bkT][0:16, :], r=[f"pb{bkT}"], w=["BCtok"])
            def ld_state(j):
                si = j % 4
                for h2 in range(2):
                    dma("sp", Stb[si][h2 * 64:(h2 + 1) * 64, :, :], sts_d[j, h2::2, :, :].rearrange("k p n -> p k n"),
                        f"St{si}_{h2}", w=[f"St{si}"])
            def decay_state(j):
                si = j % 4
                for k in range(8):
                    act(Stb[si][:, k, :], Stb[si][:, k, :], AF.Identity, r=[f"St{si}", "dAch"], w=[f"St{si}"], scale=dAch[:, k, j:j + 1])
            ld_state(0)
            ld_state(1)
            decay_state(0)
            for j in range(NS):
                sbuf_i = j % 4
                St = Stb[sbuf_i]
                if j + 2 < NS:
                    ld_state(j + 2)
                if j + 1 < NS:
                    decay_state(j + 1)
                rj = rhsj[j % 2]
                ts(rj[0:16, :], BCtok[0:16, :], cst[0:16, C_ID + j:C_ID + j + 1], r=["BCtok", "cst"], w=[f"rhsj{j % 2}"])
                bkj = nb()
                mmg(pb[bkj][:, :], [(cst[0:16, C_ONES:C_ONES + 128], rj[0:16, :])], r=[f"rhsj{j % 2}", "cst"], w=[f"pb{bkj}"])
                St2 = St.rearrange("p k n -> p (k n)")
                T1v = T1s.rearrange("p (g k n) -> p g k n", g=2, k=4)
                Bv = bc(pb[bkj][:, 0:256].rearrange("p (g n) -> p g n", n=128).unsqueeze(2), [128, 2, 4, 128])
                Cv = bc(pb[bkj][:, 256:512].rearrange("p (g n) -> p g n", n=128).unsqueeze(2), [128, 2, 4, 128])
                xv = bc(xdts[:, :, j].rearrange("p (g k) -> p g k", k=4).unsqueeze(3), [128, 2, 4, 128])
                tt(T1v, Bv, xv, ALU.mult, r=[f"pb{bkj}", "xdts"], w=["T1s"])
                tt(St2, St2, T1s, ALU.add, r=[f"St{sbuf_i}", "T1s"], w=[f"St{sbuf_i}"])
                tt(T1v, St.rearrange("p (g k) n -> p g k n", k=4), Cv, ALU.mult, r=[f"pb{bkj}", f"St{sbuf_i}"], w=["T1s"])
                S.op("dve", lambda e, j=j: e.tensor_reduce(out=ysm[:, :, j], in_=T1s.rearrange("p (k n) -> p k n", n=128),
                                                           axis=mybir.AxisListType.X, op=ALU.add), r=["T1s"], w=["ysm"])
                for h2 in range(2):
                    dma("act", nss_d[j, h2::2, :, :].rearrange("k p n -> p k n"), St[h2 * 64:(h2 + 1) * 64, :, :],
                        f"nss{sbuf_i}_{h2}", r=[f"St{sbuf_i}"])
                ada_chunks(16 + 2 * j, 2)
            ada_finish()
            tt(gs, xcs[:, 0:8, :], bc(vecs[:, V_DCH:V_DCH + 8].unsqueeze(2), [128, 8, NS]), ALU.mult, r=["xcs", "vecs"], w=["gs"])
            tt(gs, gs, ysm, ALU.add, r=["gs", "ysm"], w=["gs"])
            bkz = nb()
            for k in range(8):
                mmg(pb[bkz][:, k * 16:(k + 1) * 16], [(wz[:, kk, k * 128:(k + 1) * 128], Uv[:, kk, T:NTOK]) for kk in range(8)],
                    r=["wz", "U4"], w=[f"pb{bkz}"])
            zs2 = zs.rearrange("p k j -> p (k j)")
            act(zs2, pb[bkz][:, 0:128], r=[f"pb{bkz}"], w=["zs"])
            act(t1s, zs2, AF.Exp, r=["zs"], w=["t1s"], scale=-1.0)
            act(t1s, t1s, AF.Ln, r=["t1s"], w=["t1s"], bias=1.0)
            act(t1s, t1s, AF.Exp, r=["t1s"], w=["t1s"], scale=-1.0)
            tt(t1s, t1s, zs2, ALU.mult, r=["t1s", "zs"], w=["t1s"])
            gs2 = gs.rearrange("p k j -> p (k j)")
            tt(gs2, gs2, t1s, ALU.mult, r=["gs", "t1s"], w=["gs"])
            act(sqs.rearrange("p k j -> p (k j)"), gs2, AF.Square, r=["gs"], w=["sqs"])
            bkn = nb()
            for g in range(2):
                mmg(pb[bkn][:, g * 16:(g + 1) * 16], [(onesb[:], sqs[:, 4 * g + kk, :]) for kk in range(4)],
                    r=["onesb", "sqs"], w=[f"pb{bkn}"])
            act(t2s[:, 0:32], pb[bkn][:, 0:32], AF.Ln, r=[f"pb{bkn}"], w=["t2s"], scale=1.0 / 512, bias=RMS_EPS)
            act(t2s[:, 0:32], t2s[:, 0:32], AF.Exp, r=["t2s"], w=["t2s"], scale=-0.5)
            for g in range(2):
                tt(gs[:, 4 * g:4 * g + 4, :], gs[:, 4 * g:4 * g + 4, :], bc(t2s[:, g * 16:(g + 1) * 16].unsqueeze(1), [128, 4, NS]),
                   ALU.mult, r=["gs", "t2s"], w=["gs"])
            tt(ymix[:, 8:16, T:NTOK], gs, bc(vecs[:, V_SNWCH:V_SNWCH + 8].unsqueeze(2), [128, 8, NS]), ALU.mult,
               r=["gs", "vecs"], w=["ymix_ssm_s"])
            S.fence()

            if KSTOP < 6:
                return
            o = 0
            pful = [r2f(alloc_f(2052), 2050) for _ in range(2)]
            gcs = [r2f(alloc_f(512), 512) for _ in range(2)]
            cA = [r2f(alloc_f(512), 512) for _ in range(3)]
            cB = [r2f(alloc_f(512), 512) for _ in range(2)]
            gbcv = [r2f(alloc_f(512), 512) for _ in range(4)]
            sqb = [r2b(alloc_f(256), 512) for _ in range(4)]
            rs = [r2f(alloc_f(512), 512) for _ in range(2)]
            ps_s = r2f(alloc_f(16), 16)
            for i in range(2):
                S.op("dve", lambda e, i=i: e.memset(pful[i][:, 0:2], 0.0), w=[f"pf{i}z"])
            iters = []
            for kb in range(4):
                for cc in range(2):
                    for i, (t0, n) in enumerate(TILES):
                        iters.append((kb, cc, i, t0, n))
            slot_of = {}
            BG, BH, BB, BQ = (0, 1), (2, 3), (4, 5), (6, 7)

            def get_slots(kb):
                if kb not in slot_of:
                    base = B_CONV[kb - 1][2] if kb > 0 else B_CONV[kb][0]
                    slot_of[kb] = [wneed(b, base) for b in B_CONV[kb]]
                return slot_of[kb]

            def cS1(it):
                kb, cc, i, t0, n = iters[it]
                k = kb * 2 + cc
                slots = get_slots(kb)
                pbuf = k % 2
                pf = pful[pbuf]
                cw = lambda tap, k=k: vecs[:, V_CW + tap * 8 + k:V_CW + tap * 8 + k + 1]
                bg, bh = BG[it % 2], BH[it % 2]
                for (bank, gi) in ((bg, 0), (bh, 1)):
                    slot, key = slots[gi]
                    mmg(pb[bank][:, 0:n], [(slot[:, kk, cc * 128:(cc + 1) * 128], Uv[:, kk, t0:t0 + n]) for kk in range(8)],
                        r=[key, f"U{i}"], w=[f"pb{bank}"])
                tb = it % 2
                t3 = it % 3
                act(gcs[tb][:, 0:n], pb[bg][:, 0:n], r=[f"pb{bg}"], w=[f"gcs{tb}"])
                if i < 4:
                    tt(pf[:, 2 + t0:2 + t0 + n], pb[bh][:, 0:n], gcs[tb][:, 0:n], ALU.mult,
                       r=[f"pb{bh}", f"gcs{tb}", f"pf{pbuf}z"], w=[f"pf{pbuf}_{i}"])
                    rd = [f"pf{pbuf}_{i}"] + ([f"pf{pbuf}_{i - 1}"] if i > 0 else [f"pf{pbuf}z"])
                    act(cB[tb], pf[:, t0:t0 + n], AF.Identity, r=rd + ["vecs"], w=[f"cB{tb}"], scale=cw(0))
                    stt(cA[t3], pf[:, t0 + 1:t0 + 1 + n], cw(1), cB[tb], ALU.mult, ALU.add, r=rd + [f"cB{tb}"], w=[f"cA{t3}"])
                    stt(cA[t3], pf[:, t0 + 2:t0 + 2 + n], cw(2), cA[t3], ALU.mult, ALU.add, r=rd + [f"cA{t3}"], w=[f"cA{t3}"])
                    if i == 3:
                        S.op("pool", lambda e, k=k, pf=pf: e.tensor_copy(out=ncp_sb[:, k, :], in_=pf[:, T:T + 2]),
                             r=[f"pf{pbuf}_3"], w=["ncp_sb"])
                else:
                    tt(ps_s, pb[bh][:, 0:n], gcs[tb][:, 0:n], ALU.mult, r=[f"pb{bh}", f"gcs{tb}"], w=["ps_s"])
                    ts(cA[t3][:, 0:n], stc[:, k, :, 0], cw(0), r=["stc", "vecs"], w=[f"cA{t3}"])
                    stt(cA[t3][:, 0:n], stc[:, k, :, 1], cw(1), cA[t3][:, 0:n], ALU.mult, ALU.add, r=["stc", f"cA{t3}"], w=[f"cA{t3}"])
                    stt(cA[t3][:, 0:n], ps_s, cw(2), cA[t3][:, 0:n], ALU.mult, ALU.add, r=["ps_s", f"cA{t3}"], w=[f"cA{t3}"])
                    S.op("pool", lambda e, k=k: e.tensor_copy(out=ncs_sb[:, k, :, 0], in_=stc[:, k, :, 1]), r=["stc"], w=["ncs_a"])
                    S.op("pool", lambda e, k=k: e.tensor_copy(out=ncs_sb[:, k, :, 1], in_=ps_s), r=["ps_s"], w=["ncs_b"])

            def cS2(it):
                kb, cc, i, t0, n = iters[it]
                slot, key = get_slots(kb)[2]
                bb = BB[it % 2]
                t3 = it % 3
                t4 = it % 4
                mmg(pb[bb][:, 0:n], [(slot[:, kk, cc * 128:(cc + 1) * 128], Uv[:, kk, t0:t0 + n]) for kk in range(8)],
                    r=[key, f"U{i}"], w=[f"pb{bb}"])
                tt(gbcv[t4][:, 0:n], pb[bb][:, 0:n], cA[t3][:, 0:n], ALU.mult, r=[f"pb{bb}", f"cA{t3}"], w=[f"gbcv{t4}"])
                act(sqb[t4][:, 0:n], gbcv[t4][:, 0:n], AF.Square, r=[f"gbcv{t4}"], w=[f"sqb{t4}"])

            def cS3(it):
                kb, cc, i, t0, n = iters[it]
                k = kb * 2 + cc
                bq = BQ[it % 2]
                t3 = it % 4
                tb = it % 2
                mmg(pb[bq][:, 0:n], [(bonesb[:], sqb[t3][:, 0:n])], r=["bonesb", f"sqb{t3}"], w=[f"pb{bq}"])
                act(rs[tb][:, 0:n], pb[bq][:, 0:n], AF.Ln, r=[f"pb{bq}"], w=[f"rs{tb}"], scale=1.0 / 64, bias=RMS_EPS)
                act(rs[tb][:, 0:n], rs[tb][:, 0:n], AF.Exp, r=[f"rs{tb}"], w=[f"rs{tb}"], scale=-0.5)
                stt(ymix[:, k, t0:t0 + n], gbcv[t3][:, 0:n], vecs[:, V_CNW + k:V_CNW + k + 1], rs[tb][:, 0:n], ALU.mult, ALU.mult,
                    r=[f"gbcv{t3}", f"rs{tb}", "vecs"], w=[f"ymc{k}_{i}"])

            NI = len(iters)
            for s_ in range(NI + 3):
                if s_ < NI:
                    cS1(s_)
                if 0 <= s_ - 1 < NI:
                    cS2(s_ - 1)
                if 0 <= s_ - 3 < NI:
                    cS3(s_ - 3)
            dma("sp", ncp_d, ncp_sb[:].rearrange("p a b -> p (a b)"), "ncp", r=["ncp_sb"])
            dma("sp", ncs_d, ncs_sb[:].rearrange("p a b c -> p (a b c)"), "ncs", r=["ncs_a", "ncs_b"])
            S.fence()

            if KSTOP < 7:
                return
            X1 = R2[:, :].rearrange("p (k t) -> p k t", t=NTOK)
            xk = [Ub32[:, i * 2048:(i + 1) * 2048] for i in range(2)]
            o32 = 4096
            sqt1 = U[:, 2048:2048 + 4096].rearrange("p (k t) -> p k t", t=512)
            sqt2 = [sqt1, sqt1]
            _mean = Ub32[:, 3072:3584]
            _msq = Ub32[:, 3584:4096]
            st4 = [[_mean, _msq, Ub32[:, 4096 + j * 1024:4608 + j * 1024], Ub32[:, 4608 + j * 1024:5120 + j * 1024]] for j in range(2)]
            lt1 = [Ub32[:, 6144 + i * 512:6656 + i * 512] for i in range(2)]
            assert 7168 <= 4 * NTOK
            Vv = R1[:, 0:8 * NTOK].rearrange("p (k t) -> p k t", t=NTOK)
            HQ = R1[:, 8 * NTOK:16 * NTOK].rearrange("p (k t) -> p k t", t=NTOK)

            def layer_norm_all(outs, post, inline=False):
                def stats(i):
                    t0, n = TILES[i]
                    sb_ = i % 2
                    for kk in range(8):
                        if (not inline) and kk % 2 == 1 and n == 512:
                            tt(sqt2[sb_][:, kk, 0:n], X1[:, kk, t0:t0 + n], X1[:, kk, t0:t0 + n], ALU.mult,
                               r=[f"X1_{kk}_{i}"], w=[f"sqt_{kk}"], eng="pool")
                        else:
                            act(sqt2[sb_][:, kk, 0:n], X1[:, kk, t0:t0 + n], AF.Square, r=[f"X1_{kk}_{i}"], w=[f"sqt_{kk}"])
                    b1, b2 = nb(), nb()
                    mmg(pb[b1][:, 0:n], [(cst[:, C_ONES:C_ONES + 128], X1[:, kk, t0:t0 + n]) for kk in range(8)],
                        r=["cst"] + [f"X1_{kk}_{i}" for kk in range(8)], w=[f"pb{b1}"])
                    mmg(pb[b2][:, 0:n], [(onesb[:], sqt2[sb_][:, kk, 0:n]) for kk in range(8)],
                        r=["onesb"] + [f"sqt_{kk}" for kk in range(8)], w=[f"pb{b2}"])
                    mean, msq, rstd, nmr = st4[sb_]
                    ts(mean[:, 0:n], pb[b1][:, 0:n], 1.0 / 1024, r=[f"pb{b1}"], w=["st_mean"])
                    tt(msq[:, 0:n], mean[:, 0:n], mean[:, 0:n], ALU.mult, r=["st_mean"], w=["st_msq"])
                    stt(msq[:, 0:n], pb[b2][:, 0:n], 1.0 / 1024, msq[:, 0:n], ALU.mult, ALU.subtract, r=[f"pb{b2}", "st_msq"], w=["st_msq"])
                    act(rstd[:, 0:n], msq[:, 0:n], AF.Ln, r=["st_msq"], w=[f"st_rstd{sb_}"], bias=LN_EPS)
                    act(rstd[:, 0:n], rstd[:, 0:n], AF.Exp, r=[f"st_rstd{sb_}"], w=[f"st_rstd{sb_}"], scale=-0.5)
                    stt(nmr[:, 0:n], mean[:, 0:n], -1.0, rstd[:, 0:n], ALU.mult, ALU.mult, r=["st_mean", f"st_rstd{sb_}"], w=[f"st_nmr{sb_}"])

                def norm(i):
                    t0, n = TILES[i]
                    sb_ = i % 2
                    mean, msq, rstd, nmr = st4[sb_]
                    for kk in range(8):
                        lb = kk % 2
                        tt(lt1[lb][:, 0:n], X1[:, kk, t0:t0 + n], rstd[:, 0:n], ALU.mult, r=[f"X1_{kk}_{i}", f"st_rstd{sb_}"], w=[f"lt1{lb}"])
                        tt(lt1[lb][:, 0:n], lt1[lb][:, 0:n], nmr[:, 0:n], ALU.add, r=[f"lt1{lb}", f"st_nmr{sb_}"], w=[f"lt1{lb}"])
                        outs(i, t0, n, kk, lt1[lb][:, 0:n], f"lt1{lb}")
                    post(i, t0, n)

                if inline:
                    return lambda i: (stats(i), norm(i))
                stats(0)
                for i in range(len(TILES)):
                    if i + 1 < len(TILES):
                        stats(i + 1)
                    norm(i)

            xTrk = xT_d.rearrange("(k p) t -> p k t", p=128)
            for cb in range(4):
                slots = [wneed(b, B_OUT[cb][0]) for b in B_OUT[cb]]
                for cc in range(2):
                    kd = cb * 2 + cc
                    xb_i = kd % 2
                    dma("sp", xk[xb_i], xTrk[:, kd, :], f"xk{xb_i}", w=[f"xk{xb_i}"])
                    for i, (t0, n) in enumerate(TILES):
                        bk = nb()
                        pairs = []
                        for hh in range(2):
                            slot, key = slots[hh]
                            pairs += [(slot[:, kk, cc * 128:(cc + 1) * 128], ymix[:, hh * 8 + kk, t0:t0 + n]) for kk in range(8)]
                        rk = [slots[0][1], slots[1][1]] + [f"ymc{kk}_{i}" for kk in range(8)] + (["ymix_ssm"] if i < 4 else ["ymix_ssm_s"])
                        mmg(pb[bk][:, 0:n], pairs, r=rk, w=[f"pb{bk}"])
                        if i < 4:
                            stt(X1[:, kd, t0:t0 + n], pb[bk][:, 0:n], mod[:, 16 + kd, 0:1], xk[xb_i][:, t0:t0 + n], ALU.mult, ALU.add,
                                r=[f"pb{bk}", "mod", f"xk{xb_i}"], w=[f"X1_{kd}_{i}"])
                        else:
                            tt(X1[:, kd, t0:t0 + n], pb[bk][:, 0:n], mod[:, 16 + kd, 1:17], ALU.mult, r=[f"pb{bk}", "mod"], w=[f"X1_{kd}_{i}"])
                            tt(X1[:, kd, t0:t0 + n], X1[:, kd, t0:t0 + n], xs[:, kd, :], ALU.add, r=[f"X1_{kd}_{i}", "xs"], w=[f"X1_{kd}_{i}"])
            S.fence()
            def outs1(i, t0, n, kk, xn, xkey):
                if i < 4:
                    act(X1[:, kk, t0:t0 + n], xn, AF.Identity, r=[xkey, "vecs"], w=[f"X1_{kk}_{i}"],
                        scale=vecs[:, V_L1G + kk:V_L1G + kk + 1], bias=vecs[:, V_L1B + kk:V_L1B + kk + 1])
                    if kk in (3, 7):
                        ts(Vv[:, kk, t0:t0 + n], xn, A2[:, kk:kk + 1], B2[:, kk:kk + 1], op0=ALU.mult, op1=ALU.add,
                           r=[xkey, "A2", "B2"], w=[f"V{i}"])
                    else:
                        act(Vv[:, kk, t0:t0 + n], xn, AF.Identity, r=[xkey, "A2", "B2"], w=[f"V{i}"],
                            scale=A2[:, kk:kk + 1], bias=B2[:, kk:kk + 1])
                else:
                    act(X1[:, kk, t0:t0 + n], xn, AF.Identity, r=[xkey, "vecs"], w=[f"X1_{kk}_{i}"],
                        scale=vecs[:, V_L1G + kk:V_L1G + kk + 1], bias=vecs[:, V_L1B + kk:V_L1B + kk + 1])
                    tt(xn, X1[:, kk, t0:t0 + n], mod[:, 32 + kk, 1:17], ALU.mult, r=[f"X1_{kk}_{i}", "mod"], w=[xkey])
                    tt(Vv[:, kk, t0:t0 + n], xn, mod[:, 24 + kk, 1:17], ALU.add, r=[xkey, "mod"], w=[f"V{i}"])
            layer_norm_all(outs1, lambda i, t0, n: None)

            if KSTOP < 8:
                return
            rl = [Ub32[:, i * 512:(i + 1) * 512] for i in range(2)]
            yo = [Ub32[:, 1024 + i * 512:1536 + i * 512] for i in range(2)]
            yTr = yT_d.rearrange("(k p) t -> p k t", p=128)
            ysTr = ysT_d.rearrange("(k p) t -> p k t", p=128)

            def outs2(i, t0, n, kk, xn, xkey):
                act(X1[:, kk, t0:t0 + n], xn, AF.Identity, r=[xkey, "vecs"], w=[f"X1_{kk}_{i}"],
                    scale=vecs[:, V_L2G + kk:V_L2G + kk + 1], bias=vecs[:, V_L2B + kk:V_L2B + kk + 1])

            def post2(i, t0, n):
                if i < 4:
                    dma("sp", yTr[:, :, t0:t0 + n], X1[:, :, t0:t0 + n], f"yout{i}", r=[f"X1_{kk}_{i}" for kk in range(8)])
                else:
                    dma("sp", ysTr, X1[:, :, t0:t0 + n], f"yout{i}", r=[f"X1_{kk}_{i}" for kk in range(8)])
            ln2_tile = layer_norm_all(outs2, post2, inline=True)
            it = 0
            for q in range(4):
                for bi in range(4):
                    slot, key = wneed(B_UP[q][bi])
                    for cc in range(2):
                        f = bi * 2 + cc
                        for i, (t0, n) in enumerate(TILES):
                            bk = nb()
                            mmg(pb[bk][:, 0:n], [(slot[:, kk, cc * 128:(cc + 1) * 128], Vv[:, kk, t0:t0 + n]) for kk in range(8)],
                                r=[key, f"V{i}"], w=[f"pb{bk}"])
                            tb = it % 2
                            it += 1
                            act(rl[tb][:, 0:n], pb[bk][:, 0:n], AF.Relu, r=[f"pb{bk}"], w=[f"rl{tb}"])
                            tt(HQ[:, f, t0:t0 + n], pb[bk][:, 0:n], rl[tb][:, 0:n], ALU.mult, r=[f"pb{bk}", f"rl{tb}"], w=[f"HQ{f}_{i}"])
                def down_tile(slot, key, cc, kd, i, t0, n):
                    bk = nb()
                    mmg(pb[bk][:, 0:n], [(slot[:, kk, cc * 128:(cc + 1) * 128], HQ[:, kk, t0:t0 + n]) for kk in range(8)],
                        r=[key] + [f"HQ{kk}_{i}" for kk in range(8)], w=[f"pb{bk}"])
                    if i < 4:
                        stt(X1[:, kd, t0:t0 + n], pb[bk][:, 0:n], mod[:, 40 + kd, 0:1], X1[:, kd, t0:t0 + n], ALU.mult, ALU.add,
                            r=[f"pb{bk}", "mod", f"X1_{kd}_{i}"], w=[f"X1_{kd}_{i}"])
                    else:
                        tt(rl[0][:, 0:n], pb[bk][:, 0:n], mod[:, 40 + kd, 1:17], ALU.mult, r=[f"pb{bk}", "mod"], w=["rl0"])
                        tt(X1[:, kd, t0:t0 + n], X1[:, kd, t0:t0 + n], rl[0][:, 0:n], ALU.add, r=[f"X1_{kd}_{i}", "rl0"], w=[f"X1_{kd}_{i}"])

                if q < 3:
                    for bi in range(4):
                        slot, key = wneed(B_DN[q][bi])
                        for cc in range(2):
                            for i, (t0, n) in enumerate(TILES):
                                down_tile(slot, key, cc, bi * 2 + cc, i, t0, n)
                else:
                    slots = [wneed(b_, B_DN[q][0]) for b_ in B_DN[q]]
                    for i, (t0, n) in enumerate(TILES):
                        for bi in range(4):
                            slot, key = slots[bi]
                            for cc in range(2):
                                down_tile(slot, key, cc, bi * 2 + cc, i, t0, n)
                        ln2_tile(i)

        o = 0
        phases()
        with nc.Block() as block:
            S.emit(block)
    return nc


def _fm(v, nchunk):
    return np.ascontiguousarray(v.reshape(nchunk, 128).T)


_CACHE = {}


def kernel(x_prompt, x_sample, state_conv, state_ssm_conv, state_ssm, c_prompt, c_sample,
           w_ada, b_ada, w_in, conv_w, conv_norm_w, ssm_conv_w, ssm_conv_b, dt_bias, a_log, d_skip,
           ssm_norm_w, w_out, ln1_g, ln1_b, w_up, w_down, ln2_g, ln2_b):
    f = lambda a: np.ascontiguousarray(np.asarray(a, dtype=np.float32))
    x_prompt, x_sample, state_conv, state_ssm_conv, state_ssm = map(f, (x_prompt, x_sample, state_conv, state_ssm_conv, state_ssm))
    c_prompt, c_sample = f(c_prompt), f(c_sample)
    w_ada, w_in, w_out, w_up, w_down = f(w_ada)[0], f(w_in)[0], f(w_out)[0], f(w_up)[0], f(w_down)[0]
    b_ada, conv_w, conv_norm_w, ssm_conv_w, ssm_conv_b = f(b_ada)[0], f(conv_w)[0], f(conv_norm_w)[0], f(ssm_conv_w)[0], f(ssm_conv_b)[0]
    dt_bias, a_log, d_skip, ssm_norm_w = f(dt_bias)[0], f(a_log)[0], f(d_skip)[0], f(ssm_norm_w)[0]
    ln1_g, ln1_b, ln2_g, ln2_b = f(ln1_g)[0], f(ln1_b)[0], f(ln2_g)[0], f(ln2_b)[0]

    vecs = np.zeros((128, NV), np.float32)
    vecs[:, V_BADA:V_BADA + 48] = _fm(b_ada, 48)
    for tap in range(3):
        vecs[:, V_CW + tap * 8:V_CW + tap * 8 + 8] = _fm(conv_w[tap], 8)
    vecs[:, V_CNW:V_CNW + 8] = _fm(conv_norm_w, 8)
    for tap in range(4):
        vecs[:, V_SCW + tap * 12:V_SCW + tap * 12 + 12] = _fm(ssm_conv_w[tap], 12)
    vecs[:, V_SCB:V_SCB + 12] = _fm(ssm_conv_b, 12)
    vecs[:, V_L1G:V_L1G + 8] = _fm(ln1_g, 8)
    vecs[:, V_L1B:V_L1B + 8] = _fm(ln1_b, 8)
    vecs[:, V_L2G:V_L2G + 8] = _fm(ln2_g, 8)
    vecs[:, V_L2B:V_L2B + 8] = _fm(ln2_b, 8)
    vecs[:, V_DCH:V_DCH + 8] = _fm(np.repeat(d_skip, 64), 8)
    vecs[:, V_ALCH:V_ALCH + 8] = _fm(np.repeat(a_log, 64), 8)
    vecs[:, V_DTBCH:V_DTBCH + 8] = _fm(np.repeat(dt_bias, 64), 8)
    vecs[:, V_SNWCH:V_SNWCH + 8] = _fm(ssm_norm_w, 8)
    vecs[:, V_ALB:V_ALB + 16] = a_log[None, :]
    vecs[:, V_DTB:V_DTB + 16] = dt_bias[None, :]
    vecs[:, V_DSB:V_DSB + 16] = d_skip[None, :]
    vecs[:, V_SNWB:V_SNWB + 1024] = ssm_norm_w[None, :]
    consts = np.zeros((128, NC_), np.float32)
    idx = np.arange(128)
    consts[:, C_ID:C_ID + 128] = np.eye(128, dtype=np.float32)
    consts[:, C_MLE:C_MLE + 128] = (idx[:, None] <= idx[None, :])
    consts[:, C_MGT:C_MGT + 128] = (idx[:, None] > idx[None, :])
    consts[:, C_BONES:C_BONES + 128] = ((idx[:, None] // 64) == (idx[None, :] // 64))
    consts[:, C_ONES:C_ONES + 128] = 1.0
    wdtx = np.ascontiguousarray(np.repeat(w_in[:, 5632:5648], 64, axis=1))

    in_maps = []
    for b in range(8):
        js = slice(16 * b, 16 * b + 16)
        stc = state_conv[0, js]
        stsc = state_ssm_conv[0, js]
        in_maps.append({
            "xT": np.ascontiguousarray(x_prompt[b].T),
            "xsT": np.ascontiguousarray(x_sample[js, 0, :].T),
            "cT": np.ascontiguousarray(np.concatenate([c_prompt[b:b + 1], c_sample[js]], axis=0).T),
            "stc": np.ascontiguousarray(stc.reshape(16, 2, 8, 128).transpose(3, 2, 0, 1).reshape(128, -1)),
            "stsc": np.ascontiguousarray(stsc.reshape(16, 3, 12, 128).transpose(3, 2, 0, 1).reshape(128, -1)),
            "sts": np.ascontiguousarray(state_ssm[0, js]),
            "w_ada": w_ada, "w_in": w_in, "wdtx": wdtx, "w_out": w_out, "w_up": w_up, "w_down": w_down,
            "vecs": vecs, "consts": consts,
        })
    if "nc" not in _CACHE:
        _CACHE["nc"] = build_program()
    res = run_bass_kernel_spmd(_CACHE["nc"], in_maps, core_ids=list(range(8)))
    R = res.results
    y_prompt = np.stack([R[b]["yT"].T for b in range(8)]).astype(np.float32)
    y_sample = np.concatenate([R[b]["ysT"].T for b in range(8)], axis=0)[:, None, :].astype(np.float32)
    ncp = np.stack([R[b]["ncp"].reshape(128, 8, 2).transpose(2, 1, 0).reshape(2, 1024) for b in range(8)])[None]
    nscp = np.stack([R[b]["nscp"].reshape(128, 12, 3).transpose(2, 1, 0).reshape(3, 1536) for b in range(8)])[None]
    nsp = np.stack([R[b]["nsp"].reshape(128, 16, 64).transpose(1, 2, 0) for b in range(8)])[None]
    ncs = np.concatenate([R[b]["ncs"].reshape(128, 8, 16, 2).transpose(2, 3, 1, 0).reshape(16, 2, 1024) for b in range(8)], axis=0)[None]
    nscs = np.concatenate([R[b]["nscs"].reshape(128, 12, 16, 3).transpose(2, 3, 1, 0).reshape(16, 3, 1536) for b in range(8)], axis=0)[None]
    nss = np.concatenate([R[b]["nss"] for b in range(8)], axis=0)[None]
    c = lambda a: np.ascontiguousarray(a, dtype=np.float32)
    return (c(y_prompt), c(y_sample), c(ncp), c(nscp), c(nsp), c(ncs), c(nscs), c(nss))
```

```python
import os
import numpy as np
from contextlib import ExitStack
import concourse.bass as bass
import concourse.mybir as mybir
from concourse.bass_utils import run_bass_kernel_spmd

F32 = mybir.dt.float32
BF16 = mybir.dt.bfloat16
AF = mybir.ActivationFunctionType
ALU = mybir.AluOpType

ENGS = ("pe", "act", "dve", "pool", "sp")

T = 2048
NS = 16
NTOK = T + NS
ALPHA = 2.0 ** 0.25
LN_EPS = 1e-5 / (ALPHA * ALPHA)
RMS_EPS = 1e-5
NSLOT = 6
TILES = [(0, 512), (512, 512), (1024, 512), (1536, 512), (2048, 16)]

V_BADA, V_CW, V_CNW, V_SCW, V_SCB = 0, 48, 72, 80, 128
V_L1G, V_L1B, V_L2G, V_L2B = 140, 148, 156, 164
V_DCH, V_ALCH, V_DTBCH, V_SNWCH = 172, 180, 188, 196
V_ALB, V_DTB, V_DSB, V_SNWB = 204, 220, 236, 252
NV = 252 + 1024
C_ID, C_MLE, C_MGT, C_BONES, C_ONES = 0, 128, 256, 384, 512
NC_ = 640


KSTOP = int(os.environ.get('KSTOP', '99'))
KSUB = int(os.environ.get('KSUB', '99'))
QA = int(os.environ.get('QA', '8'))
QB = int(os.environ.get('QB', '12'))
QC = int(os.environ.get('QC', '2'))
QORD = os.environ.get('QORD', 'BAC')
KI = int(os.environ.get('KI', '99'))


class _Stop(Exception):
    pass


class Sched:
    def __init__(self, nc, es):
        self.nc = nc
        self.es = es
        self.q = {e: [] for e in ENGS}
        self.sems = {}
        self.cnt = {}
        self.waited = {e: {} for e in ENGS}
        self.lastw = {}
        self.readers = {}
        for e in ENGS:
            self._sem("E_" + e)

    def _sem(self, name):
        if name not in self.sems:
            self.sems[name] = self.es.enter_context(self.nc.semaphore(name))
            self.cnt[name] = 0
        return self.sems[name]

    def op(self, eng, fn, r=(), w=(), dma=None):
        deps = {}

        def add(d):
            if d is None:
                return
            s, v, e2 = d
            if e2 == "pe" and eng == "pe" and dma is None:
                return
            if deps.get(s, 0) < v:
                deps[s] = v

        w = list(w) + [k for k in r if k.startswith("pb") and k not in w]
        for k in r:
            add(self.lastw.get(k))
        for k in w:
            add(self.lastw.get(k))
            for d in self.readers.get(k, ()):
                add(d)
        waits = []
        for s, v in deps.items():
            if self.waited[eng].get(s, 0) < v:
                self.waited[eng][s] = v
                waits.append((s, v))
        if dma is not None:
            sname = "D_" + dma
            self._sem(sname)
            self.cnt[sname] += 16
            me = (sname, self.cnt[sname], "dma")
            inc = 16
        else:
            sname = "E_" + eng
            self.cnt[sname] += 1
            me = (sname, self.cnt[sname], eng)
            inc = 1
        self.q[eng].append((waits, fn, sname, inc))
        for k in w:
            self.lastw[k] = me
            self.readers[k] = []
        for k in r:
            self.readers.setdefault(k, []).append(me)
        return me

    def fence(self):
        snap = dict(self.cnt)
        for e in ENGS:
            waits = []
            for s, v in snap.items():
                if v > 0 and self.waited[e].get(s, 0) < v and s != "E_" + e:
                    self.waited[e][s] = v
                    waits.append((s, v))
            if waits:
                self.q[e].append((waits, None, None, 0))

    def emit(self, block):
        sems = self.sems
        fin = [(s, v) for s, v in self.cnt.items() if v > 0]

        def make(ename):
            def body(eng):
                for waits, fn, sname, inc in self.q[ename]:
                    for s, v in waits:
                        eng.wait_ge(sems[s], v)
                    if fn is None:
                        continue
                    inst = fn(eng)
                    inst.then_inc(sems[sname], inc)
                if ename == "sp":
                    for s, v in fin:
                        eng.wait_ge(sems[s], v)
            return body

        block.tensor(make("pe"))
        block.scalar(make("act"))
        block.vector(make("dve"))
        block.gpsimd(make("pool"))
        block.sync(make("sp"))


def build_program():
    nc = bass.Bass("TRN2", target_bir_lowering=False)
    din = lambda n, sh: nc.dram_tensor(n, sh, F32, kind="ExternalInput").ap()
    dout = lambda n, sh: nc.dram_tensor(n, sh, F32, kind="ExternalOutput").ap()
    xT_d = din("xT", [1024, T])
    xsT_d = din("xsT", [1024, NS])
    cT_d = din("cT", [1024, 17])
    stc_d = din("stc", [128, 8 * NS * 2])
    stsc_d = din("stsc", [128, 12 * NS * 3])
    sts_d = din("sts", [NS, 16, 64, 128])
    w_ada_d = din("w_ada", [1024, 6144])
    w_in_d = din("w_in", [1024, 5648])
    wdtx_d = din("wdtx", [1024, 1024])
    w_out_d = din("w_out", [2048, 1024])
    w_up_d = din("w_up", [1024, 4096])
    w_down_d = din("w_down", [4096, 1024])
    vecs_d = din("vecs", [128, NV])
    cst_d = din("consts", [128, NC_])
    yT_d = dout("yT", [1024, T])
    ysT_d = dout("ysT", [1024, NS])
    ncp_d = dout("ncp", [128, 16])
    nscp_d = dout("nscp", [128, 36])
    nsp_d = dout("nsp", [128, 1024])
    ncs_d = dout("ncs", [128, 8 * NS * 2])
    nscs_d = dout("nscs", [128, 12 * NS * 3])
    nss_d = dout("nss", [NS, 16, 64, 128])

    with ExitStack() as es:
        S = Sched(nc, es)
        sb = lambda n, sh, dt=F32: es.enter_context(nc.sbuf_tensor("s_" + n, sh, dt))
        R1 = sb("R1", [128, 16 * NTOK], BF16)
        R2 = sb("R2", [128, 8 * NTOK], F32)
        U = sb("U", [128, 8 * NTOK], BF16)
        ring = [sb(f"wr{i}", [128, 8, 256], BF16) for i in range(NSLOT)]
        vecs = sb("vecs", [128, NV])
        cst = sb("cst", [128, NC_])
        mod = sb("mod", [128, 48, 17])
        cTf = sb("cTf", [128, 8, 17])
        cTb = sb("cTb", [128, 8, 17], BF16)
        xs = sb("xs", [128, 8, NS])
        identb = sb("identb", [128, 128], BF16)
        bonesb = sb("bonesb", [128, 128], BF16)
        onesb = sb("onesb", [128, 128], BF16)
        aneg = sb("aneg", [128, 16])
        anegch = sb("anegch", [128, 8])
        wdt = sb("wdt", [128, 8, 16], BF16)
        A2 = sb("A2", [128, 8])
        B2 = sb("B2", [128, 8])
        ncp_sb = sb("ncp_sb", [128, 8, 2])
        nscp_sb = sb("nscp_sb", [128, 12, 3])
        stc = sb("stc", [128, 8, NS, 2])
        stsc = sb("stsc", [128, 12, NS, 3])
        ncs_sb = sb("ncs_sb", [128, 8, NS, 2])
        nscs_sb = sb("nscs_sb", [128, 12, NS, 3])
        xcs = sb("xcs", [128, 12, NS])
        small = sb("small", [128, 64])
        pb = [es.enter_context(nc.psum_tensor(f"pb{i}", [128, 512], F32)) for i in range(8)]
        pbb = [p.bitcast(BF16) for p in pb]

        R1b = R1
        ymix = R1[:, :].rearrange("p (k t) -> p k t", t=NTOK)
        Uv = U[:, :].rearrange("p (k t) -> p k t", t=NTOK)
        R2b = R2.bitcast(BF16)
        Ub32 = U.bitcast(F32)

        def r2f(off, n):
            return R2[:, off:off + n]

        def r2b(off_f32, n):
            return R2b[:, 2 * off_f32:2 * off_f32 + n]

        def act(out, in_, func=AF.Copy, r=(), w=(), **kw):
            S.op("act", lambda e: e.activation(out=out, in_=in_, func=func, **kw), r=r, w=w)

        def tt(out, in0, in1, op, r=(), w=(), eng="dve"):
            S.op(eng, lambda e: e.tensor_tensor(out=out, in0=in0, in1=in1, op=op), r=r, w=w)

        def ts(out, in0, s1, s2=None, op0=ALU.mult, op1=None, r=(), w=(), eng="dve"):
            if op1 is None:
                S.op(eng, lambda e: e.tensor_scalar(out=out, in0=in0, scalar1=s1, scalar2=None, op0=op0), r=r, w=w)
            else:
                S.op(eng, lambda e: e.tensor_scalar(out=out, in0=in0, scalar1=s1, scalar2=s2, op0=op0, op1=op1), r=r, w=w)

        def stt(out, in0, scalar, in1, op0, op1, r=(), w=(), accum=None):
            if accum is None:
                S.op("dve", lambda e: e.scalar_tensor_tensor(out=out, in0=in0, scalar=scalar, in1=in1, op0=op0, op1=op1), r=r, w=w)
            else:
                S.op("dve", lambda e: e.scalar_tensor_tensor(out=out, in0=in0, scalar=scalar, in1=in1, op0=op0, op1=op1, accum_out=accum), r=r, w=w)

        def mmg(out, pairs, r=(), w=()):
            def f(e):
                n = len(pairs)
                last = None
                for i, (l, rr) in enumerate(pairs):
                    last = e.matmul(out, lhsT=l, rhs=rr, start=(i == 0), stop=(i == n - 1))
                return last
            S.op("pe", f, r=r, w=w)

        def trg(items, r=(), w=()):
            def f(e):
                last = None
                for (o, i_, idn) in items:
                    last = e.transpose(out=o, in_=i_, identity=idn)
                return last
            S.op("pe", f, r=r, w=w)

        def dma(eng, out, in_, key, r=(), w=()):
            S.op(eng, lambda e: e.dma_start(out=out, in_=in_), r=r, w=w, dma=key)

        def bc(ap, shape):
            return ap.broadcast_to(shape)

        blocks = []

        def addblk(wd, r0, c0, ncols=256):
            blocks.append((wd[r0:r0 + 1024, c0:c0 + ncols].rearrange("(k p) c -> p k c", p=128), ncols))
            return len(blocks) - 1

        wstate = {"next": 0}

        def wneed(i, base=None):
            lim = min(len(blocks), (i if base is None else base) + NSLOT)
            while wstate["next"] < lim:
                j = wstate["next"]
                src, ncols = blocks[j]
                sl = j % NSLOT
                dma("pool", ring[sl][:, :, 0:ncols], src, f"wr{sl}", w=[f"wr{sl}"])
                wstate["next"] += 1
            assert wstate["next"] > i, ("weight block not loaded", i, base)
            return ring[i % NSLOT], f"wr{i % NSLOT}"

        B_XBC = [addblk(w_in_d, 0, 4096 + c * 256) for c in range(6)]
        B_DTX = [addblk(wdtx_d, 0, c * 256) for c in range(4)]
        B_ADA = [None] * 8 + [addblk(w_ada_d, 0, c * 256) for c in range(8, 24)]
        B_CONV = []
        for kb in range(4):
            B_CONV.append([addblk(w_in_d, 0, g * 1024 + kb * 256) for g in (1, 2, 0)])
        B_OUT = []
        for cb in range(4):
            B_OUT.append([addblk(w_out_d, h * 1024, cb * 256) for h in range(2)])
        B_UP, B_DN = [], []
        for q in range(4):
            B_UP.append([addblk(w_up_d, 0, q * 1024 + i * 256) for i in range(4)])
            B_DN.append([addblk(w_down_d, q * 1024, i * 256) for i in range(4)])

        bank_rr = {"i": 0}

        def nb():
            i = bank_rr["i"] % 8
            bank_rr["i"] += 1
            return i

        def phases():
            nonlocal o
            if KSTOP < 0:
                return
            dma("sp", vecs[:], vecs_d, "vecs", w=["vecs"])
            dma("sp", cst[:], cst_d, "cst", w=["cst"])
            dma("sp", cTf[:], cT_d.rearrange("(k p) c -> p k c", p=128), "cT", w=["cTf"])
            dma("sp", xs[:], xsT_d.rearrange("(k p) c -> p k c", p=128), "xs", w=["xs"])
            dma("sp", stc[:].rearrange("p a b c -> p (a b c)"), stc_d, "stc", w=["stc"])
            dma("sp", stsc[:].rearrange("p a b c -> p (a b c)"), stsc_d, "stsc", w=["stsc"])
            dma("pool", wdt[:], w_in_d[:, 5632:5648].rearrange("(k p) c -> p k c", p=128), "wdt", w=["wdt"])
            act(cTb[:], cTf[:], r=["cTf"], w=["cTb"])
            act(identb[:], cst[:, C_ID:C_ID + 128], r=["cst"], w=["identb"])
            act(bonesb[:], cst[:, C_BONES:C_BONES + 128], r=["cst"], w=["bonesb"])
            act(onesb[:], cst[:, C_ONES:C_ONES + 128], r=["cst"], w=["onesb"])
            act(aneg[:], vecs[:, V_ALB:V_ALB + 16], AF.Exp, r=["vecs"], w=["aneg"])
            ts(aneg[:], aneg[:], -1.0, r=["aneg"], w=["aneg"])
            act(anegch[:], vecs[:, V_ALCH:V_ALCH + 8], AF.Exp, r=["vecs"], w=["anegch"])
            ts(anegch[:], anegch[:], -1.0, r=["anegch"], w=["anegch"])

            if KSTOP < 1:
                return
            def ada_chunks(c0, ncks):
                bk = nb()
                for cl in range(ncks):
                    c = c0 + cl
                    slot, key = wneed(B_ADA[c // 2])
                    cc = c % 2
                    mmg(pb[bk][:, cl * 32:cl * 32 + 17],
                        [(slot[:, kk, cc * 128:(cc + 1) * 128], cTb[:, kk, :]) for kk in range(8)],
                        r=[key, "cTb"], w=[f"pb{bk}"])
                tt(mod[:, c0:c0 + ncks, :],
                   pb[bk][:, 0:ncks * 32].rearrange("p (c x) -> p c x", x=32)[:, :, 0:17],
                   bc(vecs[:, V_BADA + c0:V_BADA + c0 + ncks].unsqueeze(2), [128, ncks, 17]),
                   ALU.add, r=[f"pb{bk}", "vecs"], w=["mod" if c0 < 16 else "mod2"])
            wa32 = R1.bitcast(F32)[:, 0:16384].rearrange("p (k c) -> p k c", c=2048)
            for q4 in range(4):
                dma("sp", wa32[:, :, q4 * 512:(q4 + 1) * 512], w_ada_d[:, q4 * 512:(q4 + 1) * 512].rearrange("(k p) c -> p k c", p=128),
                    f"wa{q4}", w=[f"wa{q4}"])
            bk = nb()
            for c in range(16):
                mmg(pb[bk][:, c * 32:c * 32 + 17], [(wa32[:, kk, c * 128:(c + 1) * 128], cTf[:, kk, :]) for kk in range(8)],
                    r=[f"wa{c // 4}", "cTf"], w=[f"pb{bk}"])
            tt(mod[:, 0:16, :], pb[bk][:, :].rearrange("p (c x) -> p c x", x=32)[:, :, 0:17],
               bc(vecs[:, V_BADA:V_BADA + 16].unsqueeze(2), [128, 16, 17]), ALU.add, r=[f"pb{bk}", "vecs"], w=["mod"])
            ts(mod[:, 8:16, :], mod[:, 8:16, :], 1.0, op0=ALU.add, r=["mod"], w=["mod"])

            def ada_finish():
                ts(mod[:, 32:40, :], mod[:, 32:40, :], 1.0, op0=ALU.add, r=["mod2"], w=["mod2"])
                ts(mod[:, 16:24, :], mod[:, 16:24, :], 1.0, 1.0 / ALPHA, op0=ALU.add, op1=ALU.mult, r=["mod2"], w=["mod2"])
                ts(mod[:, 40:48, :], mod[:, 40:48, :], 1.0, 1.0 / ALPHA, op0=ALU.add, op1=ALU.mult, r=["mod2"], w=["mod2"])
                tt(A2[:], vecs[:, V_L1G:V_L1G + 8], mod[:, 32:40, 0], ALU.mult, r=["vecs", "mod2"], w=["A2"])
                tt(B2[:], vecs[:, V_L1B:V_L1B + 8], mod[:, 32:40, 0], ALU.mult, r=["vecs", "mod2"], w=["B2"])
                tt(B2[:], B2[:], mod[:, 24:32, 0], ALU.add, r=["B2", "mod2"], w=["B2"])

            if KSTOP < 2:
                return
            xTr = xT_d.rearrange("(k p) t -> p k t", p=128)
            xin = [r2f(i * 4096, 4096).rearrange("p (k t) -> p k t", t=512) for i in range(2)]
            for i in range(4):
                t0 = i * 512
                b = i % 2
                dma("sp", xin[b], xTr[:, :, t0:t0 + 512], f"xin{b}", w=[f"xin{b}"])
                for kk in range(8):
                    if kk % 2 == 0:
                        act(Uv[:, kk, t0:t0 + 512], xin[b][:, kk, :], AF.Identity, r=[f"xin{b}", "mod"], w=[f"U{i}"],
                            scale=mod[:, 8 + kk, 0:1], bias=mod[:, kk, 0:1])
                    else:
                        ts(Uv[:, kk, t0:t0 + 512], xin[b][:, kk, :], mod[:, 8 + kk, 0:1], mod[:, kk, 0:1], op0=ALU.mult, op1=ALU.add,
                           r=[f"xin{b}", "mod"], w=[f"U{i}"])
            us_tmp = small[:, 0:0]
            ustmp = xcs[:, 0:8, :]
            tt(ustmp, xs[:], mod[:, 8:16, 1:17], ALU.mult, r=["xs", "mod"], w=["ustmp"])
            tt(Uv[:, :, T:NTOK], ustmp, mod[:, 0:8, 1:17], ALU.add, r=["ustmp", "mod"], w=["U4"])
            S.fence()

            if KSTOP < 3:
                return
            BCT = r2b(0, 4 * T).rearrange("p (k t) -> p k t", t=T)
            pre = [r2f(4096 + i * 2064, 2051) for i in range(2)]
            ctmp = [[r2f(8224 + (i * 2 + j) * 512, 512) for j in range(2)] for i in range(4)]
            for i in range(2):
                S.op("dve", lambda e, i=i: e.memset(pre[i][:, 0:3], 0.0), w=[f"pre{i}z"])
            it = 0
            pend = []
            for blk in range(6):
                for cc in range(2):
                    kx = blk * 2 + cc
                    slot, key = wneed(B_XBC[blk])
                    pbuf = kx % 2
                    prb = pre[pbuf]
                    wcol = lambda tap, kx=kx: vecs[:, V_SCW + tap * 12 + kx:V_SCW + tap * 12 + kx + 1]
                    for i, (t0, n) in enumerate(TILES):
                        bk = nb()
                        mmg(pb[bk][:, 0:n], [(slot[:, kk, cc * 128:(cc + 1) * 128], Uv[:, kk, t0:t0 + n]) for kk in range(8)],
                            r=[key, f"U{i}"], w=[f"pb{bk}"])
                        if i < 4:
                            tb = it % 4
                            it += 1
                            c0, c1 = ctmp[tb]
                            act(prb[:, 3 + t0:3 + t0 + n], pb[bk][:, 0:n], r=[f"pb{bk}", f"pre{pbuf}z"], w=[f"pre{pbuf}_{i}"])
                            rd = [f"pre{pbuf}_{i}"] + ([f"pre{pbuf}_{i - 1}"] if i > 0 else [f"pre{pbuf}z"])
                            act(c0, prb[:, t0:t0 + n], AF.Identity, r=rd + ["vecs"], w=[f"c0_{tb}"], scale=wcol(0))
                            if pend:
                                pend.pop()()
                            stt(c1, prb[:, t0 + 1:t0 + 1 + n], wcol(1), c0, ALU.mult, ALU.add, r=rd + [f"c0_{tb}"], w=[f"c1_{tb}"])
                            stt(c0, prb[:, t0 + 2:t0 + 2 + n], wcol(2), c1, ALU.mult, ALU.add, r=rd + [f"c1_{tb}"], w=[f"c0_{tb}"])
                            stt(c1, pb[bk][:, 0:n], wcol(3), c0, ALU.mult, ALU.add, r=[f"pb{bk}", f"c0_{tb}"], w=[f"c1_{tb}"])
                            if kx < 8:
                                dst = ymix[:, 8 + kx, t0:t0 + n]
                                wk = [f"xbf{kx}_{i}"]
                            else:
                                dst = BCT[:, kx - 8, t0:t0 + n]
                                wk = [f"BCT{i}"]
                            pend.append(lambda dst=dst, c1=c1, tb=tb, wk=wk, kx=kx: act(dst, c1, AF.Silu, r=[f"c1_{tb}", "vecs"], w=wk,
                                                                                         bias=vecs[:, V_SCB + kx:V_SCB + kx + 1]))
                        else:
                            if pend:
                                pend.pop()()
                            act(nscs_sb[:, kx, :, 2], pb[bk][:, 0:n], r=[f"pb{bk}"], w=["xbcs"])
                            cs = small[:, 0:16]
                            ts(cs, stsc[:, kx, :, 0], wcol(0), r=["stsc", "vecs"], w=["cs"])
                            stt(cs, stsc[:, kx, :, 1], wcol(1), cs, ALU.mult, ALU.add, r=["cs", "stsc"], w=["cs"])
                            stt(cs, stsc[:, kx, :, 2], wcol(2), cs, ALU.mult, ALU.add, r=["cs", "stsc"], w=["cs"])
                            stt(cs, nscs_sb[:, kx, :, 2], wcol(3), cs, ALU.mult, ALU.add, r=["cs", "xbcs"], w=["cs"])
                            act(xcs[:, kx, :], cs, AF.Silu, r=["cs", "vecs"], w=["xcs"], bias=vecs[:, V_SCB + kx:V_SCB + kx + 1])
                            S.op("pool", lambda e, kx=kx: e.tensor_copy(out=nscs_sb[:, kx, :, 0:2], in_=stsc[:, kx, :, 1:3]), r=["stsc"], w=["nscs_a"])
                    S.op("pool", lambda e, kx=kx, prb=prb: e.tensor_copy(out=nscp_sb[:, kx, :], in_=prb[:, T:T + 3]),
                         r=[f"pre{pbuf}_3"], w=["nscp_sb"])
            if pend:
                pend.pop()()
            dma("sp", nscp_d, nscp_sb[:].rearrange("p a b -> p (a b)"), "nscp", r=["nscp_sb"])
            dma("sp", nscs_d, nscs_sb[:].rearrange("p a b c -> p (a b c)"), "nscs", r=["nscs_a", "xbcs"])

            dts = sb("dts", [128, 8, NS])
            bk = nb()
            for k in range(8):
                slot, key = wneed(B_DTX[k // 2])
                cc = k % 2
                mmg(pb[bk][:, k * 16:(k + 1) * 16], [(slot[:, kk, cc * 128:(cc + 1) * 128], Uv[:, kk, T:NTOK]) for kk in range(8)],
                    r=[key, "U4"], w=[f"pb{bk}"])
            act(dts[:].rearrange("p a b -> p (a b)"), pb[bk][:, 0:128], r=[f"pb{bk}"], w=["dts"])
            S.fence()

            if KSTOP < 4:
                return
            wz = R1[:, 0:8192].rearrange("p (k c) -> p k c", c=1024)
            for i in range(4):
                dma("pool", wz[:, :, i * 256:(i + 1) * 256], w_in_d[:, 3072 + i * 256:3072 + (i + 1) * 256].rearrange("(k p) c -> p k c", p=128),
                    f"wz{i}", w=["wz"])
            if KSUB < -3:
                return
            o = 4096
            def alloc_f(n):
                nonlocal o
                a = o
                o += n
                return a
            Sst = r2f(alloc_f(1024), 1024)
            Sbf = r2b(alloc_f(512), 1024)
            dtall = r2f(alloc_f(256), 256)
            dta = r2f(alloc_f(256), 256)
            acs = r2f(alloc_f(256), 256)
            Ecol = r2f(alloc_f(256), 256)
            dec = r2f(alloc_f(256), 256)
            wst = r2f(alloc_f(256), 256)
            r1free = 8192
            def r1b(n):
                nonlocal r1free
                a = r1free
                r1free += n
                return R1[:, a:a + n]
            xTb = [r1b(1024) for _ in range(2)]
            xdt = [r1b(1024) for _ in range(2)]
            xdtw = [r1b(1024) for _ in range(2)]
            Mh = [r1b(2048).rearrange("p (h s) -> p h s", s=128), r2b(alloc_f(1024), 2048).rearrange("p (h s) -> p h s", s=128)]
            assert r1free <= 8 * NTOK
            Btok = [r2b(alloc_f(128), 256) for _ in range(2)]
            Ah4 = [r2f(alloc_f(512), 512) for _ in range(2)]
            Lh4 = [r2f(alloc_f(512), 512) for _ in range(2)]
            CBm = [r2f(alloc_f(256), 256).rearrange("p (g s) -> p g s", s=128) for _ in range(2)]
            tA = [r2f(alloc_f(1024), 1024) for _ in range(2)]
            tB = r2f(alloc_f(1024), 1024)
            yn = [r2b(alloc_f(512), 1024) for _ in range(2)]
            DI = r2b(alloc_f(1024), 2048).rearrange("p (h s) -> p h s", s=128)
            for h in range(16):
                ts(DI[:, h, :], cst[:, C_ID:C_ID + 128], vecs[:, V_DSB + h:V_DSB + h + 1], r=["cst", "vecs"], w=["DI"])
            assert o <= 8 * NTOK, o
            mle = cst[:, C_MLE:C_MLE + 128]
            mgt = cst[:, C_MGT:C_MGT + 128]

            def softplus(dst, src, bias_bc, inner, keys_r, key_w, tmp1, tmp2):
                v3 = lambda a: a.rearrange("p (a b) -> p a b", b=inner)
                tt(v3(tmp1), src, bias_bc, ALU.add, r=keys_r, w=[key_w + "_t1"])
                act(tmp2, tmp1, AF.Abs, r=[key_w + "_t1"], w=[key_w + "_t2"])
                act(tmp2, tmp2, AF.Exp, r=[key_w + "_t2"], w=[key_w + "_t2"], scale=-1.0)
                act(tmp2, tmp2, AF.Ln, r=[key_w + "_t2"], w=[key_w + "_t2"], bias=1.0)
                ts(tmp1, tmp1, 0.0, op0=ALU.max, r=[key_w + "_t1"], w=[key_w + "_t1"])
                tt(dst, tmp1, tmp2, ALU.add, r=[key_w + "_t1", key_w + "_t2"], w=[key_w])

            bk = nb()
            for c in range(16):
                t0 = c * 128
                mmg(pb[bk][:, c * 16:(c + 1) * 16], [(Uv[:, kk, t0:t0 + 128], wdt[:, kk, :]) for kk in range(8)],
                    r=[f"U{c // 4}", "wdt"], w=[f"pb{bk}"])
            softplus(dtall, pb[bk][:, 0:256].rearrange("p (c h) -> p c h", h=16),
                     bc(vecs[:, V_DTB:V_DTB + 16].unsqueeze(1), [128, 16, 16]), 16, [f"pb{bk}", "vecs"], "dtall",
                     tA[0][:, 0:256], tA[1][:, 0:256])
            if KSUB < -2:
                return
            S.op("dve", lambda e: e.memset(Sst, 0.0), w=["Sst"])
            S.op("dve", lambda e: e.memset(Sbf, 0.0), w=["Sbf"])
            tt(dta.rearrange("p (c h) -> p c h", h=16), dtall.rearrange("p (c h) -> p c h", h=16),
               bc(aneg[:, :].unsqueeze(1), [128, 16, 16]), ALU.mult, r=["dtall", "aneg"], w=["dta"])
            bk1, bk2 = nb(), nb()
            mmg(pb[bk1][:, 0:256], [(mle, dta)], r=["cst", "dta"], w=[f"pb{bk1}"])
            mmg(pb[bk2][:, 0:256], [(cst[:, C_ONES:C_ONES + 128], dta)], r=["cst", "dta"], w=[f"pb{bk2}"])
            if KSUB < -1:
                return
            if KI < 0:
                return
            act(acs, pb[bk1][:, 0:256], r=[f"pb{bk1}"], w=["acs"])
            if KI < 1:
                return
            act(Ecol, pb[bk1][:, 0:256], AF.Exp, r=[f"pb{bk1}"], w=["Ecol"])
            if KI < 2:
                return
            act(dec, pb[bk2][:, 0:256], AF.Exp, r=[f"pb{bk2}"], w=["dec"])
            if KI < 3:
                return
            tt(wst, acs, pb[bk2][:, 0:256], ALU.subtract, r=[f"pb{bk2}", "acs", "dec", "Ecol"], w=["wst"])
            if KI < 4:
                return
            act(wst, wst, AF.Exp, r=["wst"], w=["wst"], scale=-1.0)
            if KI < 5:
                return
            tt(wst, wst, dtall, ALU.mult, r=["wst", "dtall"], w=["wst"])
            if KSUB < 1:
                return
            PT, PC, PSEG, PY, PO = 0, 1, (2, 3), (4, 5), (6, 7)
            S.op("dve", lambda e: e.memset(small[:, 20:22], -0.5), w=["negh"])

            def stageA(c):
                t0 = c * 128
                pa = c % 2
                trg([(pbb[PT][:, kx * 128:(kx + 1) * 128], ymix[:, 8 + kx, t0:t0 + 128], identb[:]) for kx in range(8)],
                    r=[f"xbf{kx}_{c // 4}" for kx in range(8)] + ["identb"], w=["pb0"])
                yield
                psT3 = pbb[PT][:, 0:1024].rearrange("p (h x) -> p h x", x=64)
                act(xTb[pa], pbb[PT][:, 0:1024], r=["pb0"], w=[f"xTb{pa}"])
                yield
                tt(xdt[pa].rearrange("p (h x) -> p h x", x=64), psT3, bc(dtall[:, c * 16:(c + 1) * 16].unsqueeze(2), [128, 16, 64]),
                   ALU.mult, r=["pb0", "dtall"], w=[f"xdt{pa}"])
                yield
                tt(xdtw[pa].rearrange("p (h x) -> p h x", x=64), psT3, bc(wst[:, c * 16:(c + 1) * 16].unsqueeze(2), [128, 16, 64]),
                   ALU.mult, r=["pb0", "wst"], w=[f"xdtw{pa}"])
                yield
                trg([(pbb[PC][:, 512 + g * 128:512 + (g + 1) * 128], BCT[:, g, t0:t0 + 128], identb[:]) for g in range(2)],
                    r=[f"BCT{c // 4}", "identb"], w=["pb1"])
                yield
                act(Btok[pa], pbb[PC][:, 512:768], r=["pb1"], w=[f"Btok{pa}"])
                yield
                def cbf(e, t0=t0):
                    last = None
                    for g in range(2):
                        last = e.matmul(pb[PC][:, g * 128:(g + 1) * 128], lhsT=BCT[:, g, t0:t0 + 128], rhs=BCT[:, 2 + g, t0:t0 + 128],
                                        start=True, stop=True)
                    return last
                S.op("pe", cbf, r=[f"BCT{c // 4}"], w=["pb1"])
                yield
                tt(CBm[pa], pb[PC][:, 0:256].rearrange("p (g s) -> p g s", s=128), bc(mle.unsqueeze(1), [128, 2, 128]),
                   ALU.mult, r=["pb1", "cst"], w=[f"CBm{pa}"])
                yield
                for q in range(4):
                    qb = q % 2
                    sbk = PSEG[qb]
                    a3 = Ah4[qb].rearrange("p (h t) -> p h t", t=128)
                    tt(a3, bc(mgt.unsqueeze(1), [128, 4, 128]), bc(dta[:, c * 16 + 4 * q:c * 16 + 4 * q + 4].unsqueeze(2), [128, 4, 128]),
                       ALU.mult, r=["cst", "dta"], w=[f"Ah{qb}"])
                    yield
                    def segf(e, qb=qb, sbk=sbk):
                        last = None
                        for j in range(4):
                            last = e.matmul(pb[sbk][:, j * 128:(j + 1) * 128], lhsT=Ah4[qb][:, j * 128:(j + 1) * 128], rhs=mle,
                                            start=True, stop=True)
                        return last
                    S.op("pe", segf, r=[f"Ah{qb}", "cst"], w=[f"pb{sbk}"])
                    yield
                    act(Lh4[qb], pb[sbk][:, :], AF.Exp, r=[f"pb{sbk}"], w=[f"Lh{qb}"])
                    yield
                    tt(Mh[pa][:, 4 * q:4 * q + 4, :], Lh4[qb].rearrange("p (h t) -> p h t", t=128),
                       bc(CBm[pa][:, q // 2, :].unsqueeze(1), [128, 4, 128]), ALU.mult,
                       r=[f"Lh{qb}", f"CBm{pa}"], w=[f"Mh{pa}_{q}"], eng="pool")
                    yield

            def stageB(c):
                t0 = c * 128
                pa = c % 2
                ab = c % 2
                for g in range(2):
                    mmg(pb[PO[g]][:, :], [(BCT[:, 2 + g, t0:t0 + 128], Sbf[:, g * 512:(g + 1) * 512])],
                        r=[f"BCT{c // 4}", "Sbf"], w=[f"pb{6 + g}"])
                    yield
                for h in range(16):
                    yb = PY[h // 8]
                    mmg(pb[yb][:, (h % 8) * 64:(h % 8 + 1) * 64],
                        [(Mh[pa][:, h, :], xdt[pa][:, h * 64:(h + 1) * 64]), (DI[:, h, :], xTb[pa][:, h * 64:(h + 1) * 64])],
                        r=[f"Mh{pa}_{h // 4}", f"xdt{pa}", f"xTb{pa}", "DI"], w=[f"pb{4 + h // 8}"])
                    yield
                for g in range(2):
                    tt(tA[ab][:, g * 512:(g + 1) * 512].rearrange("p (h x) -> p h x", x=64),
                       pb[PO[g]][:, :].rearrange("p (h x) -> p h x", x=64),
                       bc(Ecol[:, c * 16 + g * 8:c * 16 + g * 8 + 8].unsqueeze(2), [128, 8, 64]), ALU.mult,
                       r=[f"pb{6 + g}", "Ecol"], w=[f"tA{ab}_{g}"])
                    yield
                for g in range(2):
                    mmg(pb[PO[g]][:, :], [(Btok[pa][:, g * 128:(g + 1) * 128], xdtw[pa][:, g * 512:(g + 1) * 512])],
                        r=[f"Btok{pa}", f"xdtw{pa}"], w=[f"pb{6 + g}"])
                    yield
                tt(Sst.rearrange("p (h x) -> p h x", x=64), Sst.rearrange("p (h x) -> p h x", x=64),
                   bc(dec[:, c * 16:(c + 1) * 16].unsqueeze(2), [128, 16, 64]), ALU.mult, r=["Sst", "dec"], w=["Sst"], eng="pool")
                yield
                for g in range(2):
                    tt(Sst[:, g * 512:(g + 1) * 512], pb[PO[g]][:, :], Sst[:, g * 512:(g + 1) * 512], ALU.add,
                       r=[f"pb{6 + g}", "Sst"], w=["Sst"])
                    yield
                act(Sbf, Sst, r=["Sst"], w=["Sbf"])
                yield
                for g in range(2):
                    tt(tA[ab][:, g * 512:(g + 1) * 512], pb[PY[g]][:, :], tA[ab][:, g * 512:(g + 1) * 512], ALU.add,
                       r=[f"pb{4 + g}", f"tA{ab}_{g}"], w=[f"tA{ab}_{g}"])
                    yield
                for g in range(2):
                    mmg(pb[PO[g]][:, :], [(Uv[:, kk, t0:t0 + 128], wz[:, kk, g * 512:(g + 1) * 512]) for kk in range(8)],
                        r=[f"U{c // 4}", "wz"], w=[f"pb{6 + g}"])
                    yield
                    tbg = tB[:, g * 512:(g + 1) * 512]
                    act(tbg, pb[PO[g]][:, :], AF.Tanh, r=[f"pb{6 + g}"], w=[f"tB{g}"], scale=0.5)
                    yield
                    stt(tbg, tbg, 1.0, pb[PO[g]][:, :], ALU.add, ALU.mult, r=[f"pb{6 + g}", f"tB{g}"], w=[f"tB{g}"])
                    yield
                    stt(tA[ab][:, g * 512:(g + 1) * 512], tA[ab][:, g * 512:(g + 1) * 512], 0.5, tbg, ALU.mult, ALU.mult,
                        r=[f"tA{ab}_{g}", f"tB{g}"], w=[f"tA{ab}_{g}"])
                    yield
                    act(tbg, tA[ab][:, g * 512:(g + 1) * 512], AF.Square, r=[f"tA{ab}_{g}"], w=[f"tB{g}", f"ssq{g}"],
                        accum_out=small[:, 16 + g:17 + g])
                    yield
                ts(small[:, 18:20], small[:, 16:18], 1.0 / 512, RMS_EPS, op0=ALU.mult, op1=ALU.add, r=["ssq0", "ssq1"], w=["rstd"])
                yield
                S.op("pool", lambda e: e.tensor_tensor(out=small[:, 18:20], in0=small[:, 18:20], in1=small[:, 20:22], op=ALU.pow),
                     r=["rstd", "negh"], w=["rstd"])
                yield
                for g in range(2):
                    stt(yn[ab][:, g * 512:(g + 1) * 512], tA[ab][:, g * 512:(g + 1) * 512], small[:, 18 + g:19 + g],
                        vecs[:, V_SNWB + g * 512:V_SNWB + (g + 1) * 512], ALU.mult, ALU.mult,
                        r=[f"tA{ab}_{g}", "rstd", "vecs"], w=[f"yn{ab}"])
                    yield

            def stageC(c):
                t0 = c * 128
                ab = c % 2
                trg([(pbb[PT][:, kx * 128:(kx + 1) * 128], yn[ab][:, kx * 128:(kx + 1) * 128], identb[:]) for kx in range(8)],
                    r=[f"yn{ab}", "identb"], w=["pb0"])
                yield
                act(ymix[:, 8:16, t0:t0 + 128], pbb[PT][:, 0:1024].rearrange("p (k t) -> p k t", t=128),
                    r=["pb0"], w=[f"xbf{kx}_{c // 4}" for kx in range(8)] + ["ymix_ssm"])
                yield

            def interleave(gens_w):
                live = [[g, w] for g, w in gens_w]
                while live:
                    for ent in list(live):
                        g, w = ent
                        for _ in range(w):
                            try:
                                next(g)
                            except StopIteration:
                                live.remove(ent)
                                break

            for c in range(18):
                gd = {}
                if c < 16:
                    gd["A"] = (stageA(c), QA)
                if 1 <= c <= 16:
                    gd["B"] = (stageB(c - 1), QB)
                if c >= 2:
                    gd["C"] = (stageC(c - 2), QC)
                interleave([gd[k] for k in QORD if k in gd])
            dma("sp", nsp_d, Sst, "nsp", r=["Sst"])
            S.fence()

            if KSTOP < 5:
                return
            o = 0
            dtch = r2f(alloc_f(128), 128).rearrange("p (k j) -> p k j", j=NS)
            dAch = r2f(alloc_f(128), 128).rearrange("p (k j) -> p k j", j=NS)
            xdts = r2f(alloc_f(128), 128).rearrange("p (k j) -> p k j", j=NS)
            ysm = r2f(alloc_f(128), 128).rearrange("p (k j) -> p k j", j=NS)
            zs = r2f(alloc_f(128), 128).rearrange("p (k j) -> p k j", j=NS)
            gs = r2f(alloc_f(128), 128).rearrange("p (k j) -> p k j", j=NS)
            t1s = r2f(alloc_f(128), 128)
            t2s = r2f(alloc_f(128), 128)
            sqs = r2b(alloc_f(64), 128).rearrange("p (k j) -> p k j", j=NS)
            BCtok = r2f(alloc_f(512), 512)
            rhsj = [r2f(alloc_f(512), 512) for _ in range(2)]
            T1s = r2f(alloc_f(1024), 1024)
            Stb = [r2f(alloc_f(1024), 1024).rearrange("p (k n) -> p k n", n=128) for _ in range(4)]
            softplus(dtch.rearrange("p k j -> p (k j)"), dts[:],
                     bc(vecs[:, V_DTBCH:V_DTBCH + 8].unsqueeze(2), [128, 8, NS]), NS, ["dts", "vecs"], "dtch", t1s, t2s)
            tt(dAch, dtch, bc(anegch[:, :].unsqueeze(2), [128, 8, NS]), ALU.mult, r=["dtch", "anegch"], w=["dAch"])
            act(dAch, dAch, AF.Exp, r=["dAch"], w=["dAch"])
            tt(xdts, xcs[:, 0:8, :], dtch, ALU.mult, r=["xcs", "dtch"], w=["xdts"])
            bkT = nb()
            trg([(pb[bkT][0:16, m * 128:(m + 1) * 128], xcs[:, 8 + m, :], cst[:, C_ID:C_ID + 128]) for m in range(4)],
                r=["xcs", "cst"], w=[f"pb{bkT}"])
            act(BCtok[0:16, :], pb[bkT][0:16, :], r=[f"pb{bkT}"], w=["BCtok"])
            def ld_state(j):
                si = j % 4
                for h2 in range(2):
                    dma("sp", Stb[si][h2 * 64:(h2 + 1) * 64, :, :], sts_d[j, h2::2, :, :].rearrange("k p n -> p k n"),
                        f"St{si}_{h2}", w=[f"St{si}"])
            def decay_state(j):
                si = j % 4
                for k in range(8):
                    act(Stb[si][:, k, :], Stb[si][:, k, :], AF.Identity, r=[f"St{si}", "dAch"], w=[f"St{si}"], scale=dAch[:, k, j:j + 1])
            ld_state(0)
            ld_state(1)
            decay_state(0)
            for j in range(NS):
                sbuf_i = j % 4
                St = Stb[sbuf_i]
                if j + 2 < NS:
                    ld_state(j + 2)
                if j + 1 < NS:
                    decay_state(j + 1)
                rj = rhsj[j % 2]
                ts(rj[0:16, :], BCtok[0:16, :], cst[0:16, C_ID + j:C_ID + j + 1], r=["BCtok", "cst"], w=[f"rhsj{j % 2}"])
                bkj = nb()
                mmg(pb[bkj][:, :], [(cst[0:16, C_ONES:C_ONES + 128], rj[0:16, :])], r=[f"rhsj{j % 2}", "cst"], w=[f"pb{bkj}"])
                St2 = St.rearrange("p k n -> p (k n)")
                T1v = T1s.rearrange("p (g k n) -> p g k n", g=2, k=4)
                Bv = bc(pb[bkj][:, 0:256].rearrange("p (g n) -> p g n", n=128).unsqueeze(2), [128, 2, 4, 128])
                Cv = bc(pb[bkj][:, 256:512].rearrange("p (g n) -> p g n", n=128).unsqueeze(2), [128, 2, 4, 128])
                xv = bc(xdts[:, :, j].rearrange("p (g k) -> p g k", k=4).unsqueeze(3), [128, 2, 4, 128])
                tt(T1v, Bv, xv, ALU.mult, r=[f"pb{bkj}", "xdts"], w=["T1s"])
                tt(St2, St2, T1s, ALU.add, r=[f"St{sbuf_i}", "T1s"], w=[f"St{sbuf_i}"])
                tt(T1v, St.rearrange("p (g k) n -> p g k n", k=4), Cv, ALU.mult, r=[f"pb{bkj}", f"St{sbuf_i}"], w=["T1s"])
                S.op("dve", lambda e, j=j: e.tensor_reduce(out=ysm[:, :, j], in_=T1s.rearrange("p (k n) -> p k n", n=128),
                                                           axis=mybir.AxisListType.X, op=ALU.add), r=["T1s"], w=["ysm"])
                for h2 in range(2):
                    dma("act", nss_d[j, h2::2, :, :].rearrange("k p n -> p k n"), St[h2 * 64:(h2 + 1) * 64, :, :],
                        f"nss{sbuf_i}_{h2}", r=[f"St{sbuf_i}"])
                ada_chunks(16 + 2 * j, 2)
            ada_finish()
            tt(gs, xcs[:, 0:8, :], bc(vecs[:, V_DCH:V_DCH + 8].unsqueeze(2), [128, 8, NS]), ALU.mult, r=["xcs", "vecs"], w=["gs"])
            tt(gs, gs, ysm, ALU.add, r=["gs", "ysm"], w=["gs"])
            bkz = nb()
            for k in range(8):
                mmg(pb[bkz][:, k * 16:(k + 1) * 16], [(wz[:, kk, k * 128:(k + 1) * 128], Uv[:, kk, T:NTOK]) for kk in range(8)],
                    r=["wz", "U4"], w=[f"pb{bkz}"])
            zs2 = zs.rearrange("p k j -> p (k j)")
            act(zs2, pb[bkz][:, 0:128], r=[f"pb{bkz}"], w=["zs"])
            act(t1s, zs2, AF.Exp, r=["zs"], w=["t1s"], scale=-1.0)
            act(t1s, t1s, AF.Ln, r=["t1s"], w=["t1s"], bias=1.0)
            act(t1s, t1s, AF.Exp, r=["t1s"], w=["t1s"], scale=-1.0)
            tt(t1s, t1s, zs2, ALU.mult, r=["t1s", "zs"], w=["t1s"])
            gs2 = gs.rearrange("p k j -> p (k j)")
            tt(gs2, gs2, t1s, ALU.mult, r=["gs", "t1s"], w=["gs"])
            act(sqs.rearrange("p k j -> p (k j)"), gs2, AF.Square, r=["gs"], w=["sqs"])
            bkn = nb()
            for g in range(2):
                mmg(pb[bkn][:, g * 16:(g + 1) * 16], [(onesb[:], sqs[:, 4 * g + kk, :]) for kk in range(4)],
                    r=["onesb", "sqs"], w=[f"pb{bkn}"])
            act(t2s[:, 0:32], pb[bkn][:, 0:32], AF.Ln, r=[f"pb{bkn}"], w=["t2s"], scale=1.0 / 512, bias=RMS_EPS)
            act(t2s[:, 0:32], t2s[:, 0:32], AF.Exp, r=["t2s"], w=["t2s"], scale=-0.5)
            for g in range(2):
                tt(gs[:, 4 * g:4 * g + 4, :], gs[:, 4 * g:4 * g + 4, :], bc(t2s[:, g * 16:(g + 1) * 16].unsqueeze(1), [128, 4, NS]),
                   ALU.mult, r=["gs", "t2s"], w=["gs"])
            tt(ymix[:, 8:16, T:NTOK], gs, bc(vecs[:, V_SNWCH:V_SNWCH + 8].unsqueeze(2), [128, 8, NS]), ALU.mult,
               r=["gs", "vecs"], w=["ymix_ssm_s"])
            S.fence()

            if KSTOP < 6:
                return
            o = 0
            pful = [r2f(alloc_f(2052), 2050) for _ in range(2)]
            gcs = [r2f(alloc_f(512), 512) for _ in range(2)]
            cA = [r2f(alloc_f(512), 512) for _ in range(3)]
            cB = [r2f(alloc_f(512), 512) for _ in range(2)]
            gbcv = [r2f(alloc_f(512), 512) for _ in range(4)]
            sqb = [r2b(alloc_f(256), 512) for _ in range(4)]
            rs = [r2f(alloc_f(512), 512) for _ in range(2)]
            ps_s = r2f(alloc_f(16), 16)
            for i in range(2):
                S.op("dve", lambda e, i=i: e.memset(pful[i][:, 0:2], 0.0), w=[f"pf{i}z"])
            iters = []
            for kb in range(4):
                for cc in range(2):
                    for i, (t0, n) in enumerate(TILES):
                        iters.append((kb, cc, i, t0, n))
            slot_of = {}
            BG, BH, BB, BQ = (0, 1), (2, 3), (4, 5), (6, 7)

            def get_slots(kb):
                if kb not in slot_of:
                    base = B_CONV[kb - 1][2] if kb > 0 else B_CONV[kb][0]
                    slot_of[kb] = [wneed(b, base) for b in B_CONV[kb]]
                return slot_of[kb]

            def cS1(it):
                kb, cc, i, t0, n = iters[it]
                k = kb * 2 + cc
                slots = get_slots(kb)
                pbuf = k % 2
                pf = pful[pbuf]
                cw = lambda tap, k=k: vecs[:, V_CW + tap * 8 + k:V_CW + tap * 8 + k + 1]
                bg, bh = BG[it % 2], BH[it % 2]
                for (bank, gi) in ((bg, 0), (bh, 1)):
                    slot, key = slots[gi]
                    mmg(pb[bank][:, 0:n], [(slot[:, kk, cc * 128:(cc + 1) * 128], Uv[:, kk, t0:t0 + n]) for kk in range(8)],
                        r=[key, f"U{i}"], w=[f"pb{bank}"])
                tb = it % 2
                t3 = it % 3
                act(gcs[tb][:, 0:n], pb[bg][:, 0:n], r=[f"pb{bg}"], w=[f"gcs{tb}"])
                if i < 4:
                    tt(pf[:, 2 + t0:2 + t0 + n], pb[bh][:, 0:n], gcs[tb][:, 0:n], ALU.mult,
                       r=[f"pb{bh}", f"gcs{tb}", f"pf{pbuf}z"], w=[f"pf{pbuf}_{i}"])
                    rd = [f"pf{pbuf}_{i}"] + ([f"pf{pbuf}_{i - 1}"] if i > 0 else [f"pf{pbuf}z"])
                    act(cB[tb], pf[:, t0:t0 + n], AF.Identity, r=rd + ["vecs"], w=[f"cB{tb}"], scale=cw(0))
                    stt(cA[t3], pf[:, t0 + 1:t0 + 1 + n], cw(1), cB[tb], ALU.mult, ALU.add, r=rd + [f"cB{tb}"], w=[f"cA{t3}"])
                    stt(cA[t3], pf[:, t0 + 2:t0 + 2 + n], cw(2), cA[t3], ALU.mult, ALU.add, r=rd + [f"cA{t3}"], w=[f"cA{t3}"])
                    if i == 3:
                        S.op("pool", lambda e, k=k, pf=pf: e.tensor_copy(out=ncp_sb[:, k, :], in_=pf[:, T:T + 2]),
                             r=[f"pf{pbuf}_3"], w=["ncp_sb"])
                else:
                    tt(ps_s, pb[bh][:, 0:n], gcs[tb][:, 0:n], ALU.mult, r=[f"pb{bh}", f"gcs{tb}"], w=["ps_s"])
                    ts(cA[t3][:, 0:n], stc[:, k, :, 0], cw(0), r=["stc", "vecs"], w=[f"cA{t3}"])
                    stt(cA[t3][:, 0:n], stc[:, k, :, 1], cw(1), cA[t3][:, 0:n], ALU.mult, ALU.add, r=["stc", f"cA{t3}"], w=[f"cA{t3}"])
                    stt(cA[t3][:, 0:n], ps_s, cw(2), cA[t3][:, 0:n], ALU.mult, ALU.add, r=["ps_s", f"cA{t3}"], w=[f"cA{t3}"])
                    S.op("pool", lambda e, k=k: e.tensor_copy(out=ncs_sb[:, k, :, 0], in_=stc[:, k, :, 1]), r=["stc"], w=["ncs_a"])
                    S.op("pool", lambda e, k=k: e.tensor_copy(out=ncs_sb[:, k, :, 1], in_=ps_s), r=["ps_s"], w=["ncs_b"])

            def cS2(it):
                kb, cc, i, t0, n = iters[it]
                slot, key = get_slots(kb)[2]
                bb = BB[it % 2]
                t3 = it % 3
                t4 = it % 4
                mmg(pb[bb][:, 0:n], [(slot[:, kk, cc * 128:(cc + 1) * 128], Uv[:, kk, t0:t0 + n]) for kk in range(8)],
                    r=[key, f"U{i}"], w=[f"pb{bb}"])
                tt(gbcv[t4][:, 0:n], pb[bb][:, 0:n], cA[t3][:, 0:n], ALU.mult, r=[f"pb{bb}", f"cA{t3}"], w=[f"gbcv{t4}"])
                act(sqb[t4][:, 0:n], gbcv[t4][:, 0:n], AF.Square, r=[f"gbcv{t4}"], w=[f"sqb{t4}"])

            def cS3(it):
                kb, cc, i, t0, n = iters[it]
                k = kb * 2 + cc
                bq = BQ[it % 2]
                t3 = it % 4
                tb = it % 2
                mmg(pb[bq][:, 0:n], [(bonesb[:], sqb[t3][:, 0:n])], r=["bonesb", f"sqb{t3}"], w=[f"pb{bq}"])
                act(rs[tb][:, 0:n], pb[bq][:, 0:n], AF.Ln, r=[f"pb{bq}"], w=[f"rs{tb}"], scale=1.0 / 64, bias=RMS_EPS)
                act(rs[tb][:, 0:n], rs[tb][:, 0:n], AF.Exp, r=[f"rs{tb}"], w=[f"rs{tb}"], scale=-0.5)
                stt(ymix[:, k, t0:t0 + n], gbcv[t3][:, 0:n], vecs[:, V_CNW + k:V_CNW + k + 1], rs[tb][:, 0:n], ALU.mult, ALU.mult,
                    r=[f"gbcv{t3}", f"rs{tb}", "vecs"], w=[f"ymc{k}_{i}"])

            NI = len(iters)
            for s_ in range(NI + 3):
                if s_ < NI:
                    cS1(s_)
                if 0 <= s_ - 1 < NI:
                    cS2(s_ - 1)
                if 0 <= s_ - 3 < NI:
                    cS3(s_ - 3)
            dma("sp", ncp_d, ncp_sb[:].rearrange("p a b -> p (a b)"), "ncp", r=["ncp_sb"])
            dma("sp", ncs_d, ncs_sb[:].rearrange("p a b c -> p (a b c)"), "ncs", r=["ncs_a", "ncs_b"])
            S.fence()

            if KSTOP < 7:
                return
            X1 = R2[:, :].rearrange("p (k t) -> p k t", t=NTOK)
            xk = [Ub32[:, i * 2048:(i + 1) * 2048] for i in range(2)]
            o32 = 4096
            sqt1 = U[:, 2048:2048 + 4096].rearrange("p (k t) -> p k t", t=512)
            sqt2 = [sqt1, sqt1]
            _mean = Ub32[:, 3072:3584]
            _msq = Ub32[:, 3584:4096]
            st4 = [[_mean, _msq, Ub32[:, 4096 + j * 1024:4608 + j * 1024], Ub32[:, 4608 + j * 1024:5120 + j * 1024]] for j in range(2)]
            lt1 = [Ub32[:, 6144 + i * 512:6656 + i * 512] for i in range(2)]
            assert 7168 <= 4 * NTOK
            Vv = R1[:, 0:8 * NTOK].rearrange("p (k t) -> p k t", t=NTOK)
            HQ = R1[:, 8 * NTOK:16 * NTOK].rearrange("p (k t) -> p k t", t=NTOK)

            def layer_norm_all(outs, post, inline=False):
                def stats(i):
                    t0, n = TILES[i]
                    sb_ = i % 2
                    for kk in range(8):
                        if (not inline) and kk % 2 == 1 and n == 512:
                            tt(sqt2[sb_][:, kk, 0:n], X1[:, kk, t0:t0 + n], X1[:, kk, t0:t0 + n], ALU.mult,
                               r=[f"X1_{kk}_{i}"], w=[f"sqt_{kk}"], eng="pool")
                        else:
                            act(sqt2[sb_][:, kk, 0:n], X1[:, kk, t0:t0 + n], AF.Square, r=[f"X1_{kk}_{i}"], w=[f"sqt_{kk}"])
                    b1, b2 = nb(), nb()
                    mmg(pb[b1][:, 0:n], [(cst[:, C_ONES:C_ONES + 128], X1[:, kk, t0:t0 + n]) for kk in range(8)],
                        r=["cst"] + [f"X1_{kk}_{i}" for kk in range(8)], w=[f"pb{b1}"])
                    mmg(pb[b2][:, 0:n], [(onesb[:], sqt2[sb_][:, kk, 0:n]) for kk in range(8)],
                        r=["onesb"] + [f"sqt_{kk}" for kk in range(8)], w=[f"pb{b2}"])
                    mean, msq, rstd, nmr = st4[sb_]
                    ts(mean[:, 0:n], pb[b1][:, 0:n], 1.0 / 1024, r=[f"pb{b1}"], w=["st_mean"])
                    tt(msq[:, 0:n], mean[:, 0:n], mean[:, 0:n], ALU.mult, r=["st_mean"], w=["st_msq"])
                    stt(msq[:, 0:n], pb[b2][:, 0:n], 1.0 / 1024, msq[:, 0:n], ALU.mult, ALU.subtract, r=[f"pb{b2}", "st_msq"], w=["st_msq"])
                    act(rstd[:, 0:n], msq[:, 0:n], AF.Ln, r=["st_msq"], w=[f"st_rstd{sb_}"], bias=LN_EPS)
                    act(rstd[:, 0:n], rstd[:, 0:n], AF.Exp, r=[f"st_rstd{sb_}"], w=[f"st_rstd{sb_}"], scale=-0.5)
                    stt(nmr[:, 0:n], mean[:, 0:n], -1.0, rstd[:, 0:n], ALU.mult, ALU.mult, r=["st_mean", f"st_rstd{sb_}"], w=[f"st_nmr{sb_}"])

                def norm(i):
                    t0, n = TILES[i]
                    sb_ = i % 2
                    mean, msq, rstd, nmr = st4[sb_]
                    for kk in range(8):
                        lb = kk % 2
                        tt(lt1[lb][:, 0:n], X1[:, kk, t0:t0 + n], rstd[:, 0:n], ALU.mult, r=[f"X1_{kk}_{i}", f"st_rstd{sb_}"], w=[f"lt1{lb}"])
                        tt(lt1[lb][:, 0:n], lt1[lb][:, 0:n], nmr[:, 0:n], ALU.add, r=[f"lt1{lb}", f"st_nmr{sb_}"], w=[f"lt1{lb}"])
                        outs(i, t0, n, kk, lt1[lb][:, 0:n], f"lt1{lb}")
                    post(i, t0, n)

                if inline:
                    return lambda i: (stats(i), norm(i))
                stats(0)
                for i in range(len(TILES)):
                    if i + 1 < len(TILES):
                        stats(i + 1)
                    norm(i)

            xTrk = xT_d.rearrange("(k p) t -> p k t", p=128)
            for cb in range(4):
                slots = [wneed(b, B_OUT[cb][0]) for b in B_OUT[cb]]
                for cc in range(2):
                    kd = cb * 2 + cc
                    xb_i = kd % 2
                    dma("sp", xk[xb_i], xTrk[:, kd, :], f"xk{xb_i}", w=[f"xk{xb_i}"])
                    for i, (t0, n) in enumerate(TILES):
                        bk = nb()
                        pairs = []
                        for hh in range(2):
                            slot, key = slots[hh]
                            pairs += [(slot[:, kk, cc * 128:(cc + 1) * 128], ymix[:, hh * 8 + kk, t0:t0 + n]) for kk in range(8)]
                        rk = [slots[0][1], slots[1][1]] + [f"ymc{kk}_{i}" for kk in range(8)] + (["ymix_ssm"] if i < 4 else ["ymix_ssm_s"])
                        mmg(pb[bk][:, 0:n], pairs, r=rk, w=[f"pb{bk}"])
                        if i < 4:
                            stt(X1[:, kd, t0:t0 + n], pb[bk][:, 0:n], mod[:, 16 + kd, 0:1], xk[xb_i][:, t0:t0 + n], ALU.mult, ALU.add,
                                r=[f"pb{bk}", "mod", f"xk{xb_i}"], w=[f"X1_{kd}_{i}"])
                        else:
                            tt(X1[:, kd, t0:t0 + n], pb[bk][:, 0:n], mod[:, 16 + kd, 1:17], ALU.mult, r=[f"pb{bk}", "mod"], w=[f"X1_{kd}_{i}"])
                            tt(X1[:, kd, t0:t0 + n], X1[:, kd, t0:t0 + n], xs[:, kd, :], ALU.add, r=[f"X1_{kd}_{i}", "xs"], w=[f"X1_{kd}_{i}"])
            S.fence()
            def outs1(i, t0, n, kk, xn, xkey):
                if i < 4:
                    if kk in (1, 5):
                        ts(X1[:, kk, t0:t0 + n], xn, vecs[:, V_L1G + kk:V_L1G + kk + 1], vecs[:, V_L1B + kk:V_L1B + kk + 1],
                           op0=ALU.mult, op1=ALU.add, r=[xkey, "vecs"], w=[f"X1_{kk}_{i}"], eng="pool")
                    else:
                        act(X1[:, kk, t0:t0 + n], xn, AF.Identity, r=[xkey, "vecs"], w=[f"X1_{kk}_{i}"],
                            scale=vecs[:, V_L1G + kk:V_L1G + kk + 1], bias=vecs[:, V_L1B + kk:V_L1B + kk + 1])
                    if kk in (3, 7):
                        ts(Vv[:, kk, t0:t0 + n], xn, A2[:, kk:kk + 1], B2[:, kk:kk + 1], op0=ALU.mult, op1=ALU.add,
                           r=[xkey, "A2", "B2"], w=[f"V{i}"])
                    else:
                        act(Vv[:, kk, t0:t0 + n], xn, AF.Identity, r=[xkey, "A2", "B2"], w=[f"V{i}"],
                            scale=A2[:, kk:kk + 1], bias=B2[:, kk:kk + 1])
                else:
                    act(X1[:, kk, t0:t0 + n], xn, AF.Identity, r=[xkey, "vecs"], w=[f"X1_{kk}_{i}"],
                        scale=vecs[:, V_L1G + kk:V_L1G + kk + 1], bias=vecs[:, V_L1B + kk:V_L1B + kk + 1])
                    tt(xn, X1[:, kk, t0:t0 + n], mod[:, 32 + kk, 1:17], ALU.mult, r=[f"X1_{kk}_{i}", "mod"], w=[xkey])
                    tt(Vv[:, kk, t0:t0 + n], xn, mod[:, 24 + kk, 1:17], ALU.add, r=[xkey, "mod"], w=[f"V{i}"])
            layer_norm_all(outs1, lambda i, t0, n: None)

            if KSTOP < 8:
                return
            rl = [Ub32[:, i * 512:(i + 1) * 512] for i in range(2)]
            yo = [Ub32[:, 1024 + i * 512:1536 + i * 512] for i in range(2)]
            yTr = yT_d.rearrange("(k p) t -> p k t", p=128)
            ysTr = ysT_d.rearrange("(k p) t -> p k t", p=128)

            def outs2(i, t0, n, kk, xn, xkey):
                act(X1[:, kk, t0:t0 + n], xn, AF.Identity, r=[xkey, "vecs"], w=[f"X1_{kk}_{i}"],
                    scale=vecs[:, V_L2G + kk:V_L2G + kk + 1], bias=vecs[:, V_L2B + kk:V_L2B + kk + 1])

            def post2(i, t0, n):
                if i < 4:
                    dma("sp", yTr[:, :, t0:t0 + n], X1[:, :, t0:t0 + n], f"yout{i}", r=[f"X1_{kk}_{i}" for kk in range(8)])
                else:
                    dma("sp", ysTr, X1[:, :, t0:t0 + n], f"yout{i}", r=[f"X1_{kk}_{i}" for kk in range(8)])
            ln2_tile = layer_norm_all(outs2, post2, inline=True)
            it = 0
            for q in range(4):
                for bi in range(4):
                    slot, key = wneed(B_UP[q][bi])
                    for cc in range(2):
                        f = bi * 2 + cc
                        for i, (t0, n) in enumerate(TILES):
                            bk = nb()
                            mmg(pb[bk][:, 0:n], [(slot[:, kk, cc * 128:(cc + 1) * 128], Vv[:, kk, t0:t0 + n]) for kk in range(8)],
                                r=[key, f"V{i}"], w=[f"pb{bk}"])
                            tb = it % 2
                            it += 1
                            act(rl[tb][:, 0:n], pb[bk][:, 0:n], AF.Relu, r=[f"pb{bk}"], w=[f"rl{tb}"])
                            tt(HQ[:, f, t0:t0 + n], pb[bk][:, 0:n], rl[tb][:, 0:n], ALU.mult, r=[f"pb{bk}", f"rl{tb}"], w=[f"HQ{f}_{i}"])
                def down_tile(slot, key, cc, kd, i, t0, n):
                    bk = nb()
                    mmg(pb[bk][:, 0:n], [(slot[:, kk, cc * 128:(cc + 1) * 128], HQ[:, kk, t0:t0 + n]) for kk in range(8)],
                        r=[key] + [f"HQ{kk}_{i}" for kk in range(8)], w=[f"pb{bk}"])
                    if i < 4:
                        stt(X1[:, kd, t0:t0 + n], pb[bk][:, 0:n], mod[:, 40 + kd, 0:1], X1[:, kd, t0:t0 + n], ALU.mult, ALU.add,
                            r=[f"pb{bk}", "mod", f"X1_{kd}_{i}"], w=[f"X1_{kd}_{i}"])
                    else:
                        tt(rl[0][:, 0:n], pb[bk][:, 0:n], mod[:, 40 + kd, 1:17], ALU.mult, r=[f"pb{bk}", "mod"], w=["rl0"])
                        tt(X1[:, kd, t0:t0 + n], X1[:, kd, t0:t0 + n], rl[0][:, 0:n], ALU.add, r=[f"X1_{kd}_{i}", "rl0"], w=[f"X1_{kd}_{i}"])

                if q < 3:
                    for bi in range(4):
                        slot, key = wneed(B_DN[q][bi])
                        for cc in range(2):
                            for i, (t0, n) in enumerate(TILES):
                                down_tile(slot, key, cc, bi * 2 + cc, i, t0, n)
                else:
                    slots = [wneed(b_, B_DN[q][0]) for b_ in B_DN[q]]
                    for i, (t0, n) in enumerate(TILES):
                        for bi in range(4):
                            slot, key = slots[bi]
                            for cc in range(2):
                                down_tile(slot, key, cc, bi * 2 + cc, i, t0, n)
                        ln2_tile(i)

        o = 0
        phases()
        with nc.Block() as block:
            S.emit(block)
    return nc


def _fm(v, nchunk):
    return np.ascontiguousarray(v.reshape(nchunk, 128).T)


_CACHE = {}


def kernel(x_prompt, x_sample, state_conv, state_ssm_conv, state_ssm, c_prompt, c_sample,
           w_ada, b_ada, w_in, conv_w, conv_norm_w, ssm_conv_w, ssm_conv_b, dt_bias, a_log, d_skip,
           ssm_norm_w, w_out, ln1_g, ln1_b, w_up, w_down, ln2_g, ln2_b):
    f = lambda a: np.ascontiguousarray(np.asarray(a, dtype=np.float32))
    x_prompt, x_sample, state_conv, state_ssm_conv, state_ssm = map(f, (x_prompt, x_sample, state_conv, state_ssm_conv, state_ssm))
    c_prompt, c_sample = f(c_prompt), f(c_sample)
    w_ada, w_in, w_out, w_up, w_down = f(w_ada)[0], f(w_in)[0], f(w_out)[0], f(w_up)[0], f(w_down)[0]
    b_ada, conv_w, conv_norm_w, ssm_conv_w, ssm_conv_b = f(b_ada)[0], f(conv_w)[0], f(conv_norm_w)[0], f(ssm_conv_w)[0], f(ssm_conv_b)[0]
    dt_bias, a_log, d_skip, ssm_norm_w = f(dt_bias)[0], f(a_log)[0], f(d_skip)[0], f(ssm_norm_w)[0]
    ln1_g, ln1_b, ln2_g, ln2_b = f(ln1_g)[0], f(ln1_b)[0], f(ln2_g)[0], f(ln2_b)[0]

    vecs = np.zeros((128, NV), np.float32)
    vecs[:, V_BADA:V_BADA + 48] = _fm(b_ada, 48)
    for tap in range(3):
        vecs[:, V_CW + tap * 8:V_CW + tap * 8 + 8] = _fm(conv_w[tap], 8)
    vecs[:, V_CNW:V_CNW + 8] = _fm(conv_norm_w, 8)
    for tap in range(4):
        vecs[:, V_SCW + tap * 12:V_SCW + tap * 12 + 12] = _fm(ssm_conv_w[tap], 12)
    vecs[:, V_SCB:V_SCB + 12] = _fm(ssm_conv_b, 12)
    vecs[:, V_L1G:V_L1G + 8] = _fm(ln1_g, 8)
    vecs[:, V_L1B:V_L1B + 8] = _fm(ln1_b, 8)
    vecs[:, V_L2G:V_L2G + 8] = _fm(ln2_g, 8)
    vecs[:, V_L2B:V_L2B + 8] = _fm(ln2_b, 8)
    vecs[:, V_DCH:V_DCH + 8] = _fm(np.repeat(d_skip, 64), 8)
    vecs[:, V_ALCH:V_ALCH + 8] = _fm(np.repeat(a_log, 64), 8)
    vecs[:, V_DTBCH:V_DTBCH + 8] = _fm(np.repeat(dt_bias, 64), 8)
    vecs[:, V_SNWCH:V_SNWCH + 8] = _fm(ssm_norm_w, 8)
    vecs[:, V_ALB:V_ALB + 16] = a_log[None, :]
    vecs[:, V_DTB:V_DTB + 16] = dt_bias[None, :]
    vecs[:, V_DSB:V_DSB + 16] = d_skip[None, :]
    vecs[:, V_SNWB:V_SNWB + 1024] = ssm_norm_w[None, :]
    consts = np.zeros((128, NC_), np.float32)
    idx = np.arange(128)
    consts[:, C_ID:C_ID + 128] = np.eye(128, dtype=np.float32)
    consts[:, C_MLE:C_MLE + 128] = (idx[:, None] <= idx[None, :])
    consts[:, C_MGT:C_MGT + 128] = (idx[:, None] > idx[None, :])
    consts[:, C_BONES:C_BONES + 128] = ((idx[:, None] // 64) == (idx[None, :] // 64))
    consts[:, C_ONES:C_ONES + 128] = 1.0
    wdtx = np.ascontiguousarray(np.repeat(w_in[:, 5632:5648], 64, axis=1))

    in_maps = []
    for b in range(8):
        js = slice(16 * b, 16 * b + 16)
        stc = state_conv[0, js]
        stsc = state_ssm_conv[0, js]
        in_maps.append({
            "xT": np.ascontiguousarray(x_prompt[b].T),
            "xsT": np.ascontiguousarray(x_sample[js, 0, :].T),
            "cT": np.ascontiguousarray(np.concatenate([c_prompt[b:b + 1], c_sample[js]], axis=0).T),
            "stc": np.ascontiguousarray(stc.reshape(16, 2, 8, 128).transpose(3, 2, 0, 1).reshape(128, -1)),
            "stsc": np.ascontiguousarray(stsc.reshape(16, 3, 12, 128).transpose(3, 2, 0, 1).reshape(128, -1)),
            "sts": np.ascontiguousarray(state_ssm[0, js]),
            "w_ada": w_ada, "w_in": w_in, "wdtx": wdtx, "w_out": w_out, "w_up": w_up, "w_down": w_down,
            "vecs": vecs, "consts": consts,
        })
    if "nc" not in _CACHE:
        _CACHE["nc"] = build_program()
    res = run_bass_kernel_spmd(_CACHE["nc"], in_maps, core_ids=list(range(8)))
    R = res.results
    y_prompt = np.stack([R[b]["yT"].T for b in range(8)]).astype(np.float32)
    y_sample = np.concatenate([R[b]["ysT"].T for b in range(8)], axis=0)[:, None, :].astype(np.float32)
    ncp = np.stack([R[b]["ncp"].reshape(128, 8, 2).transpose(2, 1, 0).reshape(2, 1024) for b in range(8)])[None]
    nscp = np.stack([R[b]["nscp"].reshape(128, 12, 3).transpose(2, 1, 0).reshape(3, 1536) for b in range(8)])[None]
    nsp = np.stack([R[b]["nsp"].reshape(128, 16, 64).transpose(1, 2, 0) for b in range(8)])[None]
    ncs = np.concatenate([R[b]["ncs"].reshape(128, 8, 16, 2).transpose(2, 3, 1, 0).reshape(16, 2, 1024) for b in range(8)], axis=0)[None]
    nscs = np.concatenate([R[b]["nscs"].reshape(128, 12, 16, 3).transpose(2, 3, 1, 0).reshape(16, 3, 1536) for b in range(8)], axis=0)[None]
    nss = np.concatenate([R[b]["nss"] for b in range(8)], axis=0)[None]
    c = lambda a: np.ascontiguousarray(a, dtype=np.float32)
    return (c(y_prompt), c(y_sample), c(ncp), c(nscp), c(nsp), c(ncs), c(nscs), c(nss))
```
